# Optimizing a Trainium2 kernel written in Bass

```python
import jax, jax.numpy as jnp
from jax import lax
import numpy as np

D_MODEL = 2048
BATCH = 1
SEQ = 16384
DEPTH = 2

D_HGRN = D_MODEL // 2
HGRN_HEAD_DIM = 128
HGRN_HEADS = D_HGRN // HGRN_HEAD_DIM
D_CONV = D_MODEL - D_HGRN
D_MIX = D_HGRN + D_CONV
CONV_WIDTH = 3
CHUNK = 64
D_FF = -(-8 * D_MODEL // (3 * 256)) * 256
EPS = 1e-6
SPLIT_SIZES = (D_HGRN, D_HGRN, D_HGRN, D_HGRN, D_HGRN, D_CONV, D_CONV, D_CONV)
D_IN_PROJ = sum(SPLIT_SIZES)
SPLIT_OFFSETS = [int(o) for o in np.cumsum(SPLIT_SIZES)[:-1]]

kernel_name = "bidir_hgrn2_shortconv_hybrid"


def _rmsnorm(x, w):
    x32 = x.astype(jnp.float32)
    y = x32 * lax.rsqrt(jnp.mean(x32 * x32, axis=-1, keepdims=True) + EPS)
    return (y * w.astype(jnp.float32)).astype(x.dtype)


def _lower_bounds(lb_param):
    c = jnp.cumsum(jax.nn.softmax(lb_param.astype(jnp.float32), axis=0), axis=0)
    return c - c[0:1]


def _chunk_recurrence(q, k, v, log_f):
    bsz, s, h, dk = q.shape
    dv = v.shape[-1]
    n = s // CHUNK

    def to_chunks(t):
        return t.reshape(bsz, n, CHUNK, h, t.shape[-1]).transpose(1, 0, 3, 2, 4)

    mask = jnp.tril(jnp.ones((CHUNK, CHUNK), dtype=bool))[:, :, None]

    def step(state, inp):
        qi, ki, vi, gi = inp
        b = jnp.cumsum(gi, axis=-2)
        o_inter = jnp.einsum('bhck,bhkv->bhcv', qi * jnp.exp(b), state)
        diff = b[..., :, None, :] - b[..., None, :, :]
        decay = jnp.exp(jnp.where(mask, diff, -jnp.inf))
        scores = jnp.einsum('bhtk,bhsk,bhtsk->bhts', qi, ki, decay)
        o_intra = jnp.einsum('bhts,bhsv->bhtv', scores, vi)
        b_last = b[..., -1:, :]
        new_state = state * jnp.exp(b_last)[..., 0, :, None] + jnp.einsum(
            'bhck,bhcv->bhkv', ki * jnp.exp(b_last - b), vi)
        return new_state, o_inter + o_intra

    state0 = jnp.zeros((bsz, h, dk, dv), jnp.float32)
    _, o = lax.scan(step, state0, (to_chunks(q), to_chunks(k), to_chunks(v), to_chunks(log_f)))
    return o.transpose(1, 0, 3, 2, 4).reshape(bsz, s, h, dv)


def _hgrn2_mixer(q_pre, fz_fwd, fz_bwd, i_pre, gate, lb_f, lb_b, norm_w):
    bsz, s, _ = q_pre.shape

    def heads(t):
        return t.reshape(bsz, s, HGRN_HEADS, HGRN_HEAD_DIM).astype(jnp.float32)

    q = jax.nn.silu(heads(q_pre))
    v = heads(i_pre)

    def gates(fz, lb):
        lb = lb.reshape(HGRN_HEADS, HGRN_HEAD_DIM)
        log_f = jnp.logaddexp(jnp.log(lb), jnp.log1p(-lb) + jax.nn.log_sigmoid(heads(fz)))
        return -jnp.expm1(log_f), log_f

    k_f, g_f = gates(fz_fwd, lb_f)
    k_b, g_b = gates(fz_bwd, lb_b)
    o_fwd = _chunk_recurrence(q, k_f, v, g_f)
    flip = lambda t: jnp.flip(t, axis=1)
    o_bwd = flip(_chunk_recurrence(flip(q), flip(k_b), flip(v), flip(g_b)))
    o = _rmsnorm(o_fwd + o_bwd, norm_w.reshape(HGRN_HEADS, HGRN_HEAD_DIM))
    o = o.reshape(bsz, s, D_HGRN) * jax.nn.silu(gate.astype(jnp.float32))
    return o.astype(q_pre.dtype)


def _short_conv_mixer(b_gate, c_gate, h, conv_w):
    u = c_gate * h
    rhs = conv_w[:, None, :].astype(u.dtype)
    y = lax.conv_general_dilated(u, rhs, window_strides=(1,), padding=((1, 1),),
                                 dimension_numbers=('NWC', 'WIO', 'NWC'),
                                 feature_group_count=D_CONV)
    return b_gate * y


def setup_inputs(seed: int = 0) -> dict:
    key = jax.random.key(seed)
    ks = jax.random.split(key, 12)
    f32 = jnp.float32
    nrm = lambda k, shape, fan_in: jax.random.normal(k, shape, f32) * (fan_in ** -0.5)
    return {
        "x": jax.random.normal(ks[0], (BATCH, SEQ, D_MODEL), f32),
        "attn_norm_w": 1.0 + 0.02 * jax.random.normal(ks[1], (DEPTH, D_MODEL), f32),
        "w_in": nrm(ks[2], (DEPTH, D_MODEL, D_IN_PROJ), D_MODEL),
        "lb_fwd": 0.1 * jax.random.normal(ks[3], (DEPTH, D_HGRN), f32),
        "lb_bwd": 0.1 * jax.random.normal(ks[4], (DEPTH, D_HGRN), f32),
        "hgrn_norm_w": 1.0 + 0.02 * jax.random.normal(ks[5], (DEPTH, D_HGRN), f32),
        "conv_w": nrm(ks[6], (DEPTH, CONV_WIDTH, D_CONV), CONV_WIDTH),
        "w_out": nrm(ks[7], (DEPTH, D_MIX, D_MODEL), D_MIX),
        "ffn_norm_w": 1.0 + 0.02 * jax.random.normal(ks[8], (DEPTH, D_MODEL), f32),
        "w_gate_up": nrm(ks[9], (DEPTH, D_MODEL, 2 * D_FF), D_MODEL),
        "w_down": nrm(ks[10], (DEPTH, D_FF, D_MODEL), D_FF),
        "final_norm_w": 1.0 + 0.02 * jax.random.normal(ks[11], (D_MODEL,), f32),
    }


def reference(x, attn_norm_w, w_in, lb_fwd, lb_bwd, hgrn_norm_w, conv_w, w_out,
              ffn_norm_w, w_gate_up, w_down, final_norm_w):
    lbs_f = _lower_bounds(lb_fwd)
    lbs_b = _lower_bounds(lb_bwd)
    for l in range(DEPTH):
        h = _rmsnorm(x, attn_norm_w[l])
        proj = jnp.einsum('bsd,de->bse', h, w_in[l])
        q_pre, fz_f, fz_b, i_pre, gate, b_gate, c_gate, h_conv = jnp.split(proj, SPLIT_OFFSETS, axis=-1)
        y_rec = _hgrn2_mixer(q_pre, fz_f, fz_b, i_pre, gate, lbs_f[l], lbs_b[l], hgrn_norm_w[l])
        y_conv = _short_conv_mixer(b_gate, c_gate, h_conv, conv_w[l])
        mixed = jnp.concatenate([y_rec, y_conv], axis=-1)
        x = x + jnp.einsum('bse,ed->bsd', mixed, w_out[l])
        h2 = _rmsnorm(x, ffn_norm_w[l])
        g, u = jnp.split(jnp.einsum('bsd,df->bsf', h2, w_gate_up[l]), 2, axis=-1)
        x = x + jnp.einsum('bsf,fd->bsd', jax.nn.silu(g) * u, w_down[l])
    return _rmsnorm(x, final_norm_w)
```

```python
import numpy as np
import concourse.bass as bass
import concourse.mybir as mybir

F32 = mybir.dt.float32
BF16 = mybir.dt.bfloat16
AF = mybir.ActivationFunctionType
ALU = mybir.AluOpType
AX = mybir.AxisListType


class Buf:
    __slots__ = ("name", "w", "r", "dsem")

    def __init__(self, name):
        self.name = name
        self.w = {}
        self.r = {}
        self.dsem = None


class DSem:
    def __init__(self, sem):
        self.sem = sem
        self.count = 0


class Eng:
    def __init__(self, name, eng, sem):
        self.name = name
        self.eng = eng
        self.sem = sem
        self.count = 0
        self.seen = {}
        self.prog = []


class Sched:
    def __init__(self, nc, stack):
        self.nc = nc
        self.stack = stack
        self.engs = {}
        self.nsem = 0
        self.n_wait = 0

    def new_sem(self, name):
        self.nsem += 1
        return self.stack.enter_context(self.nc.semaphore(name))

    def add_engine(self, name, eng):
        e = Eng(name, eng, self.new_sem("s_" + name))
        self.engs[name] = e
        return e

    def _need(self, need, evs):
        for k, (sem, val) in evs.items():
            if k not in need or need[k][1] < val:
                need[k] = (sem, val)

    def _waits(self, E, reads, writes):
        need = {}
        for b in reads:
            self._need(need, b.w)
        for b in writes:
            self._need(need, b.w)
            self._need(need, b.r)
        for k, (sem, val) in need.items():
            if E.seen.get(k, 0) < val:
                E.prog.append(("w", sem, val))
                E.seen[k] = val
                self.n_wait += 1

    def _record(self, ev, reads, writes):
        k = id(ev[0])
        for b in writes:
            b.w = {k: ev}
            b.r = {}
        for b in reads:
            if b in writes:
                continue
            b.r[k] = ev

    def op(self, E, reads, writes, fn):
        self._waits(E, reads, writes)
        E.count += 1
        E.prog.append(("o", fn, E.sem, 1))
        ev = (E.sem, E.count)
        self._record(ev, reads, writes)
        return ev

    def dma(self, E, reads, writes, fn, owner, ndma=1):
        if owner.dsem is None:
            owner.dsem = DSem(self.new_sem("d_" + owner.name))
        self._waits(E, reads, writes)
        E.prog.append(("d", fn, owner.dsem.sem, 16))
        owner.dsem.count += 16 * ndma
        ev = (owner.dsem.sem, owner.dsem.count)
        self._record(ev, reads, writes)
        return ev

    def wait_all(self, E, bufs):
        need = {}
        for b in bufs:
            self._need(need, b.w)
            self._need(need, b.r)
        for k, (sem, val) in need.items():
            if E.seen.get(k, 0) < val:
                E.prog.append(("w", sem, val))
                E.seen[k] = val

    def replay(self, E):
        for it in E.prog:
            if it[0] == "w":
                E.eng.wait_ge(it[1], it[2])
            elif it[0] == "o":
                ins = it[1]()
                if isinstance(ins, (list, tuple)):
                    ins = ins[-1]
                ins.then_inc(it[2], it[3])
            else:
                ins = it[1]()
                if not isinstance(ins, (list, tuple)):
                    ins = [ins]
                for i in ins:
                    i.then_inc(it[2], it[3])

from contextlib import ExitStack
from concourse.bass_utils import run_bass_kernel_spmd

NCORES = 8
D = 2048
DC = 16
DFF = 5632
FC = 44
EPS = 1e-6


class Ctx:
    def __init__(self):
        self.nc = nc = bass.Bass("TRN2", target_bir_lowering=False)
        self.st = ExitStack()
        self.S = S = Sched(nc, self.st)
        self.PE = S.add_engine("pe", nc.tensor)
        self.ACT = S.add_engine("act", nc.scalar)
        self.DVE = S.add_engine("dve", nc.vector)
        self.POOL = S.add_engine("pool", nc.gpsimd)
        self.SP = S.add_engine("sp", nc.sync)
        self.n = 0
        self.capture = None

    def _emit(self, f):
        if self.capture is not None:
            self.capture.append(f)
        else:
            f()

    def sb(self, shape, dt, name="t"):
        self.n += 1
        nm = "%s_%d" % (name, self.n)
        return self.st.enter_context(self.nc.sbuf_tensor(nm, shape, dt)), Buf(nm)

    def ps(self, shape, dt, name="p"):
        self.n += 1
        nm = "%s_%d" % (name, self.n)
        return self.st.enter_context(self.nc.psum_tensor(nm, shape, dt)), Buf(nm)

    def din(self, name, shape, dt=F32):
        return self.nc.dram_tensor(name, shape, dt, kind="ExternalInput")

    def dout(self, name, shape, dt=F32):
        return self.nc.dram_tensor(name, shape, dt, kind="ExternalOutput")

    def dscr(self, name, shape, dt=F32):
        return self.nc.dram_tensor(name, shape, dt)

    def act(self, reads, writes, out, in_, func, **kw):
        nc = self.nc
        self._emit(lambda: self.S.op(self.ACT, reads, writes, lambda: nc.scalar.activation(out=out, in_=in_, func=func, **kw)))

    def dve(self, reads, writes, fn):
        self._emit(lambda: self.S.op(self.DVE, reads, writes, fn))

    def pe(self, reads, writes, fn):
        self._emit(lambda: self.S.op(self.PE, reads, writes, fn))

    def load(self, E, out, in_, wbuf, rbufs=()):
        eng = E.eng
        self._emit(lambda: self.S.dma(E, list(rbufs), [wbuf], lambda: eng.dma_start(out=out, in_=in_), wbuf))

    def store(self, E, out, in_, rbuf, wbuf):
        eng = E.eng
        self._emit(lambda: self.S.dma(E, [rbuf], [wbuf], lambda: eng.dma_start(out=out, in_=in_), rbuf))

    def captured(self, fn, *args):
        self.capture = lst = []
        fn(*args)
        self.capture = None
        return lst

    @staticmethod
    def merge(A, B):
        ia = ib = 0
        while ia < len(A) or ib < len(B):
            if ib >= len(B) or (ia < len(A) and ia * len(B) <= ib * len(A)):
                A[ia](); ia += 1
            else:
                B[ib](); ib += 1

    def finish(self, out_bufs):
        S, nc = self.S, self.nc
        for E in (self.SP, self.POOL):
            S.wait_all(E, out_bufs)
        with nc.Block() as block:
            @block.tensor
            def _(e): S.replay(self.PE)
            @block.scalar
            def _(e): S.replay(self.ACT)
            @block.vector
            def _(e): S.replay(self.DVE)
            @block.gpsimd
            def _(e): S.replay(self.POOL)
            @block.sync
            def _(e): S.replay(self.SP)
        self.st.close()
        return nc


def rstd_from_psum(C, ps_ss, Bps, out, Bout, n, epsc, Beps):
    C.act([Bps, Beps], [Bout], out, ps_ss, AF.Ln, scale=1.0 / n, bias=epsc)
    C.act([Bout], [Bout], out, out, AF.Exp, scale=-0.5)


def build_mixer(ntok, layer):
    C = Ctx()
    nc, S = C.nc, C.S
    PE, ACT, DVE, POOL, SP = C.PE, C.ACT, C.DVE, C.POOL, C.SP
    TL = 512
    NTL = ntok // TL
    xT = C.din("xT", [D, ntok])
    wA = C.din("wA", [128, DC, 1024])
    nw_d = C.din("nw", [128, DC])
    lb_d = C.din("lb", [128, 4])
    hw_d = C.din("hw", [128, 1])
    cw_d = C.din("cw", [128, 3])
    ident_d = C.din("ident", [128, 128])
    mask_d = C.din("masks", [64, 128])
    rmask_d = C.din("rmask", [128, TL])
    mixT = C.dout("mixT", [256, ntok])
    obwd = C.dscr("obwd", [128, ntok])
    xv = xT.ap().rearrange("(c p) t -> p c t", p=128)

    w_bf, Bw = C.sb([128, DC, 1024], BF16, "w")
    C.load(POOL, w_bf[:], wA.ap(), Bw)
    nw, Bnw = C.sb([128, DC], F32, "nw"); C.load(SP, nw[:], nw_d.ap(), Bnw)
    lb, Blb = C.sb([128, 4], F32, "lb"); C.load(SP, lb[:], lb_d.ap(), Blb)
    hw, Bhw = C.sb([128, 1], F32, "hw"); C.load(SP, hw[:], hw_d.ap(), Bhw)
    cw, Bcw = C.sb([128, 3], F32, "cw"); C.load(SP, cw[:], cw_d.ap(), Bcw)
    idf, Bidf = C.sb([128, 128], F32, "idf"); C.load(SP, idf[:], ident_d.ap(), Bidf)
    mk, Bmk = C.sb([64, 128], F32, "mk"); C.load(SP, mk[:], mask_d.ap(), Bmk)
    rm, Brm = C.sb([128, TL], F32, "rm"); C.load(SP, rm[:], rmask_d.ap(), Brm)
    idb, Bidb = C.sb([128, 128], BF16, "idb")
    C.dve([Bidf], [Bidb], lambda: nc.vector.tensor_copy(out=idb[:], in_=idf[:]))
    ones, Bones = C.sb([128, 128], BF16, "ones")
    C.dve([], [Bones], lambda: nc.vector.memset(ones[:], 1.0))
    epsc, Beps = C.sb([128, 1], F32, "eps")
    C.dve([], [Beps], lambda: nc.vector.memset(epsc[:], EPS))
    lbp, Blbp = C.sb([128, 6], F32, "lbp")
    for d_ in range(2):
        c0 = d_ * 3
        if layer == 0:
            C.dve([], [Blbp], lambda c0=c0: nc.vector.memset(lbp[:, c0:c0 + 1], 0.0))
        else:
            C.dve([Blb], [Blbp], lambda c0=c0, d_=d_: nc.vector.tensor_tensor(
                out=lbp[:, c0:c0 + 1], in0=lb[:, 2 * d_ + 1:2 * d_ + 2], in1=lb[:, 2 * d_:2 * d_ + 1], op=ALU.subtract))
            C.act([Blbp], [Blbp], lbp[:, c0:c0 + 1], lbp[:, c0:c0 + 1], AF.Sigmoid)
        C.dve([Blbp], [Blbp], lambda c0=c0: nc.vector.tensor_scalar(
            out=lbp[:, c0 + 1:c0 + 2], in0=lbp[:, c0:c0 + 1], scalar1=-1.0, scalar2=1.0, op0=ALU.mult, op1=ALU.add))
        C.dve([Blbp], [Blbp], lambda c0=c0: nc.vector.tensor_scalar(
            out=lbp[:, c0 + 2:c0 + 3], in0=lbp[:, c0 + 1:c0 + 2], scalar1=-1.0, scalar2=None, op0=ALU.mult))

    hTd = C.dscr("hTd", [NTL, 128, DC, TL], BF16)
    xtt, Bxt = C.sb([128, DC, TL], F32, "xt")
    sq, Bsq = C.sb([128, DC, TL], BF16, "sq")
    hTs = [C.sb([128, DC, TL], BF16, "hT") for _ in range(2)]
    def t32(name): return C.sb([128, TL], F32, name)
    rstd, Brstd = t32("rstd"); qs, Bqs = t32("qs"); sig, Bsig = t32("sig"); g, Bg = t32("g")
    kf, Bkf = t32("kf"); bb, Bbb = t32("bb"); cc, Bcc = t32("cc"); Ei, BEi = t32("Ei")
    osum, Bosum = t32("osum"); ro, Bro = t32("ro")
    yr, Byr = t32("yr"); yc, Byc = t32("yc"); ycv, Bycv = t32("ycv"); obs, Bobs = t32("obs")
    Es = [t32("E") for _ in range(2)]; sgs = [t32("sg") for _ in range(2)]; obl = [t32("ob") for _ in range(2)]
    osq, Bosq = C.sb([128, TL], BF16, "osq")
    qbs = [C.sb([128, TL], BF16, "qb") for _ in range(2)]; kbs = [C.sb([128, TL], BF16, "kb") for _ in range(2)]
    kbts = [C.sb([64, 8, 128], BF16, "kbt") for _ in range(2)]; vts = [C.sb([64, 8, 128], BF16, "vt") for _ in range(2)]
    scT, BscT = C.sb([64, 64], BF16, "scT")
    Sb, BSb = C.sb([128, 128], BF16, "Sb")
    ub = [C.sb([128, TL], F32, "ub") for _ in range(3)]
    gb = [C.sb([128, TL], F32, "gb") for _ in range(3)]
    ps_ss, Bpss = C.ps([128, TL], F32, "pss")
    pp = [C.ps([128, TL], F32, "pp") for _ in range(3)]
    ps_o, Bpo = C.ps([128, TL], F32, "po")
    ps_m, Bpm = C.ps([128, TL], F32, "pm")
    ps_v, Bpv = C.ps([64, TL], F32, "pv")
    ps_t, Bpt = C.ps([64, 8, 128], BF16, "pt")
    ppi = [0]

    Bx_out = [Buf("mixo%d" % i) for i in range(2 * NTL + 2)]
    Bobwd = [Buf("obwd%d" % i) for i in range(NTL)]
    BhTd = [Buf("hTd%d" % i) for i in range(NTL)]

    def load_x(ti):
        C.load(SP, xtt[:], xv[:, :, ti * TL:(ti + 1) * TL], Bxt)

    def norm_tile(ti, par):
        hT, BhT = hTs[par]
        C.act([Bxt], [Bsq], sq[:], xtt[:], AF.Square)
        C.pe([Bsq, Bones], [Bpss], lambda: [nc.tensor.matmul(ps_ss[:], lhsT=ones[:], rhs=sq[:, c, :], start=(c == 0), stop=(c == DC - 1)) for c in range(DC)])
        rstd_from_psum(C, ps_ss[:], Bpss, rstd[:], Brstd, D, epsc[:], Beps)
        for c in range(DC):
            C.dve([Bxt, Bnw, Brstd], [BhT], lambda c=c: nc.vector.scalar_tensor_tensor(
                out=hT[:, c, :], in0=xtt[:, c, :], scalar=nw[:, c:c + 1], in1=rstd[:], op0=ALU.mult, op1=ALU.mult))
        C.store(POOL, hTd.ap()[ti], hT[:], BhT, BhTd[ti])

    def proj(gi, par):
        hT, BhT = hTs[par]
        p, Bp = pp[ppi[0] % 3]; ppi[0] += 1
        C.pe([BhT, Bw], [Bp], lambda: [nc.tensor.matmul(p[:], lhsT=w_bf[:, c, gi * 128:(gi + 1) * 128], rhs=hT[:, c, :], start=(c == 0), stop=(c == DC - 1)) for c in range(DC)])
        return p, Bp

    def vtok(par):
        hT, BhT = hTs[par]
        vt, Bvt = vts[par]
        for hh in range(2):
            C.pe([BhT, Bw], [Bpv], lambda hh=hh: [nc.tensor.matmul(
                ps_v[:, j * 128:(j + 1) * 128], lhsT=hT[:, c, (hh * 4 + j) * 64:(hh * 4 + j + 1) * 64], rhs=w_bf[:, c, 3 * 128:4 * 128],
                start=(c == 0), stop=(c == DC - 1)) for j in range(4) for c in range(DC)])
            C.act([Bpv], [Bvt], vt[:, hh * 4:(hh + 1) * 4, :], ps_v[:].rearrange("p (j v) -> p j v", v=128), AF.Copy)

    def gates(pz, Bpz, dirn):
        c0 = dirn * 3
        C.act([Bpz], [Bsig], sig[:], pz[:], AF.Sigmoid)
        C.act([Bsig, Blbp], [Bg], g[:], sig[:], AF.Ln, scale=lbp[:, c0 + 1:c0 + 2], bias=lbp[:, c0:c0 + 1])
        C.dve([Bsig, Blbp], [Bkf], lambda: nc.vector.tensor_scalar(out=kf[:], in0=sig[:], scalar1=lbp[:, c0 + 2:c0 + 3], scalar2=lbp[:, c0 + 1:c0 + 2], op0=ALU.mult, op1=ALU.add))
        C.dve([Bg, Brm], [Bbb], lambda: nc.vector.tensor_tensor_scan(out=bb[:], data0=rm[:], data1=g[:], initial=0.0, op0=ALU.mult, op1=ALU.add))

    def decays(src, Bsrc, par):
        E, BE = Es[par]; qb, Bqb = qbs[par]; kb, Bkb = kbs[par]; kbt, Bkbt = kbts[par]
        C.act([Bsrc], [BE], E[:], src[:], AF.Exp)
        C.act([Bsrc], [BEi], Ei[:], src[:], AF.Exp, scale=-1.0)
        C.dve([Bqs, BE], [Bqb], lambda: nc.vector.tensor_tensor(out=qb[:], in0=qs[:], in1=E[:], op=ALU.mult))
        C.dve([Bkf, BEi], [Bkb], lambda: nc.vector.tensor_tensor(out=kb[:], in0=kf[:], in1=Ei[:], op=ALU.mult))
        C.pe([Bkb, Bidb], [Bpt], lambda: [nc.tensor.transpose(out=ps_t[:, j, :], in_=kb[:, j * 64:(j + 1) * 64], identity=idb[:]) for j in range(8)])
        C.dve([Bpt], [Bkbt], lambda: nc.vector.tensor_copy(out=kbt[:], in_=ps_t[:]))

    def chunks(order, mcol, dcol, par):
        E, BE = Es[par]; qb, Bqb = qbs[par]; kb, Bkb = kbs[par]; kbt, Bkbt = kbts[par]; vt, Bvt = vts[par]
        for j in order:
            cs = slice(j * 64, (j + 1) * 64)
            dc_ = j * 64 + dcol
            C.pe([Bkb, Bqb], [Bpm], lambda cs=cs: nc.tensor.matmul(ps_m[0:64, 0:64], lhsT=kb[:, cs], rhs=qb[:, cs], start=True, stop=True))
            C.dve([Bpm, Bmk], [BscT], lambda mcol=mcol: nc.vector.tensor_tensor(out=scT[:], in0=ps_m[0:64, 0:64], in1=mk[:, mcol:mcol + 64], op=ALU.mult))
            C.pe([Bvt, BscT, BSb, Bqb], [Bpo], lambda cs=cs, j=j: [
                nc.tensor.matmul(ps_o[:, cs], lhsT=vt[:, j, :], rhs=scT[:], start=True, stop=False),
                nc.tensor.matmul(ps_o[:, cs], lhsT=Sb[:], rhs=qb[:, cs], start=False, stop=True)])
            C.pe([Bidb, BSb, Bkbt, Bvt], [Bpm], lambda j=j: [
                nc.tensor.matmul(ps_m[:, 128:256], lhsT=idb[:], rhs=Sb[:], start=True, stop=False),
                nc.tensor.matmul(ps_m[:, 128:256], lhsT=kbt[:, j, :], rhs=vt[:, j, :], start=False, stop=True)])
            C.act([Bpm, BE], [BSb], Sb[:], ps_m[:, 128:256], AF.Copy, scale=E[:, dc_:dc_ + 1])

    def prep1(ti, par):
        norm_tile(ti, par)
        if ti > 0:
            load_x(ti - 1)
        pq, Bpq = proj(0, par)
        C.act([Bpq], [Bqs], qs[:], pq[:], AF.Silu)
        pz, Bpz = proj(2, par)
        gates(pz, Bpz, 1)
        vtok(par)
        C.dve([Bg, Bbb], [Bcc], lambda: nc.vector.tensor_tensor(out=cc[:], in0=g[:], in1=bb[:], op=ALU.subtract))
        C.dve([Bcc, Bbb], [Bcc], lambda: nc.vector.tensor_tensor(
            out=cc[:].rearrange("p (c t) -> p c t", t=64), in0=cc[:].rearrange("p (c t) -> p c t", t=64),
            in1=bb[:].rearrange("p (c t) -> p c t", t=64)[:, :, 63:64].to_broadcast([128, 8, 64]), op=ALU.add))
        decays(cc, Bcc, par)

    def main1(ti, par):
        chunks(range(7, -1, -1), 64, 0, par)
        C.act([Bpo], [Bobs], obs[:], ps_o[:], AF.Copy)
        C.store(SP, obwd.ap()[:, ti * TL:(ti + 1) * TL], obs[:], Bobs, Bobwd[ti])

    C.dve([], [BSb], lambda: nc.vector.memset(Sb[:], 0.0))
    load_x(NTL - 1)
    prep1(NTL - 1, (NTL - 1) % 2)
    for ti in range(NTL - 1, -1, -1):
        A = C.captured(main1, ti, ti % 2)
        Bl = C.captured(prep1, ti - 1, (ti - 1) % 2) if ti > 0 else []
        C.merge(A, Bl)

    def conv_final(ti, left, right):
        u, Bu = ub[ti % 3]; gt, Bgt = gb[ti % 3]
        C.dve([Bu, Bcw], [Byc], lambda: nc.vector.tensor_scalar(out=yc[:], in0=u[:], scalar1=cw[:, 1:2], scalar2=None, op0=ALU.mult))
        C.dve([Bu, Bcw, Byc], [Byc], lambda: nc.vector.scalar_tensor_tensor(out=yc[:, 1:TL], in0=u[:, 0:TL - 1], scalar=cw[:, 0:1], in1=yc[:, 1:TL], op0=ALU.mult, op1=ALU.add))
        C.dve([Bu, Bcw, Byc], [Byc], lambda: nc.vector.scalar_tensor_tensor(out=yc[:, 0:TL - 1], in0=u[:, 1:TL], scalar=cw[:, 2:3], in1=yc[:, 0:TL - 1], op0=ALU.mult, op1=ALU.add))
        if left is not None:
            la, Bl = left
            C.dve([Bl, Bcw, Byc], [Byc], lambda: nc.vector.scalar_tensor_tensor(out=yc[:, 0:1], in0=la, scalar=cw[:, 0:1], in1=yc[:, 0:1], op0=ALU.mult, op1=ALU.add))
        if right is not None:
            ra, Br = right
            C.dve([Br, Bcw, Byc], [Byc], lambda: nc.vector.scalar_tensor_tensor(out=yc[:, TL - 1:TL], in0=ra, scalar=cw[:, 2:3], in1=yc[:, TL - 1:TL], op0=ALU.mult, op1=ALU.add))
        C.dve([Byc, Bgt], [Bycv], lambda: nc.vector.tensor_tensor(out=ycv[:], in0=yc[:], in1=gt[:], op=ALU.mult))
        C.store(SP, mixT.ap()[128:256, ti * TL:(ti + 1) * TL], ycv[:], Bycv, Bx_out[NTL + ti])

    def load_h(ti):
        hT, BhT = hTs[ti % 2]
        C.load(SP, hT[:], hTd.ap()[ti], BhT, [BhTd[ti]])

    def prep2(ti, par):
        ob, Bob = obl[par]; sg, Bsg = sgs[par]
        C.load(SP, ob[:], obwd.ap()[:, ti * TL:(ti + 1) * TL], Bob, [Bobwd[ti]])
        if ti + 1 < NTL:
            load_h(ti + 1)
        pq, Bpq = proj(0, par)
        C.act([Bpq], [Bqs], qs[:], pq[:], AF.Silu)
        pz, Bpz = proj(1, par)
        gates(pz, Bpz, 0)
        vtok(par)
        pg, Bpg = proj(4, par)
        C.act([Bpg], [Bsg], sg[:], pg[:], AF.Silu)
        decays(bb, Bbb, par)
        u, Bu = ub[ti % 3]; gt, Bgt = gb[ti % 3]
        p5, Bp5 = proj(5, par)
        C.act([Bp5], [Bgt], gt[:], p5[:], AF.Copy)
        p7, Bp7 = proj(7, par)
        C.act([Bp7], [Bu], u[:], p7[:], AF.Copy)
        p6, Bp6 = proj(6, par)
        C.dve([Bp6, Bu], [Bu], lambda u=u, p6=p6: nc.vector.tensor_tensor(out=u[:], in0=p6[:], in1=u[:], op=ALU.mult))
        if ti > 0:
            left = None
            if ti > 1:
                upp, Bupp = ub[(ti - 2) % 3]
                left = (upp[:, TL - 1:TL], Bupp)
            conv_final(ti - 1, left, (u[:, 0:1], Bu))

    def main2(ti, par):
        ob, Bob = obl[par]; sg, Bsg = sgs[par]
        chunks(range(8), 0, 63, par)
        C.dve([Bpo, Bob], [Bosum], lambda: nc.vector.tensor_tensor(out=osum[:], in0=ps_o[:], in1=ob[:], op=ALU.add))
        C.act([Bosum], [Bosq], osq[:], osum[:], AF.Square)
        C.pe([Bosq, Bones], [Bpss], lambda: nc.tensor.matmul(ps_ss[:], lhsT=ones[:], rhs=osq[:], start=True, stop=True))
        rstd_from_psum(C, ps_ss[:], Bpss, ro[:], Bro, 128, epsc[:], Beps)
        C.dve([Bosum, Bhw, Bro], [Byr], lambda: nc.vector.scalar_tensor_tensor(out=yr[:], in0=osum[:], scalar=hw[:, 0:1], in1=ro[:], op0=ALU.mult, op1=ALU.mult))
        C.dve([Byr, Bsg], [Byr], lambda: nc.vector.tensor_tensor(out=yr[:], in0=yr[:], in1=sg[:], op=ALU.mult))
        C.store(SP, mixT.ap()[0:128, ti * TL:(ti + 1) * TL], yr[:], Byr, Bx_out[ti])

    C.dve([], [BSb], lambda: nc.vector.memset(Sb[:], 0.0))
    load_h(0)
    prep2(0, 0)
    for ti in range(NTL):
        A = C.captured(main2, ti, ti % 2)
        Bl = C.captured(prep2, ti + 1, (ti + 1) % 2) if ti + 1 < NTL else []
        C.merge(A, Bl)
    left = None
    if NTL > 1:
        upp, Bupp = ub[(NTL - 2) % 3]
        left = (upp[:, TL - 1:TL], Bupp)
    conv_final(NTL - 1, left, None)
    return C.finish(Bx_out)


_CACHE = {}


def _consts():
    m = np.zeros((64, 128), np.float32)
    m[:, 0:64] = np.triu(np.ones((64, 64), np.float32))
    m[:, 64:128] = np.tril(np.ones((64, 64), np.float32))
    rm = np.ones((128, 512), np.float32)
    rm[:, ::64] = 0.0
    return {"ident": np.eye(128, dtype=np.float32), "masks": m, "rmask": rm}


def _pc(v):
    return np.ascontiguousarray(v.reshape(-1, 128).T)


def run_mixer(xT, p, layer):
    ntok = xT.shape[1]
    key = ("M", ntok, layer)
    if key not in _CACHE:
        _CACHE[key] = build_mixer(ntok, layer)
    nc = _CACHE[key]
    w_in = p["w_in"][layer]
    cst = _consts()
    ins = []
    for h in range(NCORES):
        cols = np.concatenate([np.arange(g * 1024 + h * 128, g * 1024 + (h + 1) * 128) for g in range(8)])
        wa = w_in[:, cols]
        wa = np.ascontiguousarray(wa.reshape(DC, 128, 1024).transpose(1, 0, 2))
        hs = slice(h * 128, (h + 1) * 128)
        lb = np.stack([p["lb_fwd"][0, hs], p["lb_fwd"][1, hs], p["lb_bwd"][0, hs], p["lb_bwd"][1, hs]], axis=1)
        d = {"xT": xT, "wA": wa, "nw": _pc(p["attn_norm_w"][layer]), "lb": np.ascontiguousarray(lb, dtype=np.float32),
             "hw": np.ascontiguousarray(p["hgrn_norm_w"][layer, hs].reshape(128, 1)),
             "cw": np.ascontiguousarray(p["conv_w"][layer][:, hs].T)}
        d.update(cst)
        ins.append(d)
    res = run_bass_kernel_spmd(nc, ins, core_ids=list(range(NCORES)))
    mixT = np.empty((D, ntok), np.float32)
    for h in range(NCORES):
        o = res.results[h]["mixT"]
        mixT[h * 128:(h + 1) * 128] = o[0:128]
        mixT[1024 + h * 128:1024 + (h + 1) * 128] = o[128:256]
    return mixT


def build_ffn(nt):
    C = Ctx()
    nc, S = C.nc, C.S
    PE, ACT, DVE, POOL, SP = C.PE, C.ACT, C.DVE, C.POOL, C.SP
    TL = 512
    HT = 1024
    NH = nt // HT
    NU = FC // 2
    xT = C.din("xT", [D, nt])
    mT = C.din("mT", [D, nt])
    wo_d = C.din("wo", [4, 128, DC, 512])
    wgu_d = C.din("wgu", [NU, 128, DC, 512])
    wd_d = C.din("wd", [NU, 128, 2, D])
    nw_d = C.din("nw", [128, DC])
    fw_d = C.din("fw", [128, DC])
    oX = C.dout("oX", [D, nt])
    oN = C.dout("oN", [D, nt])
    xv = xT.ap().rearrange("(c p) t -> p c t", p=128)
    mv = mT.ap().rearrange("(c p) t -> p c t", p=128)
    oXv = oX.ap().rearrange("(c p) t -> p c t", p=128)
    oNv = oN.ap().rearrange("(c p) t -> p c t", p=128)

    nw, Bnw = C.sb([128, DC], F32, "nw"); C.load(SP, nw[:], nw_d.ap(), Bnw)
    fw, Bfw = C.sb([128, DC], F32, "fw"); C.load(SP, fw[:], fw_d.ap(), Bfw)
    ones, Bones = C.sb([128, 128], BF16, "ones")
    C.dve([], [Bones], lambda: nc.vector.memset(ones[:], 1.0))
    epsc, Beps = C.sb([128, 1], F32, "eps")
    C.dve([], [Beps], lambda: nc.vector.memset(epsc[:], EPS))

    acc, _ = C.sb([128, DC, HT], F32, "acc")
    Bacc = [[Buf("acc%d_%d" % (c, t)) for t in range(2)] for c in range(DC)]
    Baccl = [b for r in Bacc for b in r]
    mb, _ = C.sb([128, DC, HT], BF16, "mb")
    Bmb = [Buf("mb%d" % t) for t in range(2)]
    h2, Bh2 = C.sb([128, DC, HT], BF16, "h2")
    Bh2t = [Buf("h2_%d" % t) for t in range(2)]
    rstd, _ = C.sb([128, HT], F32, "rstd")
    Brs = [Buf("rs%d" % t) for t in range(2)]
    wg = [C.sb([128, DC, 512], BF16, "wg") for _ in range(2)]
    wd = [C.sb([128, 2, D], BF16, "wd") for _ in range(2)]
    aT = [C.sb([128, 2, HT], BF16, "aT") for _ in range(2)]
    sgt = [C.sb([128, TL], F32, "sg") for _ in range(2)]
    outn, Boutn = C.sb([128, 4, TL], F32, "outn")
    pss, Bpss = C.ps([128, TL], F32, "pss")
    pg = [C.ps([128, TL], F32, "pg") for _ in range(2)]
    pu = [C.ps([128, TL], F32, "pu") for _ in range(2)]
    pd = [C.ps([128, TL], F32, "pd") for _ in range(3)]
    Bouts = []
    cnt = {"g": 0, "d": 0, "s": 0}

    def norm_half(wvec, Bwv, dst, Bdst_t):
        for tt in range(2):
            ts = slice(tt * TL, (tt + 1) * TL)
            C.act([Bacc[c][tt] for c in range(DC)], [Bmb[tt]], mb[:, :, ts], acc[:, :, ts], AF.Square)
            C.pe([Bmb[tt], Bones], [Bpss], lambda ts=ts: [nc.tensor.matmul(pss[:], lhsT=ones[:], rhs=mb[:, c, ts], start=(c == 0), stop=(c == DC - 1)) for c in range(DC)])
            rstd_from_psum(C, pss[:], Bpss, rstd[:, ts], Brs[tt], D, epsc[:], Beps)

    for hf in range(NH):
        t0 = hf * HT
        for tt in range(2):
            ts = slice(tt * TL, (tt + 1) * TL)
            for c4 in range(4):
                cs = slice(c4 * 4, c4 * 4 + 4)
                bl = [Bacc[c][tt] for c in range(c4 * 4, c4 * 4 + 4)]
                owner = bl[0]
                S.dma(SP, [], bl, lambda cs=cs, ts=ts, tt=tt, t0=t0: nc.sync.dma_start(out=acc[:, cs, ts], in_=xv[:, cs, t0 + tt * TL:t0 + (tt + 1) * TL]), owner)
            S.dma(POOL, [], [Bmb[tt]], lambda ts=ts, tt=tt, t0=t0: nc.gpsimd.dma_start(out=mb[:, :, ts], in_=mv[:, :, t0 + tt * TL:t0 + (tt + 1) * TL]), Bmb[tt])
        for uo in range(4):
            w, Bw = wg[cnt["g"] % 2]; cnt["g"] += 1
            C.load(POOL, w[:], wo_d.ap()[uo], Bw)
            for dl in range(4):
                dc = uo * 4 + dl
                for tt in range(2):
                    ts = slice(tt * TL, (tt + 1) * TL)
                    p, Bp = pd[cnt["d"] % 3]; cnt["d"] += 1
                    C.pe([Bw, Bmb[tt]], [Bp], lambda w=w, p=p, dl=dl, ts=ts: [nc.tensor.matmul(p[:], lhsT=w[:, e, dl * 128:(dl + 1) * 128], rhs=mb[:, e, ts], start=(e == 0), stop=(e == DC - 1)) for e in range(DC)])
                    C.dve([Bp, Bacc[dc][tt]], [Bacc[dc][tt]], lambda p=p, dc=dc, ts=ts: nc.vector.tensor_tensor(out=acc[:, dc, ts], in0=p[:], in1=acc[:, dc, ts], op=ALU.add))
        norm_half(nw, Bnw, h2, Bh2t)
        for tt in range(2):
            ts = slice(tt * TL, (tt + 1) * TL)
            for c in range(DC):
                C.dve([Bacc[c][tt], Bnw, Brs[tt]], [Bh2t[tt]], lambda c=c, ts=ts: nc.vector.scalar_tensor_tensor(
                    out=h2[:, c, ts], in0=acc[:, c, ts], scalar=nw[:, c:c + 1], in1=rstd[:, ts], op0=ALU.mult, op1=ALU.mult))
        for u in range(NU):
            w, Bw = wg[cnt["g"] % 2]; cnt["g"] += 1
            w2, Bw2 = wd[u % 2]
            a, Ba = aT[u % 2]
            C.load(POOL, w[:], wgu_d.ap()[u], Bw)
            C.load(POOL, w2[:], wd_d.ap()[u], Bw2)
            for fl in range(2):
                for tt in range(2):
                    ts = slice(tt * TL, (tt + 1) * TL)
                    k = cnt["s"] % 2; cnt["s"] += 1
                    pgk, Bpg = pg[k]; puk, Bpu = pu[k]; sgk, Bsg = sgt[k]
                    C.pe([Bw, Bh2t[tt]], [Bpg], lambda w=w, pgk=pgk, fl=fl, ts=ts: [nc.tensor.matmul(pgk[:], lhsT=w[:, c, fl * 128:(fl + 1) * 128], rhs=h2[:, c, ts], start=(c == 0), stop=(c == DC - 1)) for c in range(DC)])
                    C.pe([Bw, Bh2t[tt]], [Bpu], lambda w=w, puk=puk, fl=fl, ts=ts: [nc.tensor.matmul(puk[:], lhsT=w[:, c, 256 + fl * 128:256 + (fl + 1) * 128], rhs=h2[:, c, ts], start=(c == 0), stop=(c == DC - 1)) for c in range(DC)])
                    C.act([Bpg], [Bsg], sgk[:], pgk[:], AF.Silu)
                    C.dve([Bsg, Bpu], [Ba], lambda a=a, sgk=sgk, puk=puk, fl=fl, ts=ts: nc.vector.tensor_tensor(out=a[:, fl, ts], in0=sgk[:], in1=puk[:], op=ALU.mult))
            for dc in range(DC):
                for tt in range(2):
                    ts = slice(tt * TL, (tt + 1) * TL)
                    p, Bp = pd[cnt["d"] % 3]; cnt["d"] += 1
                    C.pe([Bw2, Ba], [Bp], lambda w2=w2, a=a, p=p, dc=dc, ts=ts: [nc.tensor.matmul(p[:], lhsT=w2[:, fl, dc * 128:(dc + 1) * 128], rhs=a[:, fl, ts], start=(fl == 0), stop=(fl == 1)) for fl in range(2)])
                    C.dve([Bp, Bacc[dc][tt]], [Bacc[dc][tt]], lambda p=p, dc=dc, ts=ts: nc.vector.tensor_tensor(out=acc[:, dc, ts], in0=p[:], in1=acc[:, dc, ts], op=ALU.add))
        for tt in range(2):
            ts = slice(tt * TL, (tt + 1) * TL)
            for c4 in range(4):
                cs = slice(c4 * 4, c4 * 4 + 4)
                bl = [Bacc[c][tt] for c in range(c4 * 4, c4 * 4 + 4)]
                bo = Buf("ox"); Bouts.append(bo)
                S.dma(SP, bl, [bo], lambda cs=cs, ts=ts, tt=tt, t0=t0: nc.sync.dma_start(out=oXv[:, cs, t0 + tt * TL:t0 + (tt + 1) * TL], in_=acc[:, cs, ts]), bl[0])
        norm_half(fw, Bfw, None, None)
        for tt in range(2):
            ts = slice(tt * TL, (tt + 1) * TL)
            for c4 in range(4):
                for cl in range(4):
                    c = c4 * 4 + cl
                    C.dve([Bacc[c][tt], Bfw, Brs[tt]], [Boutn], lambda c=c, cl=cl, ts=ts: nc.vector.scalar_tensor_tensor(
                        out=outn[:, cl, :], in0=acc[:, c, ts], scalar=fw[:, c:c + 1], in1=rstd[:, ts], op0=ALU.mult, op1=ALU.mult))
                bo = Buf("on"); Bouts.append(bo)
                S.dma(SP, [Boutn], [bo], lambda tt=tt, c4=c4, t0=t0: nc.sync.dma_start(out=oNv[:, c4 * 4:c4 * 4 + 4, t0 + tt * TL:t0 + (tt + 1) * TL], in_=outn[:]), Boutn)
    return C.finish(Bouts)


def run_ffn(xT, mixT, p, layer):
    ntok = xT.shape[1]
    nt = ntok // NCORES
    key = ("F", nt)
    if key not in _CACHE:
        _CACHE[key] = build_ffn(nt)
    nc = _CACHE[key]
    w_out = p["w_out"][layer]; w_gu = p["w_gate_up"][layer]; w_dn = p["w_down"][layer]
    wo = np.ascontiguousarray(w_out.reshape(DC, 128, 4, 512).transpose(2, 1, 0, 3))
    NU = FC // 2
    gcols = w_gu[:, :DFF].reshape(D, NU, 256); ucols = w_gu[:, DFF:].reshape(D, NU, 256)
    wgu = np.concatenate([gcols, ucols], axis=2)
    wgu = np.ascontiguousarray(wgu.reshape(DC, 128, NU, 512).transpose(2, 1, 0, 3))
    wd = np.ascontiguousarray(w_dn.reshape(NU, 2, 128, D).transpose(0, 2, 1, 3))
    common = {"wo": wo, "wgu": wgu, "wd": wd, "nw": _pc(p["ffn_norm_w"][layer]), "fw": _pc(p["final_norm_w"])}
    ins = []
    for c in range(NCORES):
        d = {"xT": np.ascontiguousarray(xT[:, c * nt:(c + 1) * nt]), "mT": np.ascontiguousarray(mixT[:, c * nt:(c + 1) * nt])}
        d.update(common)
        ins.append(d)
    res = run_bass_kernel_spmd(nc, ins, core_ids=list(range(NCORES)))
    x2 = np.concatenate([res.results[c]["oX"] for c in range(NCORES)], axis=1)
    xn = np.concatenate([res.results[c]["oN"] for c in range(NCORES)], axis=1)
    return x2, xn


def kernel(x, attn_norm_w, w_in, lb_fwd, lb_bwd, hgrn_norm_w, conv_w, w_out, ffn_norm_w, w_gate_up, w_down, final_norm_w):
    p = {"attn_norm_w": np.asarray(attn_norm_w, np.float32), "w_in": np.asarray(w_in, np.float32),
         "lb_fwd": np.asarray(lb_fwd, np.float32), "lb_bwd": np.asarray(lb_bwd, np.float32),
         "hgrn_norm_w": np.asarray(hgrn_norm_w, np.float32), "conv_w": np.asarray(conv_w, np.float32),
         "w_out": np.asarray(w_out, np.float32), "ffn_norm_w": np.asarray(ffn_norm_w, np.float32),
         "w_gate_up": np.asarray(w_gate_up, np.float32), "w_down": np.asarray(w_down, np.float32),
         "final_norm_w": np.asarray(final_norm_w, np.float32)}
    x = np.asarray(x, np.float32)
    xT = np.ascontiguousarray(x[0].T)
    xn = None
    for layer in range(2):
        mixT = run_mixer(xT, p, layer)
        xT, xn = run_ffn(xT, mixT, p, layer)
    return np.ascontiguousarray(xn.T)[None].astype(np.float32)
```

```python
import numpy as np
import concourse.bass as bass
import concourse.mybir as mybir

F32 = mybir.dt.float32
BF16 = mybir.dt.bfloat16
AF = mybir.ActivationFunctionType
ALU = mybir.AluOpType
AX = mybir.AxisListType


class Buf:
    __slots__ = ("name", "w", "r", "dsem")

    def __init__(self, name):
        self.name = name
        self.w = {}
        self.r = {}
        self.dsem = None


class DSem:
    def __init__(self, sem):
        self.sem = sem
        self.count = 0


class Eng:
    def __init__(self, name, eng, sem):
        self.name = name
        self.eng = eng
        self.sem = sem
        self.count = 0
        self.seen = {}
        self.prog = []


class Sched:
    def __init__(self, nc, stack):
        self.nc = nc
        self.stack = stack
        self.engs = {}
        self.nsem = 0
        self.n_wait = 0
        self.dsems = []

    def barrier(self):
        evs = [(e.sem, e.count) for e in self.engs.values() if e.count > 0]
        evs += [(d.sem, d.count) for d in self.dsems if d.count > 0]
        for E in self.engs.values():
            for sem, val in evs:
                if sem is E.sem:
                    continue
                k = id(sem)
                if E.seen.get(k, 0) < val:
                    E.prog.append(("w", sem, val))
                    E.seen[k] = val

    def new_sem(self, name):
        self.nsem += 1
        return self.stack.enter_context(self.nc.semaphore(name))

    def add_engine(self, name, eng):
        e = Eng(name, eng, self.new_sem("s_" + name))
        self.engs[name] = e
        return e

    def _need(self, need, evs):
        for k, (sem, val) in evs.items():
            if k not in need or need[k][1] < val:
                need[k] = (sem, val)

    def _waits(self, E, reads, writes):
        need = {}
        for b in reads:
            self._need(need, b.w)
        for b in writes:
            self._need(need, b.w)
            self._need(need, b.r)
        for k, (sem, val) in need.items():
            if E.seen.get(k, 0) < val:
                E.prog.append(("w", sem, val))
                E.seen[k] = val
                self.n_wait += 1

    def _record(self, ev, reads, writes):
        k = id(ev[0])
        for b in writes:
            b.w = {k: ev}
            b.r = {}
        for b in reads:
            if b in writes:
                continue
            b.r[k] = ev

    def op(self, E, reads, writes, fn):
        self._waits(E, reads, writes)
        E.count += 1
        E.prog.append(("o", fn, E.sem, 1))
        ev = (E.sem, E.count)
        self._record(ev, reads, writes)
        return ev

    def dma(self, E, reads, writes, fn, owner, ndma=1):
        if owner.dsem is None:
            owner.dsem = DSem(self.new_sem("d_" + owner.name))
            self.dsems.append(owner.dsem)
        self._waits(E, reads, writes)
        E.prog.append(("d", fn, owner.dsem.sem, 16))
        owner.dsem.count += 16 * ndma
        ev = (owner.dsem.sem, owner.dsem.count)
        self._record(ev, reads, writes)
        return ev

    def wait_all(self, E, bufs):
        need = {}
        for b in bufs:
            self._need(need, b.w)
            self._need(need, b.r)
        for k, (sem, val) in need.items():
            if E.seen.get(k, 0) < val:
                E.prog.append(("w", sem, val))
                E.seen[k] = val

    def replay(self, E):
        for it in E.prog:
            if it[0] == "w":
                E.eng.wait_ge(it[1], it[2])
            elif it[0] == "o":
                ins = it[1]()
                if isinstance(ins, (list, tuple)):
                    ins = ins[-1]
                ins.then_inc(it[2], it[3])
            else:
                ins = it[1]()
                if not isinstance(ins, (list, tuple)):
                    ins = [ins]
                for i in ins:
                    i.then_inc(it[2], it[3])

from contextlib import ExitStack
from concourse.bass_utils import run_bass_kernel_spmd

NCORES = 8
D = 2048
DC = 16
DFF = 5632
FC = 44
EPS = 1e-6


_LAST = [None]


class Ctx:
    def __init__(self):
        _LAST[0] = self
        self.nc = nc = bass.Bass("TRN2", target_bir_lowering=False)
        self.st = ExitStack()
        self.S = S = Sched(nc, self.st)
        self.PE = S.add_engine("pe", nc.tensor)
        self.ACT = S.add_engine("act", nc.scalar)
        self.DVE = S.add_engine("dve", nc.vector)
        self.POOL = S.add_engine("pool", nc.gpsimd)
        self.SP = S.add_engine("sp", nc.sync)
        self.n = 0
        self.capture = None

    def _emit(self, f):
        if self.capture is not None:
            self.capture.append(f)
        else:
            f()

    def sb(self, shape, dt, name="t"):
        self.n += 1
        nm = "%s_%d" % (name, self.n)
        return self.st.enter_context(self.nc.sbuf_tensor(nm, shape, dt)), Buf(nm)

    def ps(self, shape, dt, name="p"):
        self.n += 1
        nm = "%s_%d" % (name, self.n)
        return self.st.enter_context(self.nc.psum_tensor(nm, shape, dt)), Buf(nm)

    def din(self, name, shape, dt=F32):
        return self.nc.dram_tensor(name, shape, dt, kind="ExternalInput")

    def dout(self, name, shape, dt=F32):
        return self.nc.dram_tensor(name, shape, dt, kind="ExternalOutput")

    def dscr(self, name, shape, dt=F32):
        return self.nc.dram_tensor(name, shape, dt)

    def act(self, reads, writes, out, in_, func, **kw):
        nc = self.nc
        self._emit(lambda: self.S.op(self.ACT, reads, writes, lambda: nc.scalar.activation(out=out, in_=in_, func=func, **kw)))

    def dve(self, reads, writes, fn):
        self._emit(lambda: self.S.op(self.DVE, reads, writes, fn))

    def pe(self, reads, writes, fn):
        self._emit(lambda: self.S.op(self.PE, reads, writes, fn))

    def load(self, E, out, in_, wbuf, rbufs=()):
        eng = E.eng
        self._emit(lambda: self.S.dma(E, list(rbufs), [wbuf], lambda: eng.dma_start(out=out, in_=in_), wbuf))

    def store(self, E, out, in_, rbuf, wbuf):
        eng = E.eng
        self._emit(lambda: self.S.dma(E, [rbuf], [wbuf], lambda: eng.dma_start(out=out, in_=in_), rbuf))

    def captured(self, fn, *args):
        self.capture = lst = []
        fn(*args)
        self.capture = None
        return lst

    @staticmethod
    def merge(A, B):
        ia = ib = 0
        while ia < len(A) or ib < len(B):
            if ib >= len(B) or (ia < len(A) and ia * len(B) <= ib * len(A)):
                A[ia](); ia += 1
            else:
                B[ib](); ib += 1

    def finish(self, out_bufs):
        S, nc = self.S, self.nc
        for E in (self.SP, self.POOL):
            S.wait_all(E, out_bufs)
        with nc.Block() as block:
            @block.tensor
            def _(e): S.replay(self.PE)
            @block.scalar
            def _(e): S.replay(self.ACT)
            @block.vector
            def _(e): S.replay(self.DVE)
            @block.gpsimd
            def _(e): S.replay(self.POOL)
            @block.sync
            def _(e): S.replay(self.SP)
        self.st.close()
        return nc


def rstd_from_psum(C, ps_ss, Bps, out, Bout, n, epsc, Beps):
    C.act([Bps, Beps], [Bout], out, ps_ss, AF.Ln, scale=1.0 / n, bias=epsc)
    C.act([Bout], [Bout], out, out, AF.Exp, scale=-0.5)


def build_mixer(ntok, layer):
    C = Ctx()
    nc, S = C.nc, C.S
    PE, ACT, DVE, POOL, SP = C.PE, C.ACT, C.DVE, C.POOL, C.SP
    TL = 512
    NTL = ntok // TL
    xT = C.din("xT", [D, ntok])
    wA = C.din("wA", [128, DC, 1024])
    nw_d = C.din("nw", [128, DC])
    lb_d = C.din("lb", [128, 4])
    hw_d = C.din("hw", [128, 1])
    cw_d = C.din("cw", [128, 3])
    ident_d = C.din("ident", [128, 128])
    mask_d = C.din("masks", [64, 128])
    rmask_d = C.din("rmask", [128, TL])
    mixT = C.dout("mixT", [256, ntok])
    obwd = C.dscr("obwd", [128, ntok])
    xv = xT.ap().rearrange("(c p) t -> p c t", p=128)

    w_bf, Bw = C.sb([128, DC, 1024], BF16, "w")
    C.load(POOL, w_bf[:], wA.ap(), Bw)
    nw, Bnw = C.sb([128, DC], F32, "nw"); C.load(SP, nw[:], nw_d.ap(), Bnw)
    lb, Blb = C.sb([128, 4], F32, "lb"); C.load(SP, lb[:], lb_d.ap(), Blb)
    hw, Bhw = C.sb([128, 1], F32, "hw"); C.load(SP, hw[:], hw_d.ap(), Bhw)
    cw, Bcw = C.sb([128, 3], F32, "cw"); C.load(SP, cw[:], cw_d.ap(), Bcw)
    idf, Bidf = C.sb([128, 128], F32, "idf"); C.load(SP, idf[:], ident_d.ap(), Bidf)
    mk, Bmk = C.sb([64, 128], F32, "mk"); C.load(SP, mk[:], mask_d.ap(), Bmk)
    rm, Brm = C.sb([128, TL], F32, "rm"); C.load(SP, rm[:], rmask_d.ap(), Brm)
    idb, Bidb = C.sb([128, 128], BF16, "idb")
    C.dve([Bidf], [Bidb], lambda: nc.vector.tensor_copy(out=idb[:], in_=idf[:]))
    ones, Bones = C.sb([128, 128], BF16, "ones")
    C.dve([], [Bones], lambda: nc.vector.memset(ones[:], 1.0))
    epsc, Beps = C.sb([128, 1], F32, "eps")
    C.dve([], [Beps], lambda: nc.vector.memset(epsc[:], EPS))
    lbp, Blbp = C.sb([128, 6], F32, "lbp")
    for d_ in range(2):
        c0 = d_ * 3
        if layer == 0:
            C.dve([], [Blbp], lambda c0=c0: nc.vector.memset(lbp[:, c0:c0 + 1], 0.0))
        else:
            C.dve([Blb], [Blbp], lambda c0=c0, d_=d_: nc.vector.tensor_tensor(
                out=lbp[:, c0:c0 + 1], in0=lb[:, 2 * d_ + 1:2 * d_ + 2], in1=lb[:, 2 * d_:2 * d_ + 1], op=ALU.subtract))
            C.act([Blbp], [Blbp], lbp[:, c0:c0 + 1], lbp[:, c0:c0 + 1], AF.Sigmoid)
        C.dve([Blbp], [Blbp], lambda c0=c0: nc.vector.tensor_scalar(
            out=lbp[:, c0 + 1:c0 + 2], in0=lbp[:, c0:c0 + 1], scalar1=-1.0, scalar2=1.0, op0=ALU.mult, op1=ALU.add))
        C.dve([Blbp], [Blbp], lambda c0=c0: nc.vector.tensor_scalar(
            out=lbp[:, c0 + 2:c0 + 3], in0=lbp[:, c0 + 1:c0 + 2], scalar1=-1.0, scalar2=None, op0=ALU.mult))

    hTd = C.dscr("hTd", [NTL, 128, DC, TL], BF16)
    xts = [C.sb([128, DC, TL], F32, "xt") for _ in range(2)]
    sq, Bsq = C.sb([128, DC, TL], BF16, "sq")
    hTs = [C.sb([128, DC, TL], BF16, "hT"), (sq, Bsq)]
    def t32(name): return C.sb([128, TL], F32, name)
    alias_k = [0]
    def a32(name):
        k = alias_k[0]; alias_k[0] += 1
        assert k < DC
        return xts[1][0][:, k, :], Buf("%s_a%d" % (name, k))
    class _V:
        def __init__(self, ap): self.ap_ = ap
        def __getitem__(self, key): return self.ap_[key] if key != slice(None) else self.ap_
    def a32t(name):
        ap, b = a32(name)
        return _V(ap), b
    rstd, Brstd = t32("rstd"); sig, Bsig = t32("sig")
    bb, Bbb = t32("bb"); cc, Bcc = t32("cc"); Ei, BEi = t32("Ei")
    osum, Bosum = a32t("osum"); ro, Bro = a32t("ro")
    yr, Byr = a32t("yr"); yc, Byc = a32t("yc"); ycv, Bycv = t32("ycv"); obs, Bobs = t32("obs")
    qss = [t32("qs") for _ in range(2)]; gs = [t32("g") for _ in range(2)]; kfs = [t32("kf") for _ in range(2)]
    Es = [t32("E") for _ in range(2)]; sgs = [a32t("sg") for _ in range(3)]; obl = [a32t("ob") for _ in range(3)]
    osq, Bosq = C.sb([128, TL], BF16, "osq")
    qbs = [C.sb([128, TL], BF16, "qb") for _ in range(2)]; kbs = [C.sb([128, TL], BF16, "kb") for _ in range(2)]
    kbts = [C.sb([64, 8, 128], BF16, "kbt") for _ in range(2)]; vts = [C.sb([64, 8, 128], BF16, "vt") for _ in range(3)]
    scTs = [C.sb([64, 8, 64], BF16, "scT") for _ in range(2)]
    U, BU = C.sb([128, 9, 128], F32, "U")
    Ub, BUb = C.sb([128, 8, 128], BF16, "Ub")
    Wd, BWd = C.sb([128, 8, 128], F32, "Wd")
    vTs, BvTs = C.sb([128, TL], BF16, "vTs")
    ub = [a32t("ub") for _ in range(3)]
    gb = [a32t("gb") for _ in range(3)]
    ps_ss, Bpss = C.ps([128, TL], F32, "pss")
    pp = [C.ps([128, TL], F32, "pp") for _ in range(2)]
    ps_o, Bpo = C.ps([128, TL], F32, "po")
    ps_m, Bpm = C.ps([64, TL], F32, "pm")
    ps_t, Bpt = C.ps([64, 8, 128], BF16, "pt")
    ps_t1, Bpt1 = C.ps([64, 8, 128], BF16, "pt1")
    pdS, BpdS = C.ps([128, 4, 128], F32, "pdS")
    ppi = [0]

    Bx_out = [Buf("mixo%d" % i) for i in range(2 * NTL + 2)]
    Bobwd = [Buf("obwd%d" % i) for i in range(NTL)]
    BhTd = [Buf("hTd%d" % i) for i in range(NTL)]

    def load_x(ti):
        xtt, Bxt = xts[ti % 2]
        C.load(SP, xtt[:], xv[:, :, ti * TL:(ti + 1) * TL], Bxt)

    def norm_tile(ti, par):
        hT, BhT = hTs[par]
        xtt, Bxt = xts[ti % 2]
        C.act([Bxt], [Bsq], sq[:], xtt[:], AF.Square)
        C.pe([Bsq, Bones], [Bpss], lambda: [nc.tensor.matmul(ps_ss[:], lhsT=ones[:], rhs=sq[:, c, :], start=(c == 0), stop=(c == DC - 1)) for c in range(DC)])
        rstd_from_psum(C, ps_ss[:], Bpss, rstd[:], Brstd, D, epsc[:], Beps)
        for c in range(DC):
            C.dve([Bxt, Bnw, Brstd], [BhT], lambda c=c: nc.vector.scalar_tensor_tensor(
                out=hT[:, c, :], in0=xtt[:, c, :], scalar=nw[:, c:c + 1], in1=rstd[:], op0=ALU.mult, op1=ALU.mult))
        C.store(POOL, hTd.ap()[ti], hT[:], BhT, BhTd[ti])

    pass2 = [False]

    def proj(gi, par):
        hT, BhT = hTs[par if pass2[0] else 0]
        p, Bp = pp[ppi[0] % 2]; ppi[0] += 1
        C.pe([BhT, Bw], [Bp], lambda: [nc.tensor.matmul(p[:], lhsT=w_bf[:, c, gi * 128:(gi + 1) * 128], rhs=hT[:, c, :], start=(c == 0), stop=(c == DC - 1)) for c in range(DC)])
        return p, Bp

    def vtok(ti):
        vt, Bvt = vts[ti % 3]
        pv_, Bpv_ = proj(3, ti % 2)
        C.act([Bpv_], [BvTs], vTs[:], pv_[:], AF.Copy)
        C.pe([BvTs, Bidb], [Bpt1], lambda: [nc.tensor.transpose(out=ps_t1[:, j, :], in_=vTs[:, j * 64:(j + 1) * 64], identity=idb[:]) for j in range(8)])
        C.act([Bpt1], [Bvt], vt[:], ps_t1[:], AF.Copy)

    def qgates(ti, zgrp, dirn):
        par = ti % 2
        qs, Bqs = qss[par]; g, Bg = gs[par]; kf, Bkf = kfs[par]
        c0 = dirn * 3
        pq, Bpq = proj(0, par)
        C.act([Bpq], [Bqs], qs[:], pq[:], AF.Silu)
        pz, Bpz = proj(zgrp, par)
        C.act([Bpz], [Bsig], sig[:], pz[:], AF.Sigmoid)
        C.act([Bsig, Blbp], [Bg], g[:], sig[:], AF.Ln, scale=lbp[:, c0 + 1:c0 + 2], bias=lbp[:, c0:c0 + 1])
        C.dve([Bsig, Blbp], [Bkf], lambda: nc.vector.tensor_scalar(out=kf[:], in0=sig[:], scalar1=lbp[:, c0 + 2:c0 + 3], scalar2=lbp[:, c0 + 1:c0 + 2], op0=ALU.mult, op1=ALU.add))

    def stage2(ti, fwd):
        par = ti % 2
        qs, Bqs = qss[par]; g, Bg = gs[par]; kf, Bkf = kfs[par]
        E, BE = Es[par]; qb, Bqb = qbs[par]; kb, Bkb = kbs[par]; kbt, Bkbt = kbts[par]
        scT, BscT = scTs[par]
        mcol = 0 if fwd else 64
        C.dve([Bg, Brm], [Bbb], lambda: nc.vector.tensor_tensor_scan(out=bb[:], data0=rm[:], data1=g[:], initial=0.0, op0=ALU.mult, op1=ALU.add))
        src, Bsrc = bb, Bbb
        if not fwd:
            C.dve([Bg, Bbb], [Bcc], lambda: nc.vector.tensor_tensor(out=cc[:], in0=g[:], in1=bb[:], op=ALU.subtract))
            C.dve([Bcc, Bbb], [Bcc], lambda: nc.vector.tensor_tensor(
                out=cc[:].rearrange("p (c t) -> p c t", t=64), in0=cc[:].rearrange("p (c t) -> p c t", t=64),
                in1=bb[:].rearrange("p (c t) -> p c t", t=64)[:, :, 63:64].to_broadcast([128, 8, 64]), op=ALU.add))
            src, Bsrc = cc, Bcc
        C.act([Bsrc], [BE], E[:], src[:], AF.Exp)
        C.act([Bsrc], [BEi], Ei[:], src[:], AF.Exp, scale=-1.0)
        C.dve([Bqs, BE], [Bqb], lambda: nc.vector.tensor_tensor(out=qb[:], in0=qs[:], in1=E[:], op=ALU.mult))
        C.dve([Bkf, BEi], [Bkb], lambda: nc.vector.tensor_tensor(out=kb[:], in0=kf[:], in1=Ei[:], op=ALU.mult))
        C.pe([Bkb, Bidb], [Bpt], lambda: [nc.tensor.transpose(out=ps_t[:, j, :], in_=kb[:, j * 64:(j + 1) * 64], identity=idb[:]) for j in range(8)])
        C.dve([Bpt], [Bkbt], lambda: nc.vector.tensor_copy(out=kbt[:], in_=ps_t[:]))
        C.pe([Bkb, Bqb], [Bpm], lambda: [nc.tensor.matmul(ps_m[:, j * 64:(j + 1) * 64], lhsT=kb[:, j * 64:(j + 1) * 64], rhs=qb[:, j * 64:(j + 1) * 64], start=True, stop=True) for j in range(8)])
        C.dve([Bpm, Bmk], [BscT], lambda: nc.vector.tensor_tensor(
            out=scT[:], in0=ps_m[:].rearrange("p (c t) -> p c t", t=64),
            in1=mk[:, mcol:mcol + 64].unsqueeze(1).to_broadcast([64, 8, 64]), op=ALU.mult))

    def chunks(ti, fwd):
        par = ti % 2
        E, BE = Es[par]; qb, Bqb = qbs[par]; kbt, Bkbt = kbts[par]; vt, Bvt = vts[ti % 3]; scT, BscT = scTs[par]
        dcol = 63 if fwd else 0
        Ev = E[:].rearrange("p (c t) -> p c t", t=64)
        for hb in range(2):
            C.pe([Bkbt, Bvt], [BpdS], lambda hb=hb: [nc.tensor.matmul(pdS[:, jj, :], lhsT=kbt[:, hb * 4 + jj, :], rhs=vt[:, hb * 4 + jj, :], start=True, stop=True) for jj in range(4)])
            C.dve([BpdS, BE], [BWd], lambda hb=hb: nc.vector.tensor_tensor(
                out=Wd[:, hb * 4:(hb + 1) * 4, :], in0=pdS[:],
                in1=Ev[:, hb * 4:(hb + 1) * 4, dcol:dcol + 1].to_broadcast([128, 4, 128]), op=ALU.mult))
        order = range(8) if fwd else range(7, -1, -1)
        for j in order:
            src, dst = (j, j + 1) if fwd else (j + 1, j)
            dc_ = j * 64 + dcol
            C.dve([BU, BE, BWd], [BU], lambda src=src, dst=dst, dc_=dc_, j=j: nc.vector.scalar_tensor_tensor(
                out=U[:, dst, :], in0=U[:, src, :], scalar=E[:, dc_:dc_ + 1], in1=Wd[:, j, :], op0=ALU.mult, op1=ALU.add))
        lo = 0 if fwd else 1
        C.dve([BU], [BUb], lambda lo=lo: nc.vector.tensor_copy(out=Ub[:], in_=U[:, lo:lo + 8, :]))
        cs_, cd_ = (8, 0) if fwd else (0, 8)
        C.dve([BU], [BU], lambda cs_=cs_, cd_=cd_: nc.vector.tensor_copy(out=U[:, cd_, :], in_=U[:, cs_, :]))
        C.pe([Bvt, BscT, BUb, Bqb], [Bpo], lambda: [m for j in range(8) for m in (
            nc.tensor.matmul(ps_o[:, j * 64:(j + 1) * 64], lhsT=vt[:, j, :], rhs=scT[:, j, :], start=True, stop=False),
            nc.tensor.matmul(ps_o[:, j * 64:(j + 1) * 64], lhsT=Ub[:, j, :], rhs=qb[:, j * 64:(j + 1) * 64], start=False, stop=True))])

    def merge3(A, B, Cc):
        n = max(len(A), len(B), len(Cc), 1)
        ia = ib = ic = 0
        for k in range(1, n + 1):
            while ia < len(A) and ia * n < k * len(A):
                A[ia](); ia += 1
            while ib < len(B) and ib * n < k * len(B):
                B[ib](); ib += 1
            while ic < len(Cc) and ic * n < k * len(Cc):
                Cc[ic](); ic += 1

    def s1_bwd(ti):
        if ti > 0:
            load_x(ti - 1)
        norm_tile(ti, 0)
        qgates(ti, 2, 1)
        vtok(ti)

    def s3_bwd(ti):
        chunks(ti, False)

    def s3_bfin(ti):
        C.act([Bpo], [Bobs], obs[:], ps_o[:], AF.Copy)
        C.store(SP, obwd.ap()[:, ti * TL:(ti + 1) * TL], obs[:], Bobs, Bobwd[ti])

    C.dve([], [BU], lambda: nc.vector.memset(U[:], 0.0))
    load_x(NTL - 1)
    seq = list(range(NTL - 1, -1, -1))
    for k in range(-2, NTL):
        A = C.captured(s3_bwd, seq[k]) if 0 <= k < NTL else []
        Bl = C.captured(stage2, seq[k + 1], False) if 0 <= k + 1 < NTL else []
        Cl = C.captured(s1_bwd, seq[k + 2]) if 0 <= k + 2 < NTL else []
        merge3(A, Bl, Cl)
        if 0 <= k < NTL:
            s3_bfin(seq[k])

    def conv_final(ti, left, right):
        u, Bu = ub[ti % 3]; gt, Bgt = gb[ti % 3]
        C.dve([Bu, Bcw], [Byc], lambda: nc.vector.tensor_scalar(out=yc[:], in0=u[:], scalar1=cw[:, 1:2], scalar2=None, op0=ALU.mult))
        C.dve([Bu, Bcw, Byc], [Byc], lambda: nc.vector.scalar_tensor_tensor(out=yc[:, 1:TL], in0=u[:, 0:TL - 1], scalar=cw[:, 0:1], in1=yc[:, 1:TL], op0=ALU.mult, op1=ALU.add))
        C.dve([Bu, Bcw, Byc], [Byc], lambda: nc.vector.scalar_tensor_tensor(out=yc[:, 0:TL - 1], in0=u[:, 1:TL], scalar=cw[:, 2:3], in1=yc[:, 0:TL - 1], op0=ALU.mult, op1=ALU.add))
        if left is not None:
            la, Bl = left
            C.dve([Bl, Bcw, Byc], [Byc], lambda: nc.vector.scalar_tensor_tensor(out=yc[:, 0:1], in0=la, scalar=cw[:, 0:1], in1=yc[:, 0:1], op0=ALU.mult, op1=ALU.add))
        if right is not None:
            ra, Br = right
            C.dve([Br, Bcw, Byc], [Byc], lambda: nc.vector.scalar_tensor_tensor(out=yc[:, TL - 1:TL], in0=ra, scalar=cw[:, 2:3], in1=yc[:, TL - 1:TL], op0=ALU.mult, op1=ALU.add))
        C.dve([Byc, Bgt], [Bycv], lambda: nc.vector.tensor_tensor(out=ycv[:], in0=yc[:], in1=gt[:], op=ALU.mult))
        C.store(SP, mixT.ap()[128:256, ti * TL:(ti + 1) * TL], ycv[:], Bycv, Bx_out[NTL + ti])

    def load_h(ti):
        hT, BhT = hTs[ti % 2]
        C.load(SP, hT[:], hTd.ap()[ti], BhT, [BhTd[ti]])

    def s1_fwd(ti):
        par = ti % 2
        ob, Bob = obl[ti % 3]; sg, Bsg = sgs[ti % 3]
        C.load(SP, ob[:], obwd.ap()[:, ti * TL:(ti + 1) * TL], Bob, [Bobwd[ti]])
        if ti + 1 < NTL:
            load_h(ti + 1)
        qgates(ti, 1, 0)
        vtok(ti)
        pg, Bpg = proj(4, par)
        C.act([Bpg], [Bsg], sg[:], pg[:], AF.Silu)
        u, Bu = ub[ti % 3]; gt, Bgt = gb[ti % 3]
        p5, Bp5 = proj(5, par)
        C.act([Bp5], [Bgt], gt[:], p5[:], AF.Copy)
        p7, Bp7 = proj(7, par)
        C.act([Bp7], [Bu], u[:], p7[:], AF.Copy)
        p6, Bp6 = proj(6, par)
        C.dve([Bp6, Bu], [Bu], lambda u=u, p6=p6: nc.vector.tensor_tensor(out=u[:], in0=p6[:], in1=u[:], op=ALU.mult))
        if ti > 0:
            left = None
            if ti > 1:
                upp, Bupp = ub[(ti - 2) % 3]
                left = (upp[:, TL - 1:TL], Bupp)
            conv_final(ti - 1, left, (u[:, 0:1], Bu))

    def s3_fwd(ti):
        chunks(ti, True)

    def s3_fin(ti):
        ob, Bob = obl[ti % 3]; sg, Bsg = sgs[ti % 3]
        C.dve([Bpo, Bob], [Bosum], lambda: nc.vector.tensor_tensor(out=osum[:], in0=ps_o[:], in1=ob[:], op=ALU.add))
        C.act([Bosum], [Bosq], osq[:], osum[:], AF.Square)
        C.pe([Bosq, Bones], [Bpss], lambda: nc.tensor.matmul(ps_ss[:], lhsT=ones[:], rhs=osq[:], start=True, stop=True))
        rstd_from_psum(C, ps_ss[:], Bpss, ro[:], Bro, 128, epsc[:], Beps)
        C.dve([Bosum, Bhw, Bro], [Byr], lambda: nc.vector.scalar_tensor_tensor(out=yr[:], in0=osum[:], scalar=hw[:, 0:1], in1=ro[:], op0=ALU.mult, op1=ALU.mult))
        C.dve([Byr, Bsg], [Byr], lambda: nc.vector.tensor_tensor(out=yr[:], in0=yr[:], in1=sg[:], op=ALU.mult))
        C.store(SP, mixT.ap()[0:128, ti * TL:(ti + 1) * TL], yr[:], Byr, Bx_out[ti])

    S.barrier()
    pass2[0] = True
    C.dve([], [BU], lambda: nc.vector.memset(U[:], 0.0))
    load_h(0)
    for k in range(-2, NTL):
        A = C.captured(s3_fwd, k) if 0 <= k < NTL else []
        Bl = C.captured(stage2, k + 1, True) if 0 <= k + 1 < NTL else []
        Cl = C.captured(s1_fwd, k + 2) if 0 <= k + 2 < NTL else []
        merge3(A, Bl, Cl)
        if 0 <= k < NTL:
            s3_fin(k)
    left = None
    if NTL > 1:
        upp, Bupp = ub[(NTL - 2) % 3]
        left = (upp[:, TL - 1:TL], Bupp)
    conv_final(NTL - 1, left, None)
    return C.finish(Bx_out)


_CACHE = {}


def _consts():
    m = np.zeros((64, 128), np.float32)
    m[:, 0:64] = np.triu(np.ones((64, 64), np.float32))
    m[:, 64:128] = np.tril(np.ones((64, 64), np.float32))
    rm = np.ones((128, 512), np.float32)
    rm[:, ::64] = 0.0
    return {"ident": np.eye(128, dtype=np.float32), "masks": m, "rmask": rm}


def _pc(v):
    return np.ascontiguousarray(v.reshape(-1, 128).T)


def run_mixer(xT, p, layer):
    ntok = xT.shape[1]
    key = ("M", ntok, layer)
    if key not in _CACHE:
        _CACHE[key] = build_mixer(ntok, layer)
    nc = _CACHE[key]
    w_in = p["w_in"][layer]
    cst = _consts()
    ins = []
    for h in range(NCORES):
        cols = np.concatenate([np.arange(g * 1024 + h * 128, g * 1024 + (h + 1) * 128) for g in range(8)])
        wa = w_in[:, cols]
        wa = np.ascontiguousarray(wa.reshape(DC, 128, 1024).transpose(1, 0, 2))
        hs = slice(h * 128, (h + 1) * 128)
        lb = np.stack([p["lb_fwd"][0, hs], p["lb_fwd"][1, hs], p["lb_bwd"][0, hs], p["lb_bwd"][1, hs]], axis=1)
        d = {"xT": xT, "wA": wa, "nw": _pc(p["attn_norm_w"][layer]), "lb": np.ascontiguousarray(lb, dtype=np.float32),
             "hw": np.ascontiguousarray(p["hgrn_norm_w"][layer, hs].reshape(128, 1)),
             "cw": np.ascontiguousarray(p["conv_w"][layer][:, hs].T)}
        d.update(cst)
        ins.append(d)
    res = run_bass_kernel_spmd(nc, ins, core_ids=list(range(NCORES)))
    mixT = np.empty((D, ntok), np.float32)
    for h in range(NCORES):
        o = res.results[h]["mixT"]
        mixT[h * 128:(h + 1) * 128] = o[0:128]
        mixT[1024 + h * 128:1024 + (h + 1) * 128] = o[128:256]
    return mixT


def build_ffn(nt):
    C = Ctx()
    nc, S = C.nc, C.S
    PE, ACT, DVE, POOL, SP = C.PE, C.ACT, C.DVE, C.POOL, C.SP
    TL = 512
    HT = 1024
    NH = nt // HT
    NU = FC // 2
    xT = C.din("xT", [D, nt])
    mT = C.din("mT", [D, nt])
    wo_d = C.din("wo", [4, 128, DC, 512])
    wgu_d = C.din("wgu", [NU, 128, DC, 512])
    wd_d = C.din("wd", [NU, 128, 2, D])
    nw_d = C.din("nw", [128, DC])
    fw_d = C.din("fw", [128, DC])
    oX = C.dout("oX", [D, nt])
    oN = C.dout("oN", [D, nt])
    xv = xT.ap().rearrange("(c p) t -> p c t", p=128)
    mv = mT.ap().rearrange("(c p) t -> p c t", p=128)
    oXv = oX.ap().rearrange("(c p) t -> p c t", p=128)
    oNv = oN.ap().rearrange("(c p) t -> p c t", p=128)

    nw, Bnw = C.sb([128, DC], F32, "nw"); C.load(SP, nw[:], nw_d.ap(), Bnw)
    fw, Bfw = C.sb([128, DC], F32, "fw"); C.load(SP, fw[:], fw_d.ap(), Bfw)
    ones, Bones = C.sb([128, 128], BF16, "ones")
    C.dve([], [Bones], lambda: nc.vector.memset(ones[:], 1.0))
    epsc, Beps = C.sb([128, 1], F32, "eps")
    C.dve([], [Beps], lambda: nc.vector.memset(epsc[:], EPS))

    acc, _ = C.sb([128, DC, HT], F32, "acc")
    Bacc = [[Buf("acc%d_%d" % (c, t)) for t in range(2)] for c in range(DC)]
    Baccl = [b for r in Bacc for b in r]
    mb, _ = C.sb([128, DC, HT], BF16, "mb")
    Bmb = [Buf("mb%d" % t) for t in range(2)]
    h2, Bh2 = C.sb([128, DC, HT], BF16, "h2")
    Bh2t = [Buf("h2_%d" % t) for t in range(2)]
    rstd, _ = C.sb([128, HT], F32, "rstd")
    Brs = [Buf("rs%d" % t) for t in range(2)]
    wg = [C.sb([128, DC, 512], BF16, "wg") for _ in range(2)]
    wd = [C.sb([128, 2, D], BF16, "wd") for _ in range(2)]
    aT = [C.sb([128, 2, HT], BF16, "aT") for _ in range(2)]
    sgt = [C.sb([128, TL], F32, "sg") for _ in range(2)]
    outn, Boutn = C.sb([128, 4, TL], F32, "outn")
    pss, Bpss = C.ps([128, TL], F32, "pss")
    pg = [C.ps([128, TL], F32, "pg") for _ in range(2)]
    pu = [C.ps([128, TL], F32, "pu") for _ in range(2)]
    pd = [C.ps([128, TL], F32, "pd") for _ in range(3)]
    Bouts = []
    cnt = {"g": 0, "d": 0, "s": 0}

    def norm_half(wvec, Bwv, dst, Bdst_t):
        for tt in range(2):
            ts = slice(tt * TL, (tt + 1) * TL)
            C.act([Bacc[c][tt] for c in range(DC)], [Bmb[tt]], mb[:, :, ts], acc[:, :, ts], AF.Square)
            C.pe([Bmb[tt], Bones], [Bpss], lambda ts=ts: [nc.tensor.matmul(pss[:], lhsT=ones[:], rhs=mb[:, c, ts], start=(c == 0), stop=(c == DC - 1)) for c in range(DC)])
            rstd_from_psum(C, pss[:], Bpss, rstd[:, ts], Brs[tt], D, epsc[:], Beps)

    for hf in range(NH):
        t0 = hf * HT
        for tt in range(2):
            ts = slice(tt * TL, (tt + 1) * TL)
            for c4 in range(4):
                cs = slice(c4 * 4, c4 * 4 + 4)
                bl = [Bacc[c][tt] for c in range(c4 * 4, c4 * 4 + 4)]
                owner = bl[0]
                S.dma(SP, [], bl, lambda cs=cs, ts=ts, tt=tt, t0=t0: nc.sync.dma_start(out=acc[:, cs, ts], in_=xv[:, cs, t0 + tt * TL:t0 + (tt + 1) * TL]), owner)
            S.dma(POOL, [], [Bmb[tt]], lambda ts=ts, tt=tt, t0=t0: nc.gpsimd.dma_start(out=mb[:, :, ts], in_=mv[:, :, t0 + tt * TL:t0 + (tt + 1) * TL]), Bmb[tt])
        for uo in range(4):
            w, Bw = wg[cnt["g"] % 2]; cnt["g"] += 1
            C.load(POOL, w[:], wo_d.ap()[uo], Bw)
            for dl in range(4):
                dc = uo * 4 + dl
                for tt in range(2):
                    ts = slice(tt * TL, (tt + 1) * TL)
                    p, Bp = pd[cnt["d"] % 3]; cnt["d"] += 1
                    C.pe([Bw, Bmb[tt]], [Bp], lambda w=w, p=p, dl=dl, ts=ts: [nc.tensor.matmul(p[:], lhsT=w[:, e, dl * 128:(dl + 1) * 128], rhs=mb[:, e, ts], start=(e == 0), stop=(e == DC - 1)) for e in range(DC)])
                    C.dve([Bp, Bacc[dc][tt]], [Bacc[dc][tt]], lambda p=p, dc=dc, ts=ts: nc.vector.tensor_tensor(out=acc[:, dc, ts], in0=p[:], in1=acc[:, dc, ts], op=ALU.add))
        norm_half(nw, Bnw, h2, Bh2t)
        for tt in range(2):
            ts = slice(tt * TL, (tt + 1) * TL)
            for c in range(DC):
                C.dve([Bacc[c][tt], Bnw, Brs[tt]], [Bh2t[tt]], lambda c=c, ts=ts: nc.vector.scalar_tensor_tensor(
                    out=h2[:, c, ts], in0=acc[:, c, ts], scalar=nw[:, c:c + 1], in1=rstd[:, ts], op0=ALU.mult, op1=ALU.mult))
        for u in range(NU):
            w, Bw = wg[cnt["g"] % 2]; cnt["g"] += 1
            w2, Bw2 = wd[u % 2]
            a, Ba = aT[u % 2]
            C.load(POOL, w[:], wgu_d.ap()[u], Bw)
            C.load(POOL, w2[:], wd_d.ap()[u], Bw2)
            for fl in range(2):
                for tt in range(2):
                    ts = slice(tt * TL, (tt + 1) * TL)
                    k = cnt["s"] % 2; cnt["s"] += 1
                    pgk, Bpg = pg[k]; puk, Bpu = pu[k]; sgk, Bsg = sgt[k]
                    C.pe([Bw, Bh2t[tt]], [Bpg], lambda w=w, pgk=pgk, fl=fl, ts=ts: [nc.tensor.matmul(pgk[:], lhsT=w[:, c, fl * 128:(fl + 1) * 128], rhs=h2[:, c, ts], start=(c == 0), stop=(c == DC - 1)) for c in range(DC)])
                    C.pe([Bw, Bh2t[tt]], [Bpu], lambda w=w, puk=puk, fl=fl, ts=ts: [nc.tensor.matmul(puk[:], lhsT=w[:, c, 256 + fl * 128:256 + (fl + 1) * 128], rhs=h2[:, c, ts], start=(c == 0), stop=(c == DC - 1)) for c in range(DC)])
                    C.act([Bpg], [Bsg], sgk[:], pgk[:], AF.Silu)
                    C.dve([Bsg, Bpu], [Ba], lambda a=a, sgk=sgk, puk=puk, fl=fl, ts=ts: nc.vector.tensor_tensor(out=a[:, fl, ts], in0=sgk[:], in1=puk[:], op=ALU.mult))
            for dc in range(DC):
                for tt in range(2):
                    ts = slice(tt * TL, (tt + 1) * TL)
                    p, Bp = pd[cnt["d"] % 3]; cnt["d"] += 1
                    C.pe([Bw2, Ba], [Bp], lambda w2=w2, a=a, p=p, dc=dc, ts=ts: [nc.tensor.matmul(p[:], lhsT=w2[:, fl, dc * 128:(dc + 1) * 128], rhs=a[:, fl, ts], start=(fl == 0), stop=(fl == 1)) for fl in range(2)])
                    C.dve([Bp, Bacc[dc][tt]], [Bacc[dc][tt]], lambda p=p, dc=dc, ts=ts: nc.vector.tensor_tensor(out=acc[:, dc, ts], in0=p[:], in1=acc[:, dc, ts], op=ALU.add))
        for tt in range(2):
            ts = slice(tt * TL, (tt + 1) * TL)
            for c4 in range(4):
                cs = slice(c4 * 4, c4 * 4 + 4)
                bl = [Bacc[c][tt] for c in range(c4 * 4, c4 * 4 + 4)]
                bo = Buf("ox"); Bouts.append(bo)
                S.dma(SP, bl, [bo], lambda cs=cs, ts=ts, tt=tt, t0=t0: nc.sync.dma_start(out=oXv[:, cs, t0 + tt * TL:t0 + (tt + 1) * TL], in_=acc[:, cs, ts]), bl[0])
        norm_half(fw, Bfw, None, None)
        for tt in range(2):
            ts = slice(tt * TL, (tt + 1) * TL)
            for c4 in range(4):
                for cl in range(4):
                    c = c4 * 4 + cl
                    C.dve([Bacc[c][tt], Bfw, Brs[tt]], [Boutn], lambda c=c, cl=cl, ts=ts: nc.vector.scalar_tensor_tensor(
                        out=outn[:, cl, :], in0=acc[:, c, ts], scalar=fw[:, c:c + 1], in1=rstd[:, ts], op0=ALU.mult, op1=ALU.mult))
                bo = Buf("on"); Bouts.append(bo)
                S.dma(SP, [Boutn], [bo], lambda tt=tt, c4=c4, t0=t0: nc.sync.dma_start(out=oNv[:, c4 * 4:c4 * 4 + 4, t0 + tt * TL:t0 + (tt + 1) * TL], in_=outn[:]), Boutn)
    return C.finish(Bouts)


def run_ffn(xT, mixT, p, layer):
    ntok = xT.shape[1]
    nt = ntok // NCORES
    key = ("F", nt)
    if key not in _CACHE:
        _CACHE[key] = build_ffn(nt)
    nc = _CACHE[key]
    w_out = p["w_out"][layer]; w_gu = p["w_gate_up"][layer]; w_dn = p["w_down"][layer]
    wo = np.ascontiguousarray(w_out.reshape(DC, 128, 4, 512).transpose(2, 1, 0, 3))
    NU = FC // 2
    gcols = w_gu[:, :DFF].reshape(D, NU, 256); ucols = w_gu[:, DFF:].reshape(D, NU, 256)
    wgu = np.concatenate([gcols, ucols], axis=2)
    wgu = np.ascontiguousarray(wgu.reshape(DC, 128, NU, 512).transpose(2, 1, 0, 3))
    wd = np.ascontiguousarray(w_dn.reshape(NU, 2, 128, D).transpose(0, 2, 1, 3))
    common = {"wo": wo, "wgu": wgu, "wd": wd, "nw": _pc(p["ffn_norm_w"][layer]), "fw": _pc(p["final_norm_w"])}
    ins = []
    for c in range(NCORES):
        d = {"xT": np.ascontiguousarray(xT[:, c * nt:(c + 1) * nt]), "mT": np.ascontiguousarray(mixT[:, c * nt:(c + 1) * nt])}
        d.update(common)
        ins.append(d)
    res = run_bass_kernel_spmd(nc, ins, core_ids=list(range(NCORES)))
    x2 = np.concatenate([res.results[c]["oX"] for c in range(NCORES)], axis=1)
    xn = np.concatenate([res.results[c]["oN"] for c in range(NCORES)], axis=1)
    return x2, xn


def kernel(x, attn_norm_w, w_in, lb_fwd, lb_bwd, hgrn_norm_w, conv_w, w_out, ffn_norm_w, w_gate_up, w_down, final_norm_w):
    p = {"attn_norm_w": np.asarray(attn_norm_w, np.float32), "w_in": np.asarray(w_in, np.float32),
         "lb_fwd": np.asarray(lb_fwd, np.float32), "lb_bwd": np.asarray(lb_bwd, np.float32),
         "hgrn_norm_w": np.asarray(hgrn_norm_w, np.float32), "conv_w": np.asarray(conv_w, np.float32),
         "w_out": np.asarray(w_out, np.float32), "ffn_norm_w": np.asarray(ffn_norm_w, np.float32),
         "w_gate_up": np.asarray(w_gate_up, np.float32), "w_down": np.asarray(w_down, np.float32),
         "final_norm_w": np.asarray(final_norm_w, np.float32)}
    x = np.asarray(x, np.float32)
    xT = np.ascontiguousarray(x[0].T)
    xn = None
    for layer in range(2):
        mixT = run_mixer(xT, p, layer)
        xT, xn = run_ffn(xT, mixT, p, layer)
    return np.ascontiguousarray(xn.T)[None].astype(np.float32)
```

```python
import numpy as np
import concourse.bass as bass
import concourse.mybir as mybir

F32 = mybir.dt.float32
BF16 = mybir.dt.bfloat16
AF = mybir.ActivationFunctionType
ALU = mybir.AluOpType
AX = mybir.AxisListType


class Buf:
    __slots__ = ("name", "w", "r", "dsem")

    def __init__(self, name):
        self.name = name
        self.w = {}
        self.r = {}
        self.dsem = None


class DSem:
    def __init__(self, sem):
        self.sem = sem
        self.count = 0


class Eng:
    def __init__(self, name, eng, sem):
        self.name = name
        self.eng = eng
        self.sem = sem
        self.count = 0
        self.seen = {}
        self.prog = []


class Sched:
    def __init__(self, nc, stack):
        self.nc = nc
        self.stack = stack
        self.engs = {}
        self.nsem = 0
        self.n_wait = 0
        self.dsems = []

    def barrier(self):
        evs = [(e.sem, e.count) for e in self.engs.values() if e.count > 0]
        evs += [(d.sem, d.count) for d in self.dsems if d.count > 0]
        for E in self.engs.values():
            for sem, val in evs:
                if sem is E.sem:
                    continue
                k = id(sem)
                if E.seen.get(k, 0) < val:
                    E.prog.append(("w", sem, val))
                    E.seen[k] = val

    def new_sem(self, name):
        self.nsem += 1
        return self.stack.enter_context(self.nc.semaphore(name))

    def add_engine(self, name, eng):
        e = Eng(name, eng, self.new_sem("s_" + name))
        self.engs[name] = e
        return e

    def _need(self, need, evs):
        for k, (sem, val) in evs.items():
            if k not in need or need[k][1] < val:
                need[k] = (sem, val)

    def _waits(self, E, reads, writes):
        need = {}
        for b in reads:
            self._need(need, b.w)
        for b in writes:
            self._need(need, b.w)
            self._need(need, b.r)
        for k, (sem, val) in need.items():
            if E.seen.get(k, 0) < val:
                E.prog.append(("w", sem, val))
                E.seen[k] = val
                self.n_wait += 1

    def _record(self, ev, reads, writes):
        k = id(ev[0])
        for b in writes:
            b.w = {k: ev}
            b.r = {}
        for b in reads:
            if b in writes:
                continue
            b.r[k] = ev

    def op(self, E, reads, writes, fn):
        self._waits(E, reads, writes)
        E.count += 1
        E.prog.append(("o", fn, E.sem, 1))
        ev = (E.sem, E.count)
        self._record(ev, reads, writes)
        return ev

    def dma(self, E, reads, writes, fn, owner, ndma=1):
        if owner.dsem is None:
            owner.dsem = DSem(self.new_sem("d_" + owner.name))
            self.dsems.append(owner.dsem)
        self._waits(E, reads, writes)
        E.prog.append(("d", fn, owner.dsem.sem, 16))
        owner.dsem.count += 16 * ndma
        ev = (owner.dsem.sem, owner.dsem.count)
        self._record(ev, reads, writes)
        return ev

    def wait_all(self, E, bufs):
        need = {}
        for b in bufs:
            self._need(need, b.w)
            self._need(need, b.r)
        for k, (sem, val) in need.items():
            if E.seen.get(k, 0) < val:
                E.prog.append(("w", sem, val))
                E.seen[k] = val

    def replay(self, E):
        for it in E.prog:
            if it[0] == "w":
                E.eng.wait_ge(it[1], it[2])
            elif it[0] == "o":
                ins = it[1]()
                if isinstance(ins, (list, tuple)):
                    ins = ins[-1]
                ins.then_inc(it[2], it[3])
            else:
                ins = it[1]()
                if not isinstance(ins, (list, tuple)):
                    ins = [ins]
                for i in ins:
                    i.then_inc(it[2], it[3])

from contextlib import ExitStack
from concourse.bass_utils import run_bass_kernel_spmd

NCORES = 8
D = 2048
DC = 16
DFF = 5632
FC = 44
EPS = 1e-6


_LAST = [None]


class Ctx:
    def __init__(self):
        _LAST[0] = self
        self.nc = nc = bass.Bass("TRN2", target_bir_lowering=False)
        self.st = ExitStack()
        self.S = S = Sched(nc, self.st)
        self.PE = S.add_engine("pe", nc.tensor)
        self.ACT = S.add_engine("act", nc.scalar)
        self.DVE = S.add_engine("dve", nc.vector)
        self.POOL = S.add_engine("pool", nc.gpsimd)
        self.SP = S.add_engine("sp", nc.sync)
        self.n = 0
        self.capture = None

    def _emit(self, f):
        if self.capture is not None:
            self.capture.append(f)
        else:
            f()

    def sb(self, shape, dt, name="t"):
        self.n += 1
        nm = "%s_%d" % (name, self.n)
        return self.st.enter_context(self.nc.sbuf_tensor(nm, shape, dt)), Buf(nm)

    def ps(self, shape, dt, name="p"):
        self.n += 1
        nm = "%s_%d" % (name, self.n)
        return self.st.enter_context(self.nc.psum_tensor(nm, shape, dt)), Buf(nm)

    def din(self, name, shape, dt=F32):
        return self.nc.dram_tensor(name, shape, dt, kind="ExternalInput")

    def dout(self, name, shape, dt=F32):
        return self.nc.dram_tensor(name, shape, dt, kind="ExternalOutput")

    def dscr(self, name, shape, dt=F32):
        return self.nc.dram_tensor(name, shape, dt)

    def act(self, reads, writes, out, in_, func, **kw):
        nc = self.nc
        self._emit(lambda: self.S.op(self.ACT, reads, writes, lambda: nc.scalar.activation(out=out, in_=in_, func=func, **kw)))

    def dve(self, reads, writes, fn):
        self._emit(lambda: self.S.op(self.DVE, reads, writes, fn))

    def pe(self, reads, writes, fn):
        self._emit(lambda: self.S.op(self.PE, reads, writes, fn))

    def load(self, E, out, in_, wbuf, rbufs=()):
        eng = E.eng
        self._emit(lambda: self.S.dma(E, list(rbufs), [wbuf], lambda: eng.dma_start(out=out, in_=in_), wbuf))

    def store(self, E, out, in_, rbuf, wbuf):
        eng = E.eng
        self._emit(lambda: self.S.dma(E, [rbuf], [wbuf], lambda: eng.dma_start(out=out, in_=in_), rbuf))

    def captured(self, fn, *args):
        self.capture = lst = []
        fn(*args)
        self.capture = None
        return lst

    @staticmethod
    def merge(A, B):
        ia = ib = 0
        while ia < len(A) or ib < len(B):
            if ib >= len(B) or (ia < len(A) and ia * len(B) <= ib * len(A)):
                A[ia](); ia += 1
            else:
                B[ib](); ib += 1

    def finish(self, out_bufs):
        S, nc = self.S, self.nc
        for E in (self.SP, self.POOL):
            S.wait_all(E, out_bufs)
        with nc.Block() as block:
            @block.tensor
            def _(e): S.replay(self.PE)
            @block.scalar
            def _(e): S.replay(self.ACT)
            @block.vector
            def _(e): S.replay(self.DVE)
            @block.gpsimd
            def _(e): S.replay(self.POOL)
            @block.sync
            def _(e): S.replay(self.SP)
        self.st.close()
        return nc


def rstd_from_psum(C, ps_ss, Bps, out, Bout, n, epsc, Beps):
    C.act([Bps, Beps], [Bout], out, ps_ss, AF.Ln, scale=1.0 / n, bias=epsc)
    C.act([Bout], [Bout], out, out, AF.Exp, scale=-0.5)


def build_mixer(ntok, layer):
    C = Ctx()
    nc, S = C.nc, C.S
    PE, ACT, DVE, POOL, SP = C.PE, C.ACT, C.DVE, C.POOL, C.SP
    TL = 512
    NTL = ntok // TL
    xT = C.din("xT", [D, ntok])
    wA = C.din("wA", [128, DC, 1024])
    nw_d = C.din("nw", [128, DC])
    lb_d = C.din("lb", [128, 4])
    hw_d = C.din("hw", [128, 1])
    cw_d = C.din("cw", [128, 3])
    ident_d = C.din("ident", [128, 128])
    mask_d = C.din("masks", [64, 128])
    rmask_d = C.din("rmask", [128, TL])
    mixT = C.dout("mixT", [256, ntok])
    obwd = C.dscr("obwd", [128, ntok])
    xv = xT.ap().rearrange("(c p) t -> p c t", p=128)

    w_bf, Bw = C.sb([128, DC, 1024], BF16, "w")
    C.load(POOL, w_bf[:], wA.ap(), Bw)
    nw, Bnw = C.sb([128, DC], F32, "nw"); C.load(SP, nw[:], nw_d.ap(), Bnw)
    lb, Blb = C.sb([128, 4], F32, "lb"); C.load(SP, lb[:], lb_d.ap(), Blb)
    hw, Bhw = C.sb([128, 1], F32, "hw"); C.load(SP, hw[:], hw_d.ap(), Bhw)
    cw, Bcw = C.sb([128, 3], F32, "cw"); C.load(SP, cw[:], cw_d.ap(), Bcw)
    idf, Bidf = C.sb([128, 128], F32, "idf"); C.load(SP, idf[:], ident_d.ap(), Bidf)
    mk, Bmk = C.sb([64, 128], F32, "mk"); C.load(SP, mk[:], mask_d.ap(), Bmk)
    rm, Brm = C.sb([128, TL], F32, "rm"); C.load(SP, rm[:], rmask_d.ap(), Brm)
    idb, Bidb = C.sb([128, 128], BF16, "idb")
    C.dve([Bidf], [Bidb], lambda: nc.vector.tensor_copy(out=idb[:], in_=idf[:]))
    ones, Bones = C.sb([128, 128], BF16, "ones")
    C.dve([], [Bones], lambda: nc.vector.memset(ones[:], 1.0))
    epsc, Beps = C.sb([128, 1], F32, "eps")
    C.dve([], [Beps], lambda: nc.vector.memset(epsc[:], EPS))
    lbp, Blbp = C.sb([128, 6], F32, "lbp")
    for d_ in range(2):
        c0 = d_ * 3
        if layer == 0:
            C.dve([], [Blbp], lambda c0=c0: nc.vector.memset(lbp[:, c0:c0 + 1], 0.0))
        else:
            C.dve([Blb], [Blbp], lambda c0=c0, d_=d_: nc.vector.tensor_tensor(
                out=lbp[:, c0:c0 + 1], in0=lb[:, 2 * d_ + 1:2 * d_ + 2], in1=lb[:, 2 * d_:2 * d_ + 1], op=ALU.subtract))
            C.act([Blbp], [Blbp], lbp[:, c0:c0 + 1], lbp[:, c0:c0 + 1], AF.Sigmoid)
        C.dve([Blbp], [Blbp], lambda c0=c0: nc.vector.tensor_scalar(
            out=lbp[:, c0 + 1:c0 + 2], in0=lbp[:, c0:c0 + 1], scalar1=-1.0, scalar2=1.0, op0=ALU.mult, op1=ALU.add))
        C.dve([Blbp], [Blbp], lambda c0=c0: nc.vector.tensor_scalar(
            out=lbp[:, c0 + 2:c0 + 3], in0=lbp[:, c0 + 1:c0 + 2], scalar1=-1.0, scalar2=None, op0=ALU.mult))

    hTd = C.dscr("hTd", [NTL, 128, DC, TL], BF16)
    xts = [C.sb([128, DC, TL], F32, "xt") for _ in range(2)]
    sq, Bsq = C.sb([128, DC, TL], BF16, "sq")
    hTs = [C.sb([128, DC, TL], BF16, "hT"), (sq, Bsq)]
    def t32(name): return C.sb([128, TL], F32, name)
    alias_k = [0]
    def a32(name):
        k = alias_k[0]; alias_k[0] += 1
        assert k < DC
        return xts[1][0][:, k, :], Buf("%s_a%d" % (name, k))
    class _V:
        def __init__(self, ap): self.ap_ = ap
        def __getitem__(self, key): return self.ap_[key] if key != slice(None) else self.ap_
    def a32t(name):
        ap, b = a32(name)
        return _V(ap), b
    rstd, Brstd = t32("rstd"); sig, Bsig = t32("sig")
    bb, Bbb = t32("bb"); cc, Bcc = t32("cc"); Ei, BEi = t32("Ei")
    osum, Bosum = a32t("osum"); ro, Bro = a32t("ro")
    yr, Byr = a32t("yr"); yc, Byc = a32t("yc"); ycv, Bycv = t32("ycv"); obs, Bobs = t32("obs")
    qss = [t32("qs") for _ in range(2)]; gs = [t32("g") for _ in range(2)]; kfs = [t32("kf") for _ in range(2)]
    Es = [t32("E") for _ in range(2)]; sgs = [a32t("sg") for _ in range(3)]; obl = [a32t("ob") for _ in range(3)]
    osq, Bosq = C.sb([128, TL], BF16, "osq")
    qbs = [C.sb([128, TL], BF16, "qb") for _ in range(2)]; kbs = [C.sb([128, TL], BF16, "kb") for _ in range(2)]
    kbts = [C.sb([64, 8, 128], BF16, "kbt") for _ in range(2)]; vts = [C.sb([64, 8, 128], BF16, "vt") for _ in range(3)]
    scTs = [C.sb([64, 8, 64], BF16, "scT") for _ in range(2)]
    U, BU = C.sb([128, 9, 128], F32, "U")
    Ub, BUb = C.sb([128, 8, 128], BF16, "Ub")
    Wd, BWd = C.sb([128, 8, 128], F32, "Wd")
    vTs, BvTs = C.sb([128, TL], BF16, "vTs")
    ub = [a32t("ub") for _ in range(3)]
    gb = [a32t("gb") for _ in range(3)]
    ps_ss, Bpss = C.ps([128, TL], F32, "pss")
    pp = [C.ps([128, TL], F32, "pp") for _ in range(2)]
    ps_o, Bpo = C.ps([128, TL], F32, "po")
    ps_m, Bpm = C.ps([64, TL], F32, "pm")
    ps_t, Bpt = C.ps([64, 8, 128], BF16, "pt")
    ps_t1, Bpt1 = C.ps([64, 8, 128], BF16, "pt1")
    pdS, BpdS = C.ps([128, 4, 128], F32, "pdS")
    ppi = [0]

    vtd = C.dscr("vtd", [NTL, 64, 8, 128], BF16)
    qsd = C.dscr("qsd", [NTL, 128, TL], F32)
    Bvtd = [Buf("vtd%d" % i) for i in range(NTL)]
    Bqsd = [Buf("qsd%d" % i) for i in range(NTL)]
    Bx_out = [Buf("mixo%d" % i) for i in range(2 * NTL + 2)]
    Bobwd = [Buf("obwd%d" % i) for i in range(NTL)]
    BhTd = [Buf("hTd%d" % i) for i in range(NTL)]

    def load_x(ti):
        xtt, Bxt = xts[ti % 2]
        C.load(SP, xtt[:], xv[:, :, ti * TL:(ti + 1) * TL], Bxt)

    def norm_tile(ti, par):
        hT, BhT = hTs[par]
        xtt, Bxt = xts[ti % 2]
        C.act([Bxt], [Bsq], sq[:], xtt[:], AF.Square)
        C.pe([Bsq, Bones], [Bpss], lambda: [nc.tensor.matmul(ps_ss[:], lhsT=ones[:], rhs=sq[:, c, :], start=(c == 0), stop=(c == DC - 1)) for c in range(DC)])
        rstd_from_psum(C, ps_ss[:], Bpss, rstd[:], Brstd, D, epsc[:], Beps)
        for c in range(DC):
            C.dve([Bxt, Bnw, Brstd], [BhT], lambda c=c: nc.vector.scalar_tensor_tensor(
                out=hT[:, c, :], in0=xtt[:, c, :], scalar=nw[:, c:c + 1], in1=rstd[:], op0=ALU.mult, op1=ALU.mult))
        C.store(POOL, hTd.ap()[ti], hT[:], BhT, BhTd[ti])

    pass2 = [False]

    def proj(gi, par):
        hT, BhT = hTs[par if pass2[0] else 0]
        p, Bp = pp[ppi[0] % 2]; ppi[0] += 1
        C.pe([BhT, Bw], [Bp], lambda: [nc.tensor.matmul(p[:], lhsT=w_bf[:, c, gi * 128:(gi + 1) * 128], rhs=hT[:, c, :], start=(c == 0), stop=(c == DC - 1)) for c in range(DC)])
        return p, Bp

    def vtok(ti):
        vt, Bvt = vts[ti % 3]
        pv_, Bpv_ = proj(3, ti % 2)
        C.act([Bpv_], [BvTs], vTs[:], pv_[:], AF.Copy)
        C.pe([BvTs, Bidb], [Bpt1], lambda: [nc.tensor.transpose(out=ps_t1[:, j, :], in_=vTs[:, j * 64:(j + 1) * 64], identity=idb[:]) for j in range(8)])
        C.act([Bpt1], [Bvt], vt[:], ps_t1[:], AF.Copy)

    def qgates(ti, zgrp, dirn):
        par = ti % 2
        qs, Bqs = qss[par]; g, Bg = gs[par]; kf, Bkf = kfs[par]
        c0 = dirn * 3
        if dirn == 1:
            pq, Bpq = proj(0, par)
            C.act([Bpq], [Bqs], qs[:], pq[:], AF.Silu)
            C.store(SP, qsd.ap()[ti], qs[:], Bqs, Bqsd[ti])
        else:
            C.load(SP, qs[:], qsd.ap()[ti], Bqs, [Bqsd[ti]])
        pz, Bpz = proj(zgrp, par)
        C.act([Bpz], [Bsig], sig[:], pz[:], AF.Sigmoid)
        C.act([Bsig, Blbp], [Bg], g[:], sig[:], AF.Ln, scale=lbp[:, c0 + 1:c0 + 2], bias=lbp[:, c0:c0 + 1])
        C.dve([Bsig, Blbp], [Bkf], lambda: nc.vector.tensor_scalar(out=kf[:], in0=sig[:], scalar1=lbp[:, c0 + 2:c0 + 3], scalar2=lbp[:, c0 + 1:c0 + 2], op0=ALU.mult, op1=ALU.add))

    def stage2(ti, fwd):
        par = ti % 2
        qs, Bqs = qss[par]; g, Bg = gs[par]; kf, Bkf = kfs[par]
        E, BE = Es[par]; qb, Bqb = qbs[par]; kb, Bkb = kbs[par]; kbt, Bkbt = kbts[par]
        scT, BscT = scTs[par]
        mcol = 0 if fwd else 64
        C.dve([Bg, Brm], [Bbb], lambda: nc.vector.tensor_tensor_scan(out=bb[:], data0=rm[:], data1=g[:], initial=0.0, op0=ALU.mult, op1=ALU.add))
        src, Bsrc = bb, Bbb
        if not fwd:
            C.dve([Bg, Bbb], [Bcc], lambda: nc.vector.tensor_tensor(out=cc[:], in0=g[:], in1=bb[:], op=ALU.subtract))
            C.dve([Bcc, Bbb], [Bcc], lambda: nc.vector.tensor_tensor(
                out=cc[:].rearrange("p (c t) -> p c t", t=64), in0=cc[:].rearrange("p (c t) -> p c t", t=64),
                in1=bb[:].rearrange("p (c t) -> p c t", t=64)[:, :, 63:64].to_broadcast([128, 8, 64]), op=ALU.add))
            src, Bsrc = cc, Bcc
        C.act([Bsrc], [BE], E[:], src[:], AF.Exp)
        C.act([Bsrc], [BEi], Ei[:], src[:], AF.Exp, scale=-1.0)
        C.dve([Bqs, BE], [Bqb], lambda: nc.vector.tensor_tensor(out=qb[:], in0=qs[:], in1=E[:], op=ALU.mult))
        C.dve([Bkf, BEi], [Bkb], lambda: nc.vector.tensor_tensor(out=kb[:], in0=kf[:], in1=Ei[:], op=ALU.mult))
        C.pe([Bkb, Bidb], [Bpt], lambda: [nc.tensor.transpose(out=ps_t[:, j, :], in_=kb[:, j * 64:(j + 1) * 64], identity=idb[:]) for j in range(8)])
        C.dve([Bpt], [Bkbt], lambda: nc.vector.tensor_copy(out=kbt[:], in_=ps_t[:]))
        C.pe([Bkb, Bqb], [Bpm], lambda: [nc.tensor.matmul(ps_m[:, j * 64:(j + 1) * 64], lhsT=kb[:, j * 64:(j + 1) * 64], rhs=qb[:, j * 64:(j + 1) * 64], start=True, stop=True) for j in range(8)])
        C.dve([Bpm, Bmk], [BscT], lambda: nc.vector.tensor_tensor(
            out=scT[:], in0=ps_m[:].rearrange("p (c t) -> p c t", t=64),
            in1=mk[:, mcol:mcol + 64].unsqueeze(1).to_broadcast([64, 8, 64]), op=ALU.mult))

    def chunks(ti, fwd):
        par = ti % 2
        E, BE = Es[par]; qb, Bqb = qbs[par]; kbt, Bkbt = kbts[par]; vt, Bvt = vts[ti % 3]; scT, BscT = scTs[par]
        dcol = 63 if fwd else 0
        Ev = E[:].rearrange("p (c t) -> p c t", t=64)
        for hb in range(2):
            C.pe([Bkbt, Bvt], [BpdS], lambda hb=hb: [nc.tensor.matmul(pdS[:, jj, :], lhsT=kbt[:, hb * 4 + jj, :], rhs=vt[:, hb * 4 + jj, :], start=True, stop=True) for jj in range(4)])
            C.dve([BpdS, BE], [BWd], lambda hb=hb: nc.vector.tensor_tensor(
                out=Wd[:, hb * 4:(hb + 1) * 4, :], in0=pdS[:],
                in1=Ev[:, hb * 4:(hb + 1) * 4, dcol:dcol + 1].to_broadcast([128, 4, 128]), op=ALU.mult))
        order = range(8) if fwd else range(7, -1, -1)
        for j in order:
            src, dst = (j, j + 1) if fwd else (j + 1, j)
            dc_ = j * 64 + dcol
            C.dve([BU, BE, BWd], [BU], lambda src=src, dst=dst, dc_=dc_, j=j: nc.vector.scalar_tensor_tensor(
                out=U[:, dst, :], in0=U[:, src, :], scalar=E[:, dc_:dc_ + 1], in1=Wd[:, j, :], op0=ALU.mult, op1=ALU.add))
        lo = 0 if fwd else 1
        C.dve([BU], [BUb], lambda lo=lo: nc.vector.tensor_copy(out=Ub[:], in_=U[:, lo:lo + 8, :]))
        cs_, cd_ = (8, 0) if fwd else (0, 8)
        C.dve([BU], [BU], lambda cs_=cs_, cd_=cd_: nc.vector.tensor_copy(out=U[:, cd_, :], in_=U[:, cs_, :]))
        C.pe([Bvt, BscT, BUb, Bqb], [Bpo], lambda: [m for j in range(8) for m in (
            nc.tensor.matmul(ps_o[:, j * 64:(j + 1) * 64], lhsT=vt[:, j, :], rhs=scT[:, j, :], start=True, stop=False),
            nc.tensor.matmul(ps_o[:, j * 64:(j + 1) * 64], lhsT=Ub[:, j, :], rhs=qb[:, j * 64:(j + 1) * 64], start=False, stop=True))])

    def merge3(A, B, Cc):
        n = max(len(A), len(B), len(Cc), 1)
        ia = ib = ic = 0
        for k in range(1, n + 1):
            while ia < len(A) and ia * n < k * len(A):
                A[ia](); ia += 1
            while ib < len(B) and ib * n < k * len(B):
                B[ib](); ib += 1
            while ic < len(Cc) and ic * n < k * len(Cc):
                Cc[ic](); ic += 1

    def s1_bwd(ti):
        if ti > 0:
            load_x(ti - 1)
        norm_tile(ti, 0)
        qgates(ti, 2, 1)
        vtok(ti)
        C.store(SP, vtd.ap()[ti], vts[ti % 3][0][:], vts[ti % 3][1], Bvtd[ti])

    def s3_bwd(ti):
        chunks(ti, False)

    def s3_bfin(ti):
        C.act([Bpo], [Bobs], obs[:], ps_o[:], AF.Copy)
        C.store(SP, obwd.ap()[:, ti * TL:(ti + 1) * TL], obs[:], Bobs, Bobwd[ti])

    C.dve([], [BU], lambda: nc.vector.memset(U[:], 0.0))
    load_x(NTL - 1)
    seq = list(range(NTL - 1, -1, -1))
    for k in range(-2, NTL):
        A = C.captured(s3_bwd, seq[k]) if 0 <= k < NTL else []
        Bl = C.captured(stage2, seq[k + 1], False) if 0 <= k + 1 < NTL else []
        Cl = C.captured(s1_bwd, seq[k + 2]) if 0 <= k + 2 < NTL else []
        merge3(A, Bl, Cl)
        if 0 <= k < NTL:
            s3_bfin(seq[k])

    def conv_final(ti, left, right):
        u, Bu = ub[ti % 3]; gt, Bgt = gb[ti % 3]
        C.dve([Bu, Bcw], [Byc], lambda: nc.vector.tensor_scalar(out=yc[:], in0=u[:], scalar1=cw[:, 1:2], scalar2=None, op0=ALU.mult))
        C.dve([Bu, Bcw, Byc], [Byc], lambda: nc.vector.scalar_tensor_tensor(out=yc[:, 1:TL], in0=u[:, 0:TL - 1], scalar=cw[:, 0:1], in1=yc[:, 1:TL], op0=ALU.mult, op1=ALU.add))
        C.dve([Bu, Bcw, Byc], [Byc], lambda: nc.vector.scalar_tensor_tensor(out=yc[:, 0:TL - 1], in0=u[:, 1:TL], scalar=cw[:, 2:3], in1=yc[:, 0:TL - 1], op0=ALU.mult, op1=ALU.add))
        if left is not None:
            la, Bl = left
            C.dve([Bl, Bcw, Byc], [Byc], lambda: nc.vector.scalar_tensor_tensor(out=yc[:, 0:1], in0=la, scalar=cw[:, 0:1], in1=yc[:, 0:1], op0=ALU.mult, op1=ALU.add))
        if right is not None:
            ra, Br = right
            C.dve([Br, Bcw, Byc], [Byc], lambda: nc.vector.scalar_tensor_tensor(out=yc[:, TL - 1:TL], in0=ra, scalar=cw[:, 2:3], in1=yc[:, TL - 1:TL], op0=ALU.mult, op1=ALU.add))
        C.dve([Byc, Bgt], [Bycv], lambda: nc.vector.tensor_tensor(out=ycv[:], in0=yc[:], in1=gt[:], op=ALU.mult))
        C.store(SP, mixT.ap()[128:256, ti * TL:(ti + 1) * TL], ycv[:], Bycv, Bx_out[NTL + ti])

    def load_h(ti):
        hT, BhT = hTs[ti % 2]
        C.load(SP, hT[:], hTd.ap()[ti], BhT, [BhTd[ti]])

    def s1_fwd(ti):
        par = ti % 2
        ob, Bob = obl[ti % 3]; sg, Bsg = sgs[ti % 3]
        C.load(SP, ob[:], obwd.ap()[:, ti * TL:(ti + 1) * TL], Bob, [Bobwd[ti]])
        if ti + 1 < NTL:
            load_h(ti + 1)
        qgates(ti, 1, 0)
        C.load(SP, vts[ti % 3][0][:], vtd.ap()[ti], vts[ti % 3][1], [Bvtd[ti]])
        pg, Bpg = proj(4, par)
        C.act([Bpg], [Bsg], sg[:], pg[:], AF.Silu)
        u, Bu = ub[ti % 3]; gt, Bgt = gb[ti % 3]
        p5, Bp5 = proj(5, par)
        C.act([Bp5], [Bgt], gt[:], p5[:], AF.Copy)
        p7, Bp7 = proj(7, par)
        C.act([Bp7], [Bu], u[:], p7[:], AF.Copy)
        p6, Bp6 = proj(6, par)
        C.dve([Bp6, Bu], [Bu], lambda u=u, p6=p6: nc.vector.tensor_tensor(out=u[:], in0=p6[:], in1=u[:], op=ALU.mult))
        if ti > 0:
            left = None
            if ti > 1:
                upp, Bupp = ub[(ti - 2) % 3]
                left = (upp[:, TL - 1:TL], Bupp)
            conv_final(ti - 1, left, (u[:, 0:1], Bu))

    def s3_fwd(ti):
        chunks(ti, True)

    def s3_fin(ti):
        ob, Bob = obl[ti % 3]; sg, Bsg = sgs[ti % 3]
        C.dve([Bpo, Bob], [Bosum], lambda: nc.vector.tensor_tensor(out=osum[:], in0=ps_o[:], in1=ob[:], op=ALU.add))
        C.act([Bosum], [Bosq], osq[:], osum[:], AF.Square)
        C.pe([Bosq, Bones], [Bpss], lambda: nc.tensor.matmul(ps_ss[:], lhsT=ones[:], rhs=osq[:], start=True, stop=True))
        rstd_from_psum(C, ps_ss[:], Bpss, ro[:], Bro, 128, epsc[:], Beps)
        C.dve([Bosum, Bhw, Bro], [Byr], lambda: nc.vector.scalar_tensor_tensor(out=yr[:], in0=osum[:], scalar=hw[:, 0:1], in1=ro[:], op0=ALU.mult, op1=ALU.mult))
        C.dve([Byr, Bsg], [Byr], lambda: nc.vector.tensor_tensor(out=yr[:], in0=yr[:], in1=sg[:], op=ALU.mult))
        C.store(SP, mixT.ap()[0:128, ti * TL:(ti + 1) * TL], yr[:], Byr, Bx_out[ti])

    S.barrier()
    pass2[0] = True
    C.dve([], [BU], lambda: nc.vector.memset(U[:], 0.0))
    load_h(0)
    for k in range(-2, NTL):
        A = C.captured(s3_fwd, k) if 0 <= k < NTL else []
        Bl = C.captured(stage2, k + 1, True) if 0 <= k + 1 < NTL else []
        Cl = C.captured(s1_fwd, k + 2) if 0 <= k + 2 < NTL else []
        merge3(A, Bl, Cl)
        if 0 <= k < NTL:
            s3_fin(k)
    left = None
    if NTL > 1:
        upp, Bupp = ub[(NTL - 2) % 3]
        left = (upp[:, TL - 1:TL], Bupp)
    conv_final(NTL - 1, left, None)
    return C.finish(Bx_out)


_CACHE = {}


def _consts():
    m = np.zeros((64, 128), np.float32)
    m[:, 0:64] = np.triu(np.ones((64, 64), np.float32))
    m[:, 64:128] = np.tril(np.ones((64, 64), np.float32))
    rm = np.ones((128, 512), np.float32)
    rm[:, ::64] = 0.0
    return {"ident": np.eye(128, dtype=np.float32), "masks": m, "rmask": rm}


def _pc(v):
    return np.ascontiguousarray(v.reshape(-1, 128).T)


def run_mixer(xT, p, layer):
    ntok = xT.shape[1]
    key = ("M", ntok, layer)
    if key not in _CACHE:
        _CACHE[key] = build_mixer(ntok, layer)
    nc = _CACHE[key]
    w_in = p["w_in"][layer]
    cst = _consts()
    ins = []
    for h in range(NCORES):
        cols = np.concatenate([np.arange(g * 1024 + h * 128, g * 1024 + (h + 1) * 128) for g in range(8)])
        wa = w_in[:, cols]
        wa = np.ascontiguousarray(wa.reshape(DC, 128, 1024).transpose(1, 0, 2))
        hs = slice(h * 128, (h + 1) * 128)
        lb = np.stack([p["lb_fwd"][0, hs], p["lb_fwd"][1, hs], p["lb_bwd"][0, hs], p["lb_bwd"][1, hs]], axis=1)
        d = {"xT": xT, "wA": wa, "nw": _pc(p["attn_norm_w"][layer]), "lb": np.ascontiguousarray(lb, dtype=np.float32),
             "hw": np.ascontiguousarray(p["hgrn_norm_w"][layer, hs].reshape(128, 1)),
             "cw": np.ascontiguousarray(p["conv_w"][layer][:, hs].T)}
        d.update(cst)
        ins.append(d)
    res = run_bass_kernel_spmd(nc, ins, core_ids=list(range(NCORES)))
    mixT = np.empty((D, ntok), np.float32)
    for h in range(NCORES):
        o = res.results[h]["mixT"]
        mixT[h * 128:(h + 1) * 128] = o[0:128]
        mixT[1024 + h * 128:1024 + (h + 1) * 128] = o[128:256]
    return mixT


def build_ffn(nt):
    C = Ctx()
    nc, S = C.nc, C.S
    PE, ACT, DVE, POOL, SP = C.PE, C.ACT, C.DVE, C.POOL, C.SP
    TL = 512
    HT = 1024
    NH = nt // HT
    NU = FC // 2
    xT = C.din("xT", [D, nt])
    mT = C.din("mT", [D, nt])
    wo_d = C.din("wo", [4, 128, DC, 512])
    wgu_d = C.din("wgu", [NU, 128, DC, 512])
    wd_d = C.din("wd", [NU, 128, 2, D])
    nw_d = C.din("nw", [128, DC])
    fw_d = C.din("fw", [128, DC])
    oX = C.dout("oX", [D, nt])
    oN = C.dout("oN", [D, nt])
    xv = xT.ap().rearrange("(c p) t -> p c t", p=128)
    mv = mT.ap().rearrange("(c p) t -> p c t", p=128)
    oXv = oX.ap().rearrange("(c p) t -> p c t", p=128)
    oNv = oN.ap().rearrange("(c p) t -> p c t", p=128)

    nw, Bnw = C.sb([128, DC], F32, "nw"); C.load(SP, nw[:], nw_d.ap(), Bnw)
    fw, Bfw = C.sb([128, DC], F32, "fw"); C.load(SP, fw[:], fw_d.ap(), Bfw)
    ones, Bones = C.sb([128, 128], BF16, "ones")
    C.dve([], [Bones], lambda: nc.vector.memset(ones[:], 1.0))
    epsc, Beps = C.sb([128, 1], F32, "eps")
    C.dve([], [Beps], lambda: nc.vector.memset(epsc[:], EPS))

    acc, _ = C.sb([128, DC, HT], F32, "acc")
    Bacc = [[Buf("acc%d_%d" % (c, t)) for t in range(2)] for c in range(DC)]
    Baccl = [b for r in Bacc for b in r]
    mb, _ = C.sb([128, DC, HT], BF16, "mb")
    Bmb = [Buf("mb%d" % t) for t in range(2)]
    h2, Bh2 = C.sb([128, DC, HT], BF16, "h2")
    Bh2t = [Buf("h2_%d" % t) for t in range(2)]
    rstd, _ = C.sb([128, HT], F32, "rstd")
    Brs = [Buf("rs%d" % t) for t in range(2)]
    wg = [C.sb([128, DC, 512], BF16, "wg") for _ in range(2)]
    wd = [C.sb([128, 2, D], BF16, "wd") for _ in range(2)]
    aT = [C.sb([128, 2, HT], BF16, "aT") for _ in range(2)]
    sgt = [C.sb([128, TL], F32, "sg") for _ in range(2)]
    outn, Boutn = C.sb([128, 4, TL], F32, "outn")
    pss, Bpss = C.ps([128, TL], F32, "pss")
    pg = [C.ps([128, TL], F32, "pg") for _ in range(2)]
    pu = [C.ps([128, TL], F32, "pu") for _ in range(2)]
    pd = [C.ps([128, TL], F32, "pd") for _ in range(3)]
    Bouts = []
    cnt = {"g": 0, "d": 0, "s": 0}

    def norm_half(wvec, Bwv, dst, Bdst_t):
        for tt in range(2):
            ts = slice(tt * TL, (tt + 1) * TL)
            C.act([Bacc[c][tt] for c in range(DC)], [Bmb[tt]], mb[:, :, ts], acc[:, :, ts], AF.Square)
            C.pe([Bmb[tt], Bones], [Bpss], lambda ts=ts: [nc.tensor.matmul(pss[:], lhsT=ones[:], rhs=mb[:, c, ts], start=(c == 0), stop=(c == DC - 1)) for c in range(DC)])
            rstd_from_psum(C, pss[:], Bpss, rstd[:, ts], Brs[tt], D, epsc[:], Beps)

    for hf in range(NH):
        t0 = hf * HT
        for tt in range(2):
            ts = slice(tt * TL, (tt + 1) * TL)
            for c4 in range(4):
                cs = slice(c4 * 4, c4 * 4 + 4)
                bl = [Bacc[c][tt] for c in range(c4 * 4, c4 * 4 + 4)]
                owner = bl[0]
                S.dma(SP, [], bl, lambda cs=cs, ts=ts, tt=tt, t0=t0: nc.sync.dma_start(out=acc[:, cs, ts], in_=xv[:, cs, t0 + tt * TL:t0 + (tt + 1) * TL]), owner)
            S.dma(POOL, [], [Bmb[tt]], lambda ts=ts, tt=tt, t0=t0: nc.gpsimd.dma_start(out=mb[:, :, ts], in_=mv[:, :, t0 + tt * TL:t0 + (tt + 1) * TL]), Bmb[tt])
        for uo in range(4):
            w, Bw = wg[cnt["g"] % 2]; cnt["g"] += 1
            C.load(POOL, w[:], wo_d.ap()[uo], Bw)
            for dl in range(4):
                dc = uo * 4 + dl
                for tt in range(2):
                    ts = slice(tt * TL, (tt + 1) * TL)
                    p, Bp = pd[cnt["d"] % 3]; cnt["d"] += 1
                    C.pe([Bw, Bmb[tt]], [Bp], lambda w=w, p=p, dl=dl, ts=ts: [nc.tensor.matmul(p[:], lhsT=w[:, e, dl * 128:(dl + 1) * 128], rhs=mb[:, e, ts], start=(e == 0), stop=(e == DC - 1)) for e in range(DC)])
                    C.dve([Bp, Bacc[dc][tt]], [Bacc[dc][tt]], lambda p=p, dc=dc, ts=ts: nc.vector.tensor_tensor(out=acc[:, dc, ts], in0=p[:], in1=acc[:, dc, ts], op=ALU.add))
        norm_half(nw, Bnw, h2, Bh2t)
        for tt in range(2):
            ts = slice(tt * TL, (tt + 1) * TL)
            for c in range(DC):
                C.dve([Bacc[c][tt], Bnw, Brs[tt]], [Bh2t[tt]], lambda c=c, ts=ts: nc.vector.scalar_tensor_tensor(
                    out=h2[:, c, ts], in0=acc[:, c, ts], scalar=nw[:, c:c + 1], in1=rstd[:, ts], op0=ALU.mult, op1=ALU.mult))
        for u in range(NU):
            w, Bw = wg[cnt["g"] % 2]; cnt["g"] += 1
            w2, Bw2 = wd[u % 2]
            a, Ba = aT[u % 2]
            C.load(POOL, w[:], wgu_d.ap()[u], Bw)
            C.load(POOL, w2[:], wd_d.ap()[u], Bw2)
            for fl in range(2):
                for tt in range(2):
                    ts = slice(tt * TL, (tt + 1) * TL)
                    k = cnt["s"] % 2; cnt["s"] += 1
                    pgk, Bpg = pg[k]; puk, Bpu = pu[k]; sgk, Bsg = sgt[k]
                    C.pe([Bw, Bh2t[tt]], [Bpg], lambda w=w, pgk=pgk, fl=fl, ts=ts: [nc.tensor.matmul(pgk[:], lhsT=w[:, c, fl * 128:(fl + 1) * 128], rhs=h2[:, c, ts], start=(c == 0), stop=(c == DC - 1)) for c in range(DC)])
                    C.pe([Bw, Bh2t[tt]], [Bpu], lambda w=w, puk=puk, fl=fl, ts=ts: [nc.tensor.matmul(puk[:], lhsT=w[:, c, 256 + fl * 128:256 + (fl + 1) * 128], rhs=h2[:, c, ts], start=(c == 0), stop=(c == DC - 1)) for c in range(DC)])
                    C.act([Bpg], [Bsg], sgk[:], pgk[:], AF.Silu)
                    C.dve([Bsg, Bpu], [Ba], lambda a=a, sgk=sgk, puk=puk, fl=fl, ts=ts: nc.vector.tensor_tensor(out=a[:, fl, ts], in0=sgk[:], in1=puk[:], op=ALU.mult))
            for dc in range(DC):
                for tt in range(2):
                    ts = slice(tt * TL, (tt + 1) * TL)
                    p, Bp = pd[cnt["d"] % 3]; cnt["d"] += 1
                    C.pe([Bw2, Ba], [Bp], lambda w2=w2, a=a, p=p, dc=dc, ts=ts: [nc.tensor.matmul(p[:], lhsT=w2[:, fl, dc * 128:(dc + 1) * 128], rhs=a[:, fl, ts], start=(fl == 0), stop=(fl == 1)) for fl in range(2)])
                    C.dve([Bp, Bacc[dc][tt]], [Bacc[dc][tt]], lambda p=p, dc=dc, ts=ts: nc.vector.tensor_tensor(out=acc[:, dc, ts], in0=p[:], in1=acc[:, dc, ts], op=ALU.add))
        for tt in range(2):
            ts = slice(tt * TL, (tt + 1) * TL)
            for c4 in range(4):
                cs = slice(c4 * 4, c4 * 4 + 4)
                bl = [Bacc[c][tt] for c in range(c4 * 4, c4 * 4 + 4)]
                bo = Buf("ox"); Bouts.append(bo)
                S.dma(SP, bl, [bo], lambda cs=cs, ts=ts, tt=tt, t0=t0: nc.sync.dma_start(out=oXv[:, cs, t0 + tt * TL:t0 + (tt + 1) * TL], in_=acc[:, cs, ts]), bl[0])
        norm_half(fw, Bfw, None, None)
        for tt in range(2):
            ts = slice(tt * TL, (tt + 1) * TL)
            for c4 in range(4):
                for cl in range(4):
                    c = c4 * 4 + cl
                    C.dve([Bacc[c][tt], Bfw, Brs[tt]], [Boutn], lambda c=c, cl=cl, ts=ts: nc.vector.scalar_tensor_tensor(
                        out=outn[:, cl, :], in0=acc[:, c, ts], scalar=fw[:, c:c + 1], in1=rstd[:, ts], op0=ALU.mult, op1=ALU.mult))
                bo = Buf("on"); Bouts.append(bo)
                S.dma(SP, [Boutn], [bo], lambda tt=tt, c4=c4, t0=t0: nc.sync.dma_start(out=oNv[:, c4 * 4:c4 * 4 + 4, t0 + tt * TL:t0 + (tt + 1) * TL], in_=outn[:]), Boutn)
    return C.finish(Bouts)


def run_ffn(xT, mixT, p, layer):
    ntok = xT.shape[1]
    nt = ntok // NCORES
    key = ("F", nt)
    if key not in _CACHE:
        _CACHE[key] = build_ffn(nt)
    nc = _CACHE[key]
    w_out = p["w_out"][layer]; w_gu = p["w_gate_up"][layer]; w_dn = p["w_down"][layer]
    wo = np.ascontiguousarray(w_out.reshape(DC, 128, 4, 512).transpose(2, 1, 0, 3))
    NU = FC // 2
    gcols = w_gu[:, :DFF].reshape(D, NU, 256); ucols = w_gu[:, DFF:].reshape(D, NU, 256)
    wgu = np.concatenate([gcols, ucols], axis=2)
    wgu = np.ascontiguousarray(wgu.reshape(DC, 128, NU, 512).transpose(2, 1, 0, 3))
    wd = np.ascontiguousarray(w_dn.reshape(NU, 2, 128, D).transpose(0, 2, 1, 3))
    common = {"wo": wo, "wgu": wgu, "wd": wd, "nw": _pc(p["ffn_norm_w"][layer]), "fw": _pc(p["final_norm_w"])}
    ins = []
    for c in range(NCORES):
        d = {"xT": np.ascontiguousarray(xT[:, c * nt:(c + 1) * nt]), "mT": np.ascontiguousarray(mixT[:, c * nt:(c + 1) * nt])}
        d.update(common)
        ins.append(d)
    res = run_bass_kernel_spmd(nc, ins, core_ids=list(range(NCORES)))
    x2 = np.concatenate([res.results[c]["oX"] for c in range(NCORES)], axis=1)
    xn = np.concatenate([res.results[c]["oN"] for c in range(NCORES)], axis=1)
    return x2, xn


def kernel(x, attn_norm_w, w_in, lb_fwd, lb_bwd, hgrn_norm_w, conv_w, w_out, ffn_norm_w, w_gate_up, w_down, final_norm_w):
    p = {"attn_norm_w": np.asarray(attn_norm_w, np.float32), "w_in": np.asarray(w_in, np.float32),
         "lb_fwd": np.asarray(lb_fwd, np.float32), "lb_bwd": np.asarray(lb_bwd, np.float32),
         "hgrn_norm_w": np.asarray(hgrn_norm_w, np.float32), "conv_w": np.asarray(conv_w, np.float32),
         "w_out": np.asarray(w_out, np.float32), "ffn_norm_w": np.asarray(ffn_norm_w, np.float32),
         "w_gate_up": np.asarray(w_gate_up, np.float32), "w_down": np.asarray(w_down, np.float32),
         "final_norm_w": np.asarray(final_norm_w, np.float32)}
    x = np.asarray(x, np.float32)
    xT = np.ascontiguousarray(x[0].T)
    xn = None
    for layer in range(2):
        mixT = run_mixer(xT, p, layer)
        xT, xn = run_ffn(xT, mixT, p, layer)
    return np.ascontiguousarray(xn.T)[None].astype(np.float32)
```

```python
import numpy as np
import concourse.bass as bass
import concourse.mybir as mybir

F32 = mybir.dt.float32
BF16 = mybir.dt.bfloat16
AF = mybir.ActivationFunctionType
ALU = mybir.AluOpType
AX = mybir.AxisListType


class Buf:
    __slots__ = ("name", "w", "r", "dsem")

    def __init__(self, name):
        self.name = name
        self.w = {}
        self.r = {}
        self.dsem = None


class DSem:
    def __init__(self, sem):
        self.sem = sem
        self.count = 0


class Eng:
    def __init__(self, name, eng, sem):
        self.name = name
        self.eng = eng
        self.sem = sem
        self.count = 0
        self.seen = {}
        self.prog = []


class Sched:
    def __init__(self, nc, stack):
        self.nc = nc
        self.stack = stack
        self.engs = {}
        self.nsem = 0
        self.n_wait = 0
        self.dsems = []

    def barrier(self):
        evs = [(e.sem, e.count) for e in self.engs.values() if e.count > 0]
        evs += [(d.sem, d.count) for d in self.dsems if d.count > 0]
        for E in self.engs.values():
            for sem, val in evs:
                if sem is E.sem:
                    continue
                k = id(sem)
                if E.seen.get(k, 0) < val:
                    E.prog.append(("w", sem, val))
                    E.seen[k] = val

    def new_sem(self, name):
        self.nsem += 1
        return self.stack.enter_context(self.nc.semaphore(name))

    def add_engine(self, name, eng):
        e = Eng(name, eng, self.new_sem("s_" + name))
        self.engs[name] = e
        return e

    def _need(self, need, evs):
        for k, (sem, val) in evs.items():
            if k not in need or need[k][1] < val:
                need[k] = (sem, val)

    def _waits(self, E, reads, writes):
        need = {}
        for b in reads:
            self._need(need, b.w)
        for b in writes:
            self._need(need, b.w)
            self._need(need, b.r)
        for k, (sem, val) in need.items():
            if E.seen.get(k, 0) < val:
                E.prog.append(("w", sem, val))
                E.seen[k] = val
                self.n_wait += 1

    def _record(self, ev, reads, writes):
        k = id(ev[0])
        for b in writes:
            b.w = {k: ev}
            b.r = {}
        for b in reads:
            if b in writes:
                continue
            b.r[k] = ev

    def op(self, E, reads, writes, fn):
        self._waits(E, reads, writes)
        E.count += 1
        E.prog.append(("o", fn, E.sem, 1))
        ev = (E.sem, E.count)
        self._record(ev, reads, writes)
        return ev

    def dma(self, E, reads, writes, fn, owner, ndma=1):
        if owner.dsem is None:
            owner.dsem = DSem(self.new_sem("d_" + owner.name))
            self.dsems.append(owner.dsem)
        self._waits(E, reads, writes)
        E.prog.append(("d", fn, owner.dsem.sem, 16))
        owner.dsem.count += 16 * ndma
        ev = (owner.dsem.sem, owner.dsem.count)
        self._record(ev, reads, writes)
        return ev

    def wait_all(self, E, bufs):
        need = {}
        for b in bufs:
            self._need(need, b.w)
            self._need(need, b.r)
        for k, (sem, val) in need.items():
            if E.seen.get(k, 0) < val:
                E.prog.append(("w", sem, val))
                E.seen[k] = val

    def replay(self, E):
        for it in E.prog:
            if it[0] == "w":
                E.eng.wait_ge(it[1], it[2])
            elif it[0] == "o":
                ins = it[1]()
                if isinstance(ins, (list, tuple)):
                    ins = ins[-1]
                ins.then_inc(it[2], it[3])
            else:
                ins = it[1]()
                if not isinstance(ins, (list, tuple)):
                    ins = [ins]
                for i in ins:
                    i.then_inc(it[2], it[3])

from contextlib import ExitStack
from concourse.bass_utils import run_bass_kernel_spmd

NCORES = 8
D = 2048
DC = 16
DFF = 5632
FC = 44
EPS = 1e-6


_LAST = [None]


class Ctx:
    def __init__(self):
        _LAST[0] = self
        self.nc = nc = bass.Bass("TRN2", target_bir_lowering=False)
        self.st = ExitStack()
        self.S = S = Sched(nc, self.st)
        self.PE = S.add_engine("pe", nc.tensor)
        self.ACT = S.add_engine("act", nc.scalar)
        self.DVE = S.add_engine("dve", nc.vector)
        self.POOL = S.add_engine("pool", nc.gpsimd)
        self.SP = S.add_engine("sp", nc.sync)
        self.n = 0
        self.capture = None

    def _emit(self, f):
        if self.capture is not None:
            self.capture.append(f)
        else:
            f()

    def sb(self, shape, dt, name="t"):
        self.n += 1
        nm = "%s_%d" % (name, self.n)
        return self.st.enter_context(self.nc.sbuf_tensor(nm, shape, dt)), Buf(nm)

    def ps(self, shape, dt, name="p"):
        self.n += 1
        nm = "%s_%d" % (name, self.n)
        return self.st.enter_context(self.nc.psum_tensor(nm, shape, dt)), Buf(nm)

    def din(self, name, shape, dt=F32):
        return self.nc.dram_tensor(name, shape, dt, kind="ExternalInput")

    def dout(self, name, shape, dt=F32):
        return self.nc.dram_tensor(name, shape, dt, kind="ExternalOutput")

    def dscr(self, name, shape, dt=F32):
        return self.nc.dram_tensor(name, shape, dt)

    def act(self, reads, writes, out, in_, func, **kw):
        nc = self.nc
        self._emit(lambda: self.S.op(self.ACT, reads, writes, lambda: nc.scalar.activation(out=out, in_=in_, func=func, **kw)))

    def dve(self, reads, writes, fn):
        self._emit(lambda: self.S.op(self.DVE, reads, writes, fn))

    def pe(self, reads, writes, fn):
        self._emit(lambda: self.S.op(self.PE, reads, writes, fn))

    def load(self, E, out, in_, wbuf, rbufs=()):
        eng = E.eng
        self._emit(lambda: self.S.dma(E, list(rbufs), [wbuf], lambda: eng.dma_start(out=out, in_=in_), wbuf))

    def store(self, E, out, in_, rbuf, wbuf):
        eng = E.eng
        self._emit(lambda: self.S.dma(E, [rbuf], [wbuf], lambda: eng.dma_start(out=out, in_=in_), rbuf))

    def captured(self, fn, *args):
        self.capture = lst = []
        fn(*args)
        self.capture = None
        return lst

    @staticmethod
    def merge(A, B):
        ia = ib = 0
        while ia < len(A) or ib < len(B):
            if ib >= len(B) or (ia < len(A) and ia * len(B) <= ib * len(A)):
                A[ia](); ia += 1
            else:
                B[ib](); ib += 1

    def finish(self, out_bufs):
        S, nc = self.S, self.nc
        for E in (self.SP, self.POOL):
            S.wait_all(E, out_bufs)
        with nc.Block() as block:
            @block.tensor
            def _(e): S.replay(self.PE)
            @block.scalar
            def _(e): S.replay(self.ACT)
            @block.vector
            def _(e): S.replay(self.DVE)
            @block.gpsimd
            def _(e): S.replay(self.POOL)
            @block.sync
            def _(e): S.replay(self.SP)
        self.st.close()
        return nc


def rstd_from_psum(C, ps_ss, Bps, out, Bout, n, epsc, Beps):
    C.act([Bps, Beps], [Bout], out, ps_ss, AF.Ln, scale=1.0 / n, bias=epsc)
    C.act([Bout], [Bout], out, out, AF.Exp, scale=-0.5)


def build_mixer(ntok, layer):
    C = Ctx()
    nc, S = C.nc, C.S
    PE, ACT, DVE, POOL, SP = C.PE, C.ACT, C.DVE, C.POOL, C.SP
    TL = 512
    NTL = ntok // TL
    xT = C.din("xT", [D, ntok])
    wA = C.din("wA", [128, DC, 1024])
    nw_d = C.din("nw", [128, DC])
    lb_d = C.din("lb", [128, 4])
    hw_d = C.din("hw", [128, 1])
    cw_d = C.din("cw", [128, 3])
    ident_d = C.din("ident", [128, 128])
    mask_d = C.din("masks", [64, 128])
    rmask_d = C.din("rmask", [128, TL])
    mixT = C.dout("mixT", [256, ntok])
    obwd = C.dscr("obwd", [128, ntok])
    xv = xT.ap().rearrange("(c p) t -> p c t", p=128)

    w_bf, Bw = C.sb([128, DC, 1024], BF16, "w")
    C.load(POOL, w_bf[:], wA.ap(), Bw)
    nw, Bnw = C.sb([128, DC], F32, "nw"); C.load(SP, nw[:], nw_d.ap(), Bnw)
    lb, Blb = C.sb([128, 4], F32, "lb"); C.load(SP, lb[:], lb_d.ap(), Blb)
    hw, Bhw = C.sb([128, 1], F32, "hw"); C.load(SP, hw[:], hw_d.ap(), Bhw)
    cw, Bcw = C.sb([128, 3], F32, "cw"); C.load(SP, cw[:], cw_d.ap(), Bcw)
    idf, Bidf = C.sb([128, 128], F32, "idf"); C.load(SP, idf[:], ident_d.ap(), Bidf)
    mk, Bmk = C.sb([64, 128], F32, "mk"); C.load(SP, mk[:], mask_d.ap(), Bmk)
    rm, Brm = C.sb([128, TL], F32, "rm"); C.load(SP, rm[:], rmask_d.ap(), Brm)
    idb, Bidb = C.sb([128, 128], BF16, "idb")
    C.dve([Bidf], [Bidb], lambda: nc.vector.tensor_copy(out=idb[:], in_=idf[:]))
    ones, Bones = C.sb([128, 128], BF16, "ones")
    C.dve([], [Bones], lambda: nc.vector.memset(ones[:], 1.0))
    epsc, Beps = C.sb([128, 1], F32, "eps")
    C.dve([], [Beps], lambda: nc.vector.memset(epsc[:], EPS))
    lbp, Blbp = C.sb([128, 6], F32, "lbp")
    for d_ in range(2):
        c0 = d_ * 3
        if layer == 0:
            C.dve([], [Blbp], lambda c0=c0: nc.vector.memset(lbp[:, c0:c0 + 1], 0.0))
        else:
            C.dve([Blb], [Blbp], lambda c0=c0, d_=d_: nc.vector.tensor_tensor(
                out=lbp[:, c0:c0 + 1], in0=lb[:, 2 * d_ + 1:2 * d_ + 2], in1=lb[:, 2 * d_:2 * d_ + 1], op=ALU.subtract))
            C.act([Blbp], [Blbp], lbp[:, c0:c0 + 1], lbp[:, c0:c0 + 1], AF.Sigmoid)
        C.dve([Blbp], [Blbp], lambda c0=c0: nc.vector.tensor_scalar(
            out=lbp[:, c0 + 1:c0 + 2], in0=lbp[:, c0:c0 + 1], scalar1=-1.0, scalar2=1.0, op0=ALU.mult, op1=ALU.add))
        C.dve([Blbp], [Blbp], lambda c0=c0: nc.vector.tensor_scalar(
            out=lbp[:, c0 + 2:c0 + 3], in0=lbp[:, c0 + 1:c0 + 2], scalar1=-1.0, scalar2=None, op0=ALU.mult))

    hTd = C.dscr("hTd", [NTL, 128, DC, TL], BF16)
    xts = [C.sb([128, DC, TL], F32, "xt") for _ in range(2)]
    sq, Bsq = C.sb([128, DC, TL], BF16, "sq")
    hTs = [C.sb([128, DC, TL], BF16, "hT"), (sq, Bsq)]
    def t32(name): return C.sb([128, TL], F32, name)
    alias_k = [0]
    def a32(name):
        k = alias_k[0]; alias_k[0] += 1
        assert k < DC
        return xts[1][0][:, k, :], Buf("%s_a%d" % (name, k))
    class _V:
        def __init__(self, ap): self.ap_ = ap
        def __getitem__(self, key): return self.ap_[key] if key != slice(None) else self.ap_
    def a32t(name):
        ap, b = a32(name)
        return _V(ap), b
    rstd, Brstd = t32("rstd"); sig, Bsig = t32("sig")
    bb, Bbb = t32("bb"); cc, Bcc = t32("cc"); Ei, BEi = t32("Ei")
    osum, Bosum = a32t("osum"); ro, Bro = a32t("ro")
    yr, Byr = a32t("yr"); yc, Byc = a32t("yc"); ycv, Bycv = t32("ycv"); obs, Bobs = t32("obs")
    qss = [t32("qs") for _ in range(2)]; gs = [t32("g") for _ in range(2)]; kfs = [t32("kf") for _ in range(2)]
    Es = [t32("E") for _ in range(2)]; sgs = [a32t("sg") for _ in range(3)]; obl = [a32t("ob") for _ in range(3)]
    osq, Bosq = C.sb([128, TL], BF16, "osq")
    qbs = [C.sb([128, TL], BF16, "qb") for _ in range(2)]; kbs = [C.sb([128, TL], BF16, "kb") for _ in range(2)]
    kbts = [C.sb([64, 8, 128], BF16, "kbt") for _ in range(2)]; vts = [C.sb([64, 8, 128], BF16, "vt") for _ in range(3)]
    scTs = [C.sb([64, 8, 64], BF16, "scT") for _ in range(2)]
    U, BU = C.sb([128, 9, 128], F32, "U")
    Ub, BUb = C.sb([128, 8, 128], BF16, "Ub")
    Wd, BWd = C.sb([128, 8, 128], F32, "Wd")
    vTs, BvTs = C.sb([128, TL], BF16, "vTs")
    ub = [a32t("ub") for _ in range(3)]
    gb = [a32t("gb") for _ in range(3)]
    ps_ss, Bpss = C.ps([128, TL], F32, "pss")
    pp = [C.ps([128, TL], F32, "pp") for _ in range(2)]
    ps_o, Bpo = C.ps([128, TL], F32, "po")
    ps_m, Bpm = C.ps([64, TL], F32, "pm")
    ps_t, Bpt = C.ps([64, 8, 128], BF16, "pt")
    ps_t1, Bpt1 = C.ps([64, 8, 128], BF16, "pt1")
    pdS, BpdS = C.ps([128, 4, 128], F32, "pdS")
    ppi = [0]

    vtd = C.dscr("vtd", [NTL, 64, 8, 128], BF16)
    qsd = C.dscr("qsd", [NTL, 128, TL], F32)
    Bvtd = [Buf("vtd%d" % i) for i in range(NTL)]
    Bqsd = [Buf("qsd%d" % i) for i in range(NTL)]
    Bx_out = [Buf("mixo%d" % i) for i in range(2 * NTL + 2)]
    Bobwd = [Buf("obwd%d" % i) for i in range(NTL)]
    BhTd = [Buf("hTd%d" % i) for i in range(NTL)]

    def load_x(ti):
        xtt, Bxt = xts[ti % 2]
        C.load(SP, xtt[:], xv[:, :, ti * TL:(ti + 1) * TL], Bxt)

    def norm_tile(ti, par):
        hT, BhT = hTs[par]
        xtt, Bxt = xts[ti % 2]
        C.act([Bxt], [Bsq], sq[:], xtt[:], AF.Square)
        C.pe([Bsq, Bones], [Bpss], lambda: [nc.tensor.matmul(ps_ss[:], lhsT=ones[:], rhs=sq[:, c, :], start=(c == 0), stop=(c == DC - 1)) for c in range(DC)])
        rstd_from_psum(C, ps_ss[:], Bpss, rstd[:], Brstd, D, epsc[:], Beps)
        for c in range(DC):
            C.dve([Bxt, Bnw, Brstd], [BhT], lambda c=c: nc.vector.scalar_tensor_tensor(
                out=hT[:, c, :], in0=xtt[:, c, :], scalar=nw[:, c:c + 1], in1=rstd[:], op0=ALU.mult, op1=ALU.mult))
        C.store(SP, hTd.ap()[ti], hT[:], BhT, BhTd[ti])

    pass2 = [False]

    def proj(gi, par):
        hT, BhT = hTs[par if pass2[0] else 0]
        p, Bp = pp[ppi[0] % 2]; ppi[0] += 1
        C.pe([BhT, Bw], [Bp], lambda: [nc.tensor.matmul(p[:], lhsT=w_bf[:, c, gi * 128:(gi + 1) * 128], rhs=hT[:, c, :], start=(c == 0), stop=(c == DC - 1)) for c in range(DC)])
        return p, Bp

    def vtok(ti):
        vt, Bvt = vts[ti % 3]
        pv_, Bpv_ = proj(3, ti % 2)
        C.act([Bpv_], [BvTs], vTs[:], pv_[:], AF.Copy)
        C.pe([BvTs, Bidb], [Bpt1], lambda: [nc.tensor.transpose(out=ps_t1[:, j, :], in_=vTs[:, j * 64:(j + 1) * 64], identity=idb[:]) for j in range(8)])
        C.act([Bpt1], [Bvt], vt[:], ps_t1[:], AF.Copy)

    def qgates(ti, zgrp, dirn):
        par = ti % 2
        qs, Bqs = qss[par]; g, Bg = gs[par]; kf, Bkf = kfs[par]
        c0 = dirn * 3
        if dirn == 1:
            pq, Bpq = proj(0, par)
            C.act([Bpq], [Bqs], qs[:], pq[:], AF.Silu)
            C.store(SP, qsd.ap()[ti], qs[:], Bqs, Bqsd[ti])
        else:
            C.load(SP, qs[:], qsd.ap()[ti], Bqs, [Bqsd[ti]])
        pz, Bpz = proj(zgrp, par)
        C.act([Bpz], [Bsig], sig[:], pz[:], AF.Sigmoid)
        C.act([Bsig, Blbp], [Bg], g[:], sig[:], AF.Ln, scale=lbp[:, c0 + 1:c0 + 2], bias=lbp[:, c0:c0 + 1])
        C.dve([Bsig, Blbp], [Bkf], lambda: nc.vector.tensor_scalar(out=kf[:], in0=sig[:], scalar1=lbp[:, c0 + 2:c0 + 3], scalar2=lbp[:, c0 + 1:c0 + 2], op0=ALU.mult, op1=ALU.add))

    def stage2(ti, fwd):
        par = ti % 2
        qs, Bqs = qss[par]; g, Bg = gs[par]; kf, Bkf = kfs[par]
        E, BE = Es[par]; qb, Bqb = qbs[par]; kb, Bkb = kbs[par]; kbt, Bkbt = kbts[par]
        scT, BscT = scTs[par]
        mcol = 0 if fwd else 64
        C.dve([Bg, Brm], [Bbb], lambda: nc.vector.tensor_tensor_scan(out=bb[:], data0=rm[:], data1=g[:], initial=0.0, op0=ALU.mult, op1=ALU.add))
        src, Bsrc = bb, Bbb
        if not fwd:
            C.dve([Bg, Bbb], [Bcc], lambda: nc.vector.tensor_tensor(out=cc[:], in0=g[:], in1=bb[:], op=ALU.subtract))
            C.dve([Bcc, Bbb], [Bcc], lambda: nc.vector.tensor_tensor(
                out=cc[:].rearrange("p (c t) -> p c t", t=64), in0=cc[:].rearrange("p (c t) -> p c t", t=64),
                in1=bb[:].rearrange("p (c t) -> p c t", t=64)[:, :, 63:64].to_broadcast([128, 8, 64]), op=ALU.add))
            src, Bsrc = cc, Bcc
        C.act([Bsrc], [BE], E[:], src[:], AF.Exp)
        C.act([Bsrc], [BEi], Ei[:], src[:], AF.Exp, scale=-1.0)
        C.dve([Bqs, BE], [Bqb], lambda: nc.vector.tensor_tensor(out=qb[:], in0=qs[:], in1=E[:], op=ALU.mult))
        C.dve([Bkf, BEi], [Bkb], lambda: nc.vector.tensor_tensor(out=kb[:], in0=kf[:], in1=Ei[:], op=ALU.mult))
        C.pe([Bkb, Bidb], [Bpt], lambda: [nc.tensor.transpose(out=ps_t[:, j, :], in_=kb[:, j * 64:(j + 1) * 64], identity=idb[:]) for j in range(8)])
        C.dve([Bpt], [Bkbt], lambda: nc.vector.tensor_copy(out=kbt[:], in_=ps_t[:]))
        C.pe([Bkb, Bqb], [Bpm], lambda: [nc.tensor.matmul(ps_m[:, j * 64:(j + 1) * 64], lhsT=kb[:, j * 64:(j + 1) * 64], rhs=qb[:, j * 64:(j + 1) * 64], start=True, stop=True) for j in range(8)])
        C.dve([Bpm, Bmk], [BscT], lambda: nc.vector.tensor_tensor(
            out=scT[:], in0=ps_m[:].rearrange("p (c t) -> p c t", t=64),
            in1=mk[:, mcol:mcol + 64].unsqueeze(1).to_broadcast([64, 8, 64]), op=ALU.mult))

    def chunks(ti, fwd):
        par = ti % 2
        E, BE = Es[par]; qb, Bqb = qbs[par]; kbt, Bkbt = kbts[par]; vt, Bvt = vts[ti % 3]; scT, BscT = scTs[par]
        dcol = 63 if fwd else 0
        Ev = E[:].rearrange("p (c t) -> p c t", t=64)
        for hb in range(2):
            C.pe([Bkbt, Bvt], [BpdS], lambda hb=hb: [nc.tensor.matmul(pdS[:, jj, :], lhsT=kbt[:, hb * 4 + jj, :], rhs=vt[:, hb * 4 + jj, :], start=True, stop=True) for jj in range(4)])
            C.dve([BpdS, BE], [BWd], lambda hb=hb: nc.vector.tensor_tensor(
                out=Wd[:, hb * 4:(hb + 1) * 4, :], in0=pdS[:],
                in1=Ev[:, hb * 4:(hb + 1) * 4, dcol:dcol + 1].to_broadcast([128, 4, 128]), op=ALU.mult))
        order = range(8) if fwd else range(7, -1, -1)
        for j in order:
            src, dst = (j, j + 1) if fwd else (j + 1, j)
            dc_ = j * 64 + dcol
            C.dve([BU, BE, BWd], [BU], lambda src=src, dst=dst, dc_=dc_, j=j: nc.vector.scalar_tensor_tensor(
                out=U[:, dst, :], in0=U[:, src, :], scalar=E[:, dc_:dc_ + 1], in1=Wd[:, j, :], op0=ALU.mult, op1=ALU.add))
        lo = 0 if fwd else 1
        C.dve([BU], [BUb], lambda lo=lo: nc.vector.tensor_copy(out=Ub[:], in_=U[:, lo:lo + 8, :]))
        cs_, cd_ = (8, 0) if fwd else (0, 8)
        C.dve([BU], [BU], lambda cs_=cs_, cd_=cd_: nc.vector.tensor_copy(out=U[:, cd_, :], in_=U[:, cs_, :]))
        C.pe([Bvt, BscT, BUb, Bqb], [Bpo], lambda: [m for j in range(8) for m in (
            nc.tensor.matmul(ps_o[:, j * 64:(j + 1) * 64], lhsT=vt[:, j, :], rhs=scT[:, j, :], start=True, stop=False),
            nc.tensor.matmul(ps_o[:, j * 64:(j + 1) * 64], lhsT=Ub[:, j, :], rhs=qb[:, j * 64:(j + 1) * 64], start=False, stop=True))])

    def merge3(A, B, Cc):
        n = max(len(A), len(B), len(Cc), 1)
        ia = ib = ic = 0
        for k in range(1, n + 1):
            while ia < len(A) and ia * n < k * len(A):
                A[ia](); ia += 1
            while ib < len(B) and ib * n < k * len(B):
                B[ib](); ib += 1
            while ic < len(Cc) and ic * n < k * len(Cc):
                Cc[ic](); ic += 1

    def s1_bwd(ti):
        if ti > 0:
            load_x(ti - 1)
        norm_tile(ti, 0)
        qgates(ti, 2, 1)
        vtok(ti)
        C.store(SP, vtd.ap()[ti], vts[ti % 3][0][:], vts[ti % 3][1], Bvtd[ti])

    def s3_bwd(ti):
        chunks(ti, False)

    def s3_bfin(ti):
        C.act([Bpo], [Bobs], obs[:], ps_o[:], AF.Copy)
        C.store(SP, obwd.ap()[:, ti * TL:(ti + 1) * TL], obs[:], Bobs, Bobwd[ti])

    C.dve([], [BU], lambda: nc.vector.memset(U[:], 0.0))
    load_x(NTL - 1)
    seq = list(range(NTL - 1, -1, -1))
    for k in range(-2, NTL):
        A = C.captured(s3_bwd, seq[k]) if 0 <= k < NTL else []
        Bl = C.captured(stage2, seq[k + 1], False) if 0 <= k + 1 < NTL else []
        Cl = C.captured(s1_bwd, seq[k + 2]) if 0 <= k + 2 < NTL else []
        merge3(A, Bl, Cl)
        if 0 <= k < NTL:
            s3_bfin(seq[k])

    def conv_final(ti, left, right):
        u, Bu = ub[ti % 3]; gt, Bgt = gb[ti % 3]
        C.dve([Bu, Bcw], [Byc], lambda: nc.vector.tensor_scalar(out=yc[:], in0=u[:], scalar1=cw[:, 1:2], scalar2=None, op0=ALU.mult))
        C.dve([Bu, Bcw, Byc], [Byc], lambda: nc.vector.scalar_tensor_tensor(out=yc[:, 1:TL], in0=u[:, 0:TL - 1], scalar=cw[:, 0:1], in1=yc[:, 1:TL], op0=ALU.mult, op1=ALU.add))
        C.dve([Bu, Bcw, Byc], [Byc], lambda: nc.vector.scalar_tensor_tensor(out=yc[:, 0:TL - 1], in0=u[:, 1:TL], scalar=cw[:, 2:3], in1=yc[:, 0:TL - 1], op0=ALU.mult, op1=ALU.add))
        if left is not None:
            la, Bl = left
            C.dve([Bl, Bcw, Byc], [Byc], lambda: nc.vector.scalar_tensor_tensor(out=yc[:, 0:1], in0=la, scalar=cw[:, 0:1], in1=yc[:, 0:1], op0=ALU.mult, op1=ALU.add))
        if right is not None:
            ra, Br = right
            C.dve([Br, Bcw, Byc], [Byc], lambda: nc.vector.scalar_tensor_tensor(out=yc[:, TL - 1:TL], in0=ra, scalar=cw[:, 2:3], in1=yc[:, TL - 1:TL], op0=ALU.mult, op1=ALU.add))
        C.dve([Byc, Bgt], [Bycv], lambda: nc.vector.tensor_tensor(out=ycv[:], in0=yc[:], in1=gt[:], op=ALU.mult))
        C.store(SP, mixT.ap()[128:256, ti * TL:(ti + 1) * TL], ycv[:], Bycv, Bx_out[NTL + ti])

    def load_h(ti):
        hT, BhT = hTs[ti % 2]
        C.load(SP, hT[:], hTd.ap()[ti], BhT, [BhTd[ti]])

    def s1_fwd(ti):
        par = ti % 2
        ob, Bob = obl[ti % 3]; sg, Bsg = sgs[ti % 3]
        C.load(SP, ob[:], obwd.ap()[:, ti * TL:(ti + 1) * TL], Bob, [Bobwd[ti]])
        if ti + 1 < NTL:
            load_h(ti + 1)
        qgates(ti, 1, 0)
        C.load(SP, vts[ti % 3][0][:], vtd.ap()[ti], vts[ti % 3][1], [Bvtd[ti]])
        pg, Bpg = proj(4, par)
        C.act([Bpg], [Bsg], sg[:], pg[:], AF.Silu)
        u, Bu = ub[ti % 3]; gt, Bgt = gb[ti % 3]
        p5, Bp5 = proj(5, par)
        C.act([Bp5], [Bgt], gt[:], p5[:], AF.Copy)
        p7, Bp7 = proj(7, par)
        C.act([Bp7], [Bu], u[:], p7[:], AF.Copy)
        p6, Bp6 = proj(6, par)
        C.dve([Bp6, Bu], [Bu], lambda u=u, p6=p6: nc.vector.tensor_tensor(out=u[:], in0=p6[:], in1=u[:], op=ALU.mult))
        if ti > 0:
            left = None
            if ti > 1:
                upp, Bupp = ub[(ti - 2) % 3]
                left = (upp[:, TL - 1:TL], Bupp)
            conv_final(ti - 1, left, (u[:, 0:1], Bu))

    def s3_fwd(ti):
        chunks(ti, True)

    def s3_fin(ti):
        ob, Bob = obl[ti % 3]; sg, Bsg = sgs[ti % 3]
        C.dve([Bpo, Bob], [Bosum], lambda: nc.vector.tensor_tensor(out=osum[:], in0=ps_o[:], in1=ob[:], op=ALU.add))
        C.act([Bosum], [Bosq], osq[:], osum[:], AF.Square)
        C.pe([Bosq, Bones], [Bpss], lambda: nc.tensor.matmul(ps_ss[:], lhsT=ones[:], rhs=osq[:], start=True, stop=True))
        rstd_from_psum(C, ps_ss[:], Bpss, ro[:], Bro, 128, epsc[:], Beps)
        C.dve([Bosum, Bhw, Bro], [Byr], lambda: nc.vector.scalar_tensor_tensor(out=yr[:], in0=osum[:], scalar=hw[:, 0:1], in1=ro[:], op0=ALU.mult, op1=ALU.mult))
        C.dve([Byr, Bsg], [Byr], lambda: nc.vector.tensor_tensor(out=yr[:], in0=yr[:], in1=sg[:], op=ALU.mult))
        C.store(SP, mixT.ap()[0:128, ti * TL:(ti + 1) * TL], yr[:], Byr, Bx_out[ti])

    S.barrier()
    pass2[0] = True
    C.dve([], [BU], lambda: nc.vector.memset(U[:], 0.0))
    load_h(0)
    for k in range(-2, NTL):
        A = C.captured(s3_fwd, k) if 0 <= k < NTL else []
        Bl = C.captured(stage2, k + 1, True) if 0 <= k + 1 < NTL else []
        Cl = C.captured(s1_fwd, k + 2) if 0 <= k + 2 < NTL else []
        merge3(A, Bl, Cl)
        if 0 <= k < NTL:
            s3_fin(k)
    left = None
    if NTL > 1:
        upp, Bupp = ub[(NTL - 2) % 3]
        left = (upp[:, TL - 1:TL], Bupp)
    conv_final(NTL - 1, left, None)
    return C.finish(Bx_out)


_CACHE = {}


def _consts():
    m = np.zeros((64, 128), np.float32)
    m[:, 0:64] = np.triu(np.ones((64, 64), np.float32))
    m[:, 64:128] = np.tril(np.ones((64, 64), np.float32))
    rm = np.ones((128, 512), np.float32)
    rm[:, ::64] = 0.0
    return {"ident": np.eye(128, dtype=np.float32), "masks": m, "rmask": rm}


def _pc(v):
    return np.ascontiguousarray(v.reshape(-1, 128).T)


def run_mixer(xT, p, layer):
    ntok = xT.shape[1]
    key = ("M", ntok, layer)
    if key not in _CACHE:
        _CACHE[key] = build_mixer(ntok, layer)
    nc = _CACHE[key]
    w_in = p["w_in"][layer]
    cst = _consts()
    ins = []
    for h in range(NCORES):
        cols = np.concatenate([np.arange(g * 1024 + h * 128, g * 1024 + (h + 1) * 128) for g in range(8)])
        wa = w_in[:, cols]
        wa = np.ascontiguousarray(wa.reshape(DC, 128, 1024).transpose(1, 0, 2))
        hs = slice(h * 128, (h + 1) * 128)
        lb = np.stack([p["lb_fwd"][0, hs], p["lb_fwd"][1, hs], p["lb_bwd"][0, hs], p["lb_bwd"][1, hs]], axis=1)
        d = {"xT": xT, "wA": wa, "nw": _pc(p["attn_norm_w"][layer]), "lb": np.ascontiguousarray(lb, dtype=np.float32),
             "hw": np.ascontiguousarray(p["hgrn_norm_w"][layer, hs].reshape(128, 1)),
             "cw": np.ascontiguousarray(p["conv_w"][layer][:, hs].T)}
        d.update(cst)
        ins.append(d)
    res = run_bass_kernel_spmd(nc, ins, core_ids=list(range(NCORES)))
    mixT = np.empty((D, ntok), np.float32)
    for h in range(NCORES):
        o = res.results[h]["mixT"]
        mixT[h * 128:(h + 1) * 128] = o[0:128]
        mixT[1024 + h * 128:1024 + (h + 1) * 128] = o[128:256]
    return mixT


def build_ffn(nt):
    C = Ctx()
    nc, S = C.nc, C.S
    PE, ACT, DVE, POOL, SP = C.PE, C.ACT, C.DVE, C.POOL, C.SP
    TL = 512
    HT = 1024
    NH = nt // HT
    NU = FC // 2
    xT = C.din("xT", [D, nt])
    mT = C.din("mT", [D, nt])
    wo_d = C.din("wo", [4, 128, DC, 512])
    wgu_d = C.din("wgu", [NU, 128, DC, 512])
    wd_d = C.din("wd", [NU, 128, 2, D])
    nw_d = C.din("nw", [128, DC])
    fw_d = C.din("fw", [128, DC])
    oX = C.dout("oX", [D, nt])
    oN = C.dout("oN", [D, nt])
    xv = xT.ap().rearrange("(c p) t -> p c t", p=128)
    mv = mT.ap().rearrange("(c p) t -> p c t", p=128)
    oXv = oX.ap().rearrange("(c p) t -> p c t", p=128)
    oNv = oN.ap().rearrange("(c p) t -> p c t", p=128)

    nw, Bnw = C.sb([128, DC], F32, "nw"); C.load(SP, nw[:], nw_d.ap(), Bnw)
    fw, Bfw = C.sb([128, DC], F32, "fw"); C.load(SP, fw[:], fw_d.ap(), Bfw)
    ones, Bones = C.sb([128, 128], BF16, "ones")
    C.dve([], [Bones], lambda: nc.vector.memset(ones[:], 1.0))
    epsc, Beps = C.sb([128, 1], F32, "eps")
    C.dve([], [Beps], lambda: nc.vector.memset(epsc[:], EPS))

    acc, _ = C.sb([128, DC, HT], F32, "acc")
    Bacc = [[Buf("acc%d_%d" % (c, t)) for t in range(2)] for c in range(DC)]
    Baccl = [b for r in Bacc for b in r]
    mb, _ = C.sb([128, DC, HT], BF16, "mb")
    Bmb = [Buf("mb%d" % t) for t in range(2)]
    h2, Bh2 = C.sb([128, DC, HT], BF16, "h2")
    Bh2t = [Buf("h2_%d" % t) for t in range(2)]
    rstd, _ = C.sb([128, HT], F32, "rstd")
    Brs = [Buf("rs%d" % t) for t in range(2)]
    wg = [C.sb([128, DC, 512], BF16, "wg") for _ in range(2)]
    wd = [C.sb([128, 2, D], BF16, "wd") for _ in range(2)]
    aT = [C.sb([128, 2, HT], BF16, "aT") for _ in range(2)]
    sgt = [C.sb([128, TL], F32, "sg") for _ in range(2)]
    outn, Boutn = C.sb([128, 4, TL], F32, "outn")
    pss, Bpss = C.ps([128, TL], F32, "pss")
    pg = [C.ps([128, TL], F32, "pg") for _ in range(2)]
    pu = [C.ps([128, TL], F32, "pu") for _ in range(2)]
    pd = [C.ps([128, TL], F32, "pd") for _ in range(3)]
    Bouts = []
    cnt = {"g": 0, "d": 0, "s": 0}

    def norm_half(wvec, Bwv, dst, Bdst_t):
        for tt in range(2):
            ts = slice(tt * TL, (tt + 1) * TL)
            C.act([Bacc[c][tt] for c in range(DC)], [Bmb[tt]], mb[:, :, ts], acc[:, :, ts], AF.Square)
            C.pe([Bmb[tt], Bones], [Bpss], lambda ts=ts: [nc.tensor.matmul(pss[:], lhsT=ones[:], rhs=mb[:, c, ts], start=(c == 0), stop=(c == DC - 1)) for c in range(DC)])
            rstd_from_psum(C, pss[:], Bpss, rstd[:, ts], Brs[tt], D, epsc[:], Beps)

    for hf in range(NH):
        t0 = hf * HT
        for tt in range(2):
            ts = slice(tt * TL, (tt + 1) * TL)
            for c4 in range(4):
                cs = slice(c4 * 4, c4 * 4 + 4)
                bl = [Bacc[c][tt] for c in range(c4 * 4, c4 * 4 + 4)]
                owner = bl[0]
                S.dma(SP, [], bl, lambda cs=cs, ts=ts, tt=tt, t0=t0: nc.sync.dma_start(out=acc[:, cs, ts], in_=xv[:, cs, t0 + tt * TL:t0 + (tt + 1) * TL]), owner)
            S.dma(POOL, [], [Bmb[tt]], lambda ts=ts, tt=tt, t0=t0: nc.gpsimd.dma_start(out=mb[:, :, ts], in_=mv[:, :, t0 + tt * TL:t0 + (tt + 1) * TL]), Bmb[tt])
        for uo in range(4):
            w, Bw = wg[cnt["g"] % 2]; cnt["g"] += 1
            C.load(POOL, w[:], wo_d.ap()[uo], Bw)
            for dl in range(4):
                dc = uo * 4 + dl
                for tt in range(2):
                    ts = slice(tt * TL, (tt + 1) * TL)
                    p, Bp = pd[cnt["d"] % 3]; cnt["d"] += 1
                    C.pe([Bw, Bmb[tt]], [Bp], lambda w=w, p=p, dl=dl, ts=ts: [nc.tensor.matmul(p[:], lhsT=w[:, e, dl * 128:(dl + 1) * 128], rhs=mb[:, e, ts], start=(e == 0), stop=(e == DC - 1)) for e in range(DC)])
                    C.dve([Bp, Bacc[dc][tt]], [Bacc[dc][tt]], lambda p=p, dc=dc, ts=ts: nc.vector.tensor_tensor(out=acc[:, dc, ts], in0=p[:], in1=acc[:, dc, ts], op=ALU.add))
        norm_half(nw, Bnw, h2, Bh2t)
        for tt in range(2):
            ts = slice(tt * TL, (tt + 1) * TL)
            for c in range(DC):
                C.dve([Bacc[c][tt], Bnw, Brs[tt]], [Bh2t[tt]], lambda c=c, ts=ts: nc.vector.scalar_tensor_tensor(
                    out=h2[:, c, ts], in0=acc[:, c, ts], scalar=nw[:, c:c + 1], in1=rstd[:, ts], op0=ALU.mult, op1=ALU.mult))
        for u in range(NU):
            w, Bw = wg[cnt["g"] % 2]; cnt["g"] += 1
            w2, Bw2 = wd[u % 2]
            a, Ba = aT[u % 2]
            C.load(POOL, w[:], wgu_d.ap()[u], Bw)
            C.load(POOL, w2[:], wd_d.ap()[u], Bw2)
            for fl in range(2):
                for tt in range(2):
                    ts = slice(tt * TL, (tt + 1) * TL)
                    k = cnt["s"] % 2; cnt["s"] += 1
                    pgk, Bpg = pg[k]; puk, Bpu = pu[k]; sgk, Bsg = sgt[k]
                    C.pe([Bw, Bh2t[tt]], [Bpg], lambda w=w, pgk=pgk, fl=fl, ts=ts: [nc.tensor.matmul(pgk[:], lhsT=w[:, c, fl * 128:(fl + 1) * 128], rhs=h2[:, c, ts], start=(c == 0), stop=(c == DC - 1)) for c in range(DC)])
                    C.pe([Bw, Bh2t[tt]], [Bpu], lambda w=w, puk=puk, fl=fl, ts=ts: [nc.tensor.matmul(puk[:], lhsT=w[:, c, 256 + fl * 128:256 + (fl + 1) * 128], rhs=h2[:, c, ts], start=(c == 0), stop=(c == DC - 1)) for c in range(DC)])
                    C.act([Bpg], [Bsg], sgk[:], pgk[:], AF.Silu)
                    C.dve([Bsg, Bpu], [Ba], lambda a=a, sgk=sgk, puk=puk, fl=fl, ts=ts: nc.vector.tensor_tensor(out=a[:, fl, ts], in0=sgk[:], in1=puk[:], op=ALU.mult))
            for dc in range(DC):
                for tt in range(2):
                    ts = slice(tt * TL, (tt + 1) * TL)
                    p, Bp = pd[cnt["d"] % 3]; cnt["d"] += 1
                    C.pe([Bw2, Ba], [Bp], lambda w2=w2, a=a, p=p, dc=dc, ts=ts: [nc.tensor.matmul(p[:], lhsT=w2[:, fl, dc * 128:(dc + 1) * 128], rhs=a[:, fl, ts], start=(fl == 0), stop=(fl == 1)) for fl in range(2)])
                    C.dve([Bp, Bacc[dc][tt]], [Bacc[dc][tt]], lambda p=p, dc=dc, ts=ts: nc.vector.tensor_tensor(out=acc[:, dc, ts], in0=p[:], in1=acc[:, dc, ts], op=ALU.add))
        for tt in range(2):
            ts = slice(tt * TL, (tt + 1) * TL)
            for c4 in range(4):
                cs = slice(c4 * 4, c4 * 4 + 4)
                bl = [Bacc[c][tt] for c in range(c4 * 4, c4 * 4 + 4)]
                bo = Buf("ox"); Bouts.append(bo)
                S.dma(SP, bl, [bo], lambda cs=cs, ts=ts, tt=tt, t0=t0: nc.sync.dma_start(out=oXv[:, cs, t0 + tt * TL:t0 + (tt + 1) * TL], in_=acc[:, cs, ts]), bl[0])
        norm_half(fw, Bfw, None, None)
        for tt in range(2):
            ts = slice(tt * TL, (tt + 1) * TL)
            for c4 in range(4):
                for cl in range(4):
                    c = c4 * 4 + cl
                    C.dve([Bacc[c][tt], Bfw, Brs[tt]], [Boutn], lambda c=c, cl=cl, ts=ts: nc.vector.scalar_tensor_tensor(
                        out=outn[:, cl, :], in0=acc[:, c, ts], scalar=fw[:, c:c + 1], in1=rstd[:, ts], op0=ALU.mult, op1=ALU.mult))
                bo = Buf("on"); Bouts.append(bo)
                S.dma(SP, [Boutn], [bo], lambda tt=tt, c4=c4, t0=t0: nc.sync.dma_start(out=oNv[:, c4 * 4:c4 * 4 + 4, t0 + tt * TL:t0 + (tt + 1) * TL], in_=outn[:]), Boutn)
    return C.finish(Bouts)


def run_ffn(xT, mixT, p, layer):
    ntok = xT.shape[1]
    nt = ntok // NCORES
    key = ("F", nt)
    if key not in _CACHE:
        _CACHE[key] = build_ffn(nt)
    nc = _CACHE[key]
    w_out = p["w_out"][layer]; w_gu = p["w_gate_up"][layer]; w_dn = p["w_down"][layer]
    wo = np.ascontiguousarray(w_out.reshape(DC, 128, 4, 512).transpose(2, 1, 0, 3))
    NU = FC // 2
    gcols = w_gu[:, :DFF].reshape(D, NU, 256); ucols = w_gu[:, DFF:].reshape(D, NU, 256)
    wgu = np.concatenate([gcols, ucols], axis=2)
    wgu = np.ascontiguousarray(wgu.reshape(DC, 128, NU, 512).transpose(2, 1, 0, 3))
    wd = np.ascontiguousarray(w_dn.reshape(NU, 2, 128, D).transpose(0, 2, 1, 3))
    common = {"wo": wo, "wgu": wgu, "wd": wd, "nw": _pc(p["ffn_norm_w"][layer]), "fw": _pc(p["final_norm_w"])}
    ins = []
    for c in range(NCORES):
        d = {"xT": np.ascontiguousarray(xT[:, c * nt:(c + 1) * nt]), "mT": np.ascontiguousarray(mixT[:, c * nt:(c + 1) * nt])}
        d.update(common)
        ins.append(d)
    res = run_bass_kernel_spmd(nc, ins, core_ids=list(range(NCORES)))
    x2 = np.concatenate([res.results[c]["oX"] for c in range(NCORES)], axis=1)
    xn = np.concatenate([res.results[c]["oN"] for c in range(NCORES)], axis=1)
    return x2, xn


def kernel(x, attn_norm_w, w_in, lb_fwd, lb_bwd, hgrn_norm_w, conv_w, w_out, ffn_norm_w, w_gate_up, w_down, final_norm_w):
    p = {"attn_norm_w": np.asarray(attn_norm_w, np.float32), "w_in": np.asarray(w_in, np.float32),
         "lb_fwd": np.asarray(lb_fwd, np.float32), "lb_bwd": np.asarray(lb_bwd, np.float32),
         "hgrn_norm_w": np.asarray(hgrn_norm_w, np.float32), "conv_w": np.asarray(conv_w, np.float32),
         "w_out": np.asarray(w_out, np.float32), "ffn_norm_w": np.asarray(ffn_norm_w, np.float32),
         "w_gate_up": np.asarray(w_gate_up, np.float32), "w_down": np.asarray(w_down, np.float32),
         "final_norm_w": np.asarray(final_norm_w, np.float32)}
    x = np.asarray(x, np.float32)
    xT = np.ascontiguousarray(x[0].T)
    xn = None
    for layer in range(2):
        mixT = run_mixer(xT, p, layer)
        xT, xn = run_ffn(xT, mixT, p, layer)
    return np.ascontiguousarray(xn.T)[None].astype(np.float32)
```

```python
import numpy as np
import concourse.bass as bass
import concourse.mybir as mybir

F32 = mybir.dt.float32
BF16 = mybir.dt.bfloat16
AF = mybir.ActivationFunctionType
ALU = mybir.AluOpType
AX = mybir.AxisListType


class Buf:
    __slots__ = ("name", "w", "r", "dsem")

    def __init__(self, name):
        self.name = name
        self.w = {}
        self.r = {}
        self.dsem = None


class DSem:
    def __init__(self, sem):
        self.sem = sem
        self.count = 0


class Eng:
    def __init__(self, name, eng, sem):
        self.name = name
        self.eng = eng
        self.sem = sem
        self.count = 0
        self.seen = {}
        self.prog = []


class Sched:
    def __init__(self, nc, stack):
        self.nc = nc
        self.stack = stack
        self.engs = {}
        self.nsem = 0
        self.n_wait = 0
        self.dsems = []

    def barrier(self):
        evs = [(e.sem, e.count) for e in self.engs.values() if e.count > 0]
        evs += [(d.sem, d.count) for d in self.dsems if d.count > 0]
        for E in self.engs.values():
            for sem, val in evs:
                if sem is E.sem:
                    continue
                k = id(sem)
                if E.seen.get(k, 0) < val:
                    E.prog.append(("w", sem, val))
                    E.seen[k] = val

    def new_sem(self, name):
        self.nsem += 1
        return self.stack.enter_context(self.nc.semaphore(name))

    def add_engine(self, name, eng):
        e = Eng(name, eng, self.new_sem("s_" + name))
        self.engs[name] = e
        return e

    def _need(self, need, evs):
        for k, (sem, val) in evs.items():
            if k not in need or need[k][1] < val:
                need[k] = (sem, val)

    def _waits(self, E, reads, writes):
        need = {}
        for b in reads:
            self._need(need, b.w)
        for b in writes:
            self._need(need, b.w)
            self._need(need, b.r)
        for k, (sem, val) in need.items():
            if E.seen.get(k, 0) < val:
                E.prog.append(("w", sem, val))
                E.seen[k] = val
                self.n_wait += 1

    def _record(self, ev, reads, writes):
        k = id(ev[0])
        for b in writes:
            b.w = {k: ev}
            b.r = {}
        for b in reads:
            if b in writes:
                continue
            b.r[k] = ev

    def op(self, E, reads, writes, fn):
        self._waits(E, reads, writes)
        E.count += 1
        E.prog.append(("o", fn, E.sem, 1))
        ev = (E.sem, E.count)
        self._record(ev, reads, writes)
        return ev

    def dma(self, E, reads, writes, fn, owner, ndma=1):
        if owner.dsem is None:
            owner.dsem = DSem(self.new_sem("d_" + owner.name))
            self.dsems.append(owner.dsem)
        self._waits(E, reads, writes)
        E.prog.append(("d", fn, owner.dsem.sem, 16))
        owner.dsem.count += 16 * ndma
        ev = (owner.dsem.sem, owner.dsem.count)
        self._record(ev, reads, writes)
        return ev

    def wait_all(self, E, bufs):
        need = {}
        for b in bufs:
            self._need(need, b.w)
            self._need(need, b.r)
        for k, (sem, val) in need.items():
            if E.seen.get(k, 0) < val:
                E.prog.append(("w", sem, val))
                E.seen[k] = val

    def replay(self, E):
        for it in E.prog:
            if it[0] == "w":
                E.eng.wait_ge(it[1], it[2])
            elif it[0] == "o":
                ins = it[1]()
                if isinstance(ins, (list, tuple)):
                    ins = ins[-1]
                ins.then_inc(it[2], it[3])
            else:
                ins = it[1]()
                if not isinstance(ins, (list, tuple)):
                    ins = [ins]
                for i in ins:
                    i.then_inc(it[2], it[3])

from contextlib import ExitStack
from concourse.bass_utils import run_bass_kernel_spmd

NCORES = 8
D = 2048
DC = 16
DFF = 5632
FC = 44
EPS = 1e-6


_LAST = [None]


class Ctx:
    def __init__(self):
        _LAST[0] = self
        self.nc = nc = bass.Bass("TRN2", target_bir_lowering=False)
        self.st = ExitStack()
        self.S = S = Sched(nc, self.st)
        self.PE = S.add_engine("pe", nc.tensor)
        self.ACT = S.add_engine("act", nc.scalar)
        self.DVE = S.add_engine("dve", nc.vector)
        self.POOL = S.add_engine("pool", nc.gpsimd)
        self.SP = S.add_engine("sp", nc.sync)
        self.n = 0
        self.capture = None

    def _emit(self, f):
        if self.capture is not None:
            self.capture.append(f)
        else:
            f()

    def sb(self, shape, dt, name="t"):
        self.n += 1
        nm = "%s_%d" % (name, self.n)
        return self.st.enter_context(self.nc.sbuf_tensor(nm, shape, dt)), Buf(nm)

    def ps(self, shape, dt, name="p"):
        self.n += 1
        nm = "%s_%d" % (name, self.n)
        return self.st.enter_context(self.nc.psum_tensor(nm, shape, dt)), Buf(nm)

    def din(self, name, shape, dt=F32):
        return self.nc.dram_tensor(name, shape, dt, kind="ExternalInput")

    def dout(self, name, shape, dt=F32):
        return self.nc.dram_tensor(name, shape, dt, kind="ExternalOutput")

    def dscr(self, name, shape, dt=F32):
        return self.nc.dram_tensor(name, shape, dt)

    def act(self, reads, writes, out, in_, func, **kw):
        nc = self.nc
        self._emit(lambda: self.S.op(self.ACT, reads, writes, lambda: nc.scalar.activation(out=out, in_=in_, func=func, **kw)))

    def dve(self, reads, writes, fn):
        self._emit(lambda: self.S.op(self.DVE, reads, writes, fn))

    def pe(self, reads, writes, fn):
        self._emit(lambda: self.S.op(self.PE, reads, writes, fn))

    def load(self, E, out, in_, wbuf, rbufs=()):
        eng = E.eng
        self._emit(lambda: self.S.dma(E, list(rbufs), [wbuf], lambda: eng.dma_start(out=out, in_=in_), wbuf))

    def store(self, E, out, in_, rbuf, wbuf):
        eng = E.eng
        self._emit(lambda: self.S.dma(E, [rbuf], [wbuf], lambda: eng.dma_start(out=out, in_=in_), rbuf))

    def captured(self, fn, *args):
        self.capture = lst = []
        fn(*args)
        self.capture = None
        return lst

    @staticmethod
    def merge(A, B):
        ia = ib = 0
        while ia < len(A) or ib < len(B):
            if ib >= len(B) or (ia < len(A) and ia * len(B) <= ib * len(A)):
                A[ia](); ia += 1
            else:
                B[ib](); ib += 1

    def finish(self, out_bufs):
        S, nc = self.S, self.nc
        for E in (self.SP, self.POOL):
            S.wait_all(E, out_bufs)
        with nc.Block() as block:
            @block.tensor
            def _(e): S.replay(self.PE)
            @block.scalar
            def _(e): S.replay(self.ACT)
            @block.vector
            def _(e): S.replay(self.DVE)
            @block.gpsimd
            def _(e): S.replay(self.POOL)
            @block.sync
            def _(e): S.replay(self.SP)
        self.st.close()
        return nc


def rstd_from_psum(C, ps_ss, Bps, out, Bout, n, epsc, Beps):
    C.act([Bps, Beps], [Bout], out, ps_ss, AF.Ln, scale=1.0 / n, bias=epsc)
    C.act([Bout], [Bout], out, out, AF.Exp, scale=-0.5)


def build_mixer(ntok, layer):
    C = Ctx()
    nc, S = C.nc, C.S
    PE, ACT, DVE, POOL, SP = C.PE, C.ACT, C.DVE, C.POOL, C.SP
    TL = 512
    NTL = ntok // TL
    xT = C.din("xT", [D, ntok])
    wA = C.din("wA", [128, DC, 1024])
    nw_d = C.din("nw", [128, DC])
    lb_d = C.din("lb", [128, 4])
    hw_d = C.din("hw", [128, 1])
    cw_d = C.din("cw", [128, 3])
    ident_d = C.din("ident", [128, 128])
    mask_d = C.din("masks", [64, 128])
    rmask_d = C.din("rmask", [128, TL])
    mixT = C.dout("mixT", [256, ntok])
    obwd = C.dscr("obwd", [128, ntok])
    xv = xT.ap().rearrange("(c p) t -> p c t", p=128)

    w_bf, Bw = C.sb([128, DC, 1024], BF16, "w")
    C.load(POOL, w_bf[:], wA.ap(), Bw)
    nw, Bnw = C.sb([128, DC], F32, "nw"); C.load(SP, nw[:], nw_d.ap(), Bnw)
    lb, Blb = C.sb([128, 4], F32, "lb"); C.load(SP, lb[:], lb_d.ap(), Blb)
    hw, Bhw = C.sb([128, 1], F32, "hw"); C.load(SP, hw[:], hw_d.ap(), Bhw)
    cw, Bcw = C.sb([128, 3], F32, "cw"); C.load(SP, cw[:], cw_d.ap(), Bcw)
    idf, Bidf = C.sb([128, 128], F32, "idf"); C.load(SP, idf[:], ident_d.ap(), Bidf)
    mk, Bmk = C.sb([64, 128], F32, "mk"); C.load(SP, mk[:], mask_d.ap(), Bmk)
    rm, Brm = C.sb([128, TL], F32, "rm"); C.load(SP, rm[:], rmask_d.ap(), Brm)
    idb, Bidb = C.sb([128, 128], BF16, "idb")
    C.dve([Bidf], [Bidb], lambda: nc.vector.tensor_copy(out=idb[:], in_=idf[:]))
    ones, Bones = C.sb([128, 128], BF16, "ones")
    C.dve([], [Bones], lambda: nc.vector.memset(ones[:], 1.0))
    epsc, Beps = C.sb([128, 1], F32, "eps")
    C.dve([], [Beps], lambda: nc.vector.memset(epsc[:], EPS))
    lbp, Blbp = C.sb([128, 6], F32, "lbp")
    for d_ in range(2):
        c0 = d_ * 3
        if layer == 0:
            C.dve([], [Blbp], lambda c0=c0: nc.vector.memset(lbp[:, c0:c0 + 1], 0.0))
        else:
            C.dve([Blb], [Blbp], lambda c0=c0, d_=d_: nc.vector.tensor_tensor(
                out=lbp[:, c0:c0 + 1], in0=lb[:, 2 * d_ + 1:2 * d_ + 2], in1=lb[:, 2 * d_:2 * d_ + 1], op=ALU.subtract))
            C.act([Blbp], [Blbp], lbp[:, c0:c0 + 1], lbp[:, c0:c0 + 1], AF.Sigmoid)
        C.dve([Blbp], [Blbp], lambda c0=c0: nc.vector.tensor_scalar(
            out=lbp[:, c0 + 1:c0 + 2], in0=lbp[:, c0:c0 + 1], scalar1=-1.0, scalar2=1.0, op0=ALU.mult, op1=ALU.add))
        C.dve([Blbp], [Blbp], lambda c0=c0: nc.vector.tensor_scalar(
            out=lbp[:, c0 + 2:c0 + 3], in0=lbp[:, c0 + 1:c0 + 2], scalar1=-1.0, scalar2=None, op0=ALU.mult))

    hTd = C.dscr("hTd", [NTL, 128, DC, TL], BF16)
    xts = [C.sb([128, DC, TL], F32, "xt") for _ in range(2)]
    sq, Bsq = C.sb([128, DC, TL], BF16, "sq")
    hTs = [C.sb([128, DC, TL], BF16, "hT"), (sq, Bsq)]
    def t32(name): return C.sb([128, TL], F32, name)
    alias_k = [0]
    def a32(name):
        k = alias_k[0]; alias_k[0] += 1
        assert k < DC
        return xts[1][0][:, k, :], Buf("%s_a%d" % (name, k))
    class _V:
        def __init__(self, ap): self.ap_ = ap
        def __getitem__(self, key): return self.ap_[key] if key != slice(None) else self.ap_
    def a32t(name):
        ap, b = a32(name)
        return _V(ap), b
    rstd, Brstd = t32("rstd"); sig, Bsig = t32("sig")
    bb, Bbb = t32("bb"); cc, Bcc = t32("cc"); Ei, BEi = t32("Ei")
    osum, Bosum = a32t("osum"); ro, Bro = a32t("ro")
    yr, Byr = a32t("yr"); yc, Byc = a32t("yc"); ycv, Bycv = t32("ycv"); obs, Bobs = t32("obs")
    qss = [t32("qs") for _ in range(2)]; gs = [t32("g") for _ in range(2)]; kfs = [t32("kf") for _ in range(2)]
    Es = [t32("E") for _ in range(2)]; sgs = [a32t("sg") for _ in range(3)]; obl = [a32t("ob") for _ in range(3)]
    osq, Bosq = C.sb([128, TL], BF16, "osq")
    qbs = [C.sb([128, TL], BF16, "qb") for _ in range(2)]; kbs = [C.sb([128, TL], BF16, "kb") for _ in range(2)]
    kbts = [C.sb([64, 8, 128], BF16, "kbt") for _ in range(2)]; vts = [C.sb([64, 8, 128], BF16, "vt") for _ in range(3)]
    scTs = [C.sb([64, 8, 64], BF16, "scT") for _ in range(2)]
    U, BU = C.sb([128, 9, 128], F32, "U")
    Ub, BUb = C.sb([128, 8, 128], BF16, "Ub")
    Wd, BWd = C.sb([128, 8, 128], F32, "Wd")
    vTs, BvTs = C.sb([128, TL], BF16, "vTs")
    ub = [a32t("ub") for _ in range(3)]
    gb = [a32t("gb") for _ in range(3)]
    ps_ss, Bpss = C.ps([128, TL], F32, "pss")
    pp = [C.ps([128, TL], F32, "pp") for _ in range(2)]
    ps_o, Bpo = C.ps([128, TL], F32, "po")
    ps_mf, Bpm = C.ps([128, TL], F32, "pm")
    ps_m = ps_mf[0:64, :]
    ps_t, Bpt = C.ps([64, 8, 128], BF16, "pt")
    pp3, Bpt1 = C.ps([128, TL], F32, "pp3")
    ps_t1 = pp3[0:64, :].bitcast(BF16).rearrange("p (j v) -> p j v", v=128)
    pdS, BpdS = C.ps([128, 4, 128], F32, "pdS")
    ppi = [0]

    vtd = C.dscr("vtd", [NTL, 64, 8, 128], BF16)
    qsd = C.dscr("qsd", [NTL, 128, TL], F32)
    Bvtd = [Buf("vtd%d" % i) for i in range(NTL)]
    Bqsd = [Buf("qsd%d" % i) for i in range(NTL)]
    Bx_out = [Buf("mixo%d" % i) for i in range(2 * NTL + 2)]
    Bobwd = [Buf("obwd%d" % i) for i in range(NTL)]
    BhTd = [Buf("hTd%d" % i) for i in range(NTL)]

    def load_x(ti):
        xtt, Bxt = xts[ti % 2]
        C.load(SP, xtt[:], xv[:, :, ti * TL:(ti + 1) * TL], Bxt)

    def norm_tile(ti, par):
        hT, BhT = hTs[par]
        xtt, Bxt = xts[ti % 2]
        C.act([Bxt], [Bsq], sq[:], xtt[:], AF.Square)
        C.pe([Bsq, Bones], [Bpss], lambda: [nc.tensor.matmul(ps_ss[:], lhsT=ones[:], rhs=sq[:, c, :], start=(c == 0), stop=(c == DC - 1)) for c in range(DC)])
        rstd_from_psum(C, ps_ss[:], Bpss, rstd[:], Brstd, D, epsc[:], Beps)
        for c in range(DC):
            C.dve([Bxt, Bnw, Brstd], [BhT], lambda c=c: nc.vector.scalar_tensor_tensor(
                out=hT[:, c, :], in0=xtt[:, c, :], scalar=nw[:, c:c + 1], in1=rstd[:], op0=ALU.mult, op1=ALU.mult))
        C.store(SP, hTd.ap()[ti], hT[:], BhT, BhTd[ti])

    pass2 = [False]

    def proj(gi, par):
        hT, BhT = hTs[par if pass2[0] else 0]
        pool_ = pp + [(pp3, Bpt1)] if pass2[0] else pp
        p, Bp = pool_[ppi[0] % len(pool_)]; ppi[0] += 1
        C.pe([BhT, Bw], [Bp], lambda: [nc.tensor.matmul(p[:], lhsT=w_bf[:, c, gi * 128:(gi + 1) * 128], rhs=hT[:, c, :], start=(c == 0), stop=(c == DC - 1)) for c in range(DC)])
        return p, Bp

    def vtok(ti):
        vt, Bvt = vts[ti % 3]
        pv_, Bpv_ = proj(3, ti % 2)
        C.act([Bpv_], [BvTs], vTs[:], pv_[:], AF.Copy)
        C.pe([BvTs, Bidb], [Bpt1], lambda: [nc.tensor.transpose(out=ps_t1[:, j, :], in_=vTs[:, j * 64:(j + 1) * 64], identity=idb[:]) for j in range(8)])
        C.act([Bpt1], [Bvt], vt[:], ps_t1[:], AF.Copy)

    def qgates(ti, zgrp, dirn):
        par = ti % 2
        qs, Bqs = qss[par]; g, Bg = gs[par]; kf, Bkf = kfs[par]
        c0 = dirn * 3
        if dirn == 1:
            pq, Bpq = proj(0, par)
            C.act([Bpq], [Bqs], qs[:], pq[:], AF.Silu)
            C.store(SP, qsd.ap()[ti], qs[:], Bqs, Bqsd[ti])
        else:
            C.load(SP, qs[:], qsd.ap()[ti], Bqs, [Bqsd[ti]])
        pz, Bpz = proj(zgrp, par)
        C.act([Bpz], [Bsig], sig[:], pz[:], AF.Sigmoid)
        C.act([Bsig, Blbp], [Bg], g[:], sig[:], AF.Ln, scale=lbp[:, c0 + 1:c0 + 2], bias=lbp[:, c0:c0 + 1])
        C.dve([Bsig, Blbp], [Bkf], lambda: nc.vector.tensor_scalar(out=kf[:], in0=sig[:], scalar1=lbp[:, c0 + 2:c0 + 3], scalar2=lbp[:, c0 + 1:c0 + 2], op0=ALU.mult, op1=ALU.add))

    def stage2(ti, fwd):
        par = ti % 2
        qs, Bqs = qss[par]; g, Bg = gs[par]; kf, Bkf = kfs[par]
        E, BE = Es[par]; qb, Bqb = qbs[par]; kb, Bkb = kbs[par]; kbt, Bkbt = kbts[par]
        scT, BscT = scTs[par]
        mcol = 0 if fwd else 64
        C.dve([Bg, Brm], [Bbb], lambda: nc.vector.tensor_tensor_scan(out=bb[:], data0=rm[:], data1=g[:], initial=0.0, op0=ALU.mult, op1=ALU.add))
        src, Bsrc = bb, Bbb
        if not fwd:
            C.dve([Bg, Bbb], [Bcc], lambda: nc.vector.tensor_tensor(out=cc[:], in0=g[:], in1=bb[:], op=ALU.subtract))
            C.dve([Bcc, Bbb], [Bcc], lambda: nc.vector.tensor_tensor(
                out=cc[:].rearrange("p (c t) -> p c t", t=64), in0=cc[:].rearrange("p (c t) -> p c t", t=64),
                in1=bb[:].rearrange("p (c t) -> p c t", t=64)[:, :, 63:64].to_broadcast([128, 8, 64]), op=ALU.add))
            src, Bsrc = cc, Bcc
        C.act([Bsrc], [BE], E[:], src[:], AF.Exp)
        C.act([Bsrc], [BEi], Ei[:], src[:], AF.Exp, scale=-1.0)
        C.dve([Bqs, BE], [Bqb], lambda: nc.vector.tensor_tensor(out=qb[:], in0=qs[:], in1=E[:], op=ALU.mult))
        C.dve([Bkf, BEi], [Bkb], lambda: nc.vector.tensor_tensor(out=kb[:], in0=kf[:], in1=Ei[:], op=ALU.mult))
        C.pe([Bkb, Bidb], [Bpt], lambda: [nc.tensor.transpose(out=ps_t[:, j, :], in_=kb[:, j * 64:(j + 1) * 64], identity=idb[:]) for j in range(8)])
        C.dve([Bpt], [Bkbt], lambda: nc.vector.tensor_copy(out=kbt[:], in_=ps_t[:]))
        C.pe([Bkb, Bqb], [Bpm], lambda: [nc.tensor.matmul(ps_m[:, j * 64:(j + 1) * 64], lhsT=kb[:, j * 64:(j + 1) * 64], rhs=qb[:, j * 64:(j + 1) * 64], start=True, stop=True) for j in range(8)])
        C.dve([Bpm, Bmk], [BscT], lambda: nc.vector.tensor_tensor(
            out=scT[:], in0=ps_m.rearrange("p (c t) -> p c t", t=64),
            in1=mk[:, mcol:mcol + 64].unsqueeze(1).to_broadcast([64, 8, 64]), op=ALU.mult))

    def chunks(ti, fwd):
        par = ti % 2
        E, BE = Es[par]; qb, Bqb = qbs[par]; kbt, Bkbt = kbts[par]; vt, Bvt = vts[ti % 3]; scT, BscT = scTs[par]
        dcol = 63 if fwd else 0
        Ev = E[:].rearrange("p (c t) -> p c t", t=64)
        for hb in range(2):
            if fwd and hb == 1:
                pd_, Bpd_ = ps_ss[:].rearrange("p (j v) -> p j v", v=128), Bpss
            else:
                pd_, Bpd_ = pdS[:], BpdS
            C.pe([Bkbt, Bvt], [Bpd_], lambda hb=hb, pd_=pd_: [nc.tensor.matmul(pd_[:, jj, :], lhsT=kbt[:, hb * 4 + jj, :], rhs=vt[:, hb * 4 + jj, :], start=True, stop=True) for jj in range(4)])
            C.dve([Bpd_, BE], [BWd], lambda hb=hb, pd_=pd_: nc.vector.tensor_tensor(
                out=Wd[:, hb * 4:(hb + 1) * 4, :], in0=pd_,
                in1=Ev[:, hb * 4:(hb + 1) * 4, dcol:dcol + 1].to_broadcast([128, 4, 128]), op=ALU.mult))
        order = range(8) if fwd else range(7, -1, -1)
        for j in order:
            src, dst = (j, j + 1) if fwd else (j + 1, j)
            dc_ = j * 64 + dcol
            C.dve([BU, BE, BWd], [BU], lambda src=src, dst=dst, dc_=dc_, j=j: nc.vector.scalar_tensor_tensor(
                out=U[:, dst, :], in0=U[:, src, :], scalar=E[:, dc_:dc_ + 1], in1=Wd[:, j, :], op0=ALU.mult, op1=ALU.add))
        lo = 0 if fwd else 1
        C.dve([BU], [BUb], lambda lo=lo: nc.vector.tensor_copy(out=Ub[:], in_=U[:, lo:lo + 8, :]))
        cs_, cd_ = (8, 0) if fwd else (0, 8)
        C.dve([BU], [BU], lambda cs_=cs_, cd_=cd_: nc.vector.tensor_copy(out=U[:, cd_, :], in_=U[:, cs_, :]))
        C.pe([Bvt, BscT, BUb, Bqb], [Bpo], lambda: [m for j in range(8) for m in (
            nc.tensor.matmul(ps_o[:, j * 64:(j + 1) * 64], lhsT=vt[:, j, :], rhs=scT[:, j, :], start=True, stop=False),
            nc.tensor.matmul(ps_o[:, j * 64:(j + 1) * 64], lhsT=Ub[:, j, :], rhs=qb[:, j * 64:(j + 1) * 64], start=False, stop=True))])

    def merge3(A, B, Cc):
        n = max(len(A), len(B), len(Cc), 1)
        ia = ib = ic = 0
        for k in range(1, n + 1):
            while ia < len(A) and ia * n < k * len(A):
                A[ia](); ia += 1
            while ib < len(B) and ib * n < k * len(B):
                B[ib](); ib += 1
            while ic < len(Cc) and ic * n < k * len(Cc):
                Cc[ic](); ic += 1

    def s1_bwd(ti):
        if ti > 0:
            load_x(ti - 1)
        norm_tile(ti, 0)
        qgates(ti, 2, 1)
        vtok(ti)
        C.store(SP, vtd.ap()[ti], vts[ti % 3][0][:], vts[ti % 3][1], Bvtd[ti])

    def s3_bwd(ti):
        chunks(ti, False)

    def s3_bfin(ti):
        C.act([Bpo], [Bobs], obs[:], ps_o[:], AF.Copy)
        C.store(SP, obwd.ap()[:, ti * TL:(ti + 1) * TL], obs[:], Bobs, Bobwd[ti])

    C.dve([], [BU], lambda: nc.vector.memset(U[:], 0.0))
    load_x(NTL - 1)
    seq = list(range(NTL - 1, -1, -1))
    for k in range(-2, NTL):
        A = C.captured(s3_bwd, seq[k]) if 0 <= k < NTL else []
        Bl = C.captured(stage2, seq[k + 1], False) if 0 <= k + 1 < NTL else []
        Cl = C.captured(s1_bwd, seq[k + 2]) if 0 <= k + 2 < NTL else []
        merge3(A, Bl, Cl)
        if 0 <= k < NTL:
            s3_bfin(seq[k])

    def conv_final(ti, left, right):
        u, Bu = ub[ti % 3]; gt, Bgt = gb[ti % 3]
        C.dve([Bu, Bcw], [Byc], lambda: nc.vector.tensor_scalar(out=yc[:], in0=u[:], scalar1=cw[:, 1:2], scalar2=None, op0=ALU.mult))
        C.dve([Bu, Bcw, Byc], [Byc], lambda: nc.vector.scalar_tensor_tensor(out=yc[:, 1:TL], in0=u[:, 0:TL - 1], scalar=cw[:, 0:1], in1=yc[:, 1:TL], op0=ALU.mult, op1=ALU.add))
        C.dve([Bu, Bcw, Byc], [Byc], lambda: nc.vector.scalar_tensor_tensor(out=yc[:, 0:TL - 1], in0=u[:, 1:TL], scalar=cw[:, 2:3], in1=yc[:, 0:TL - 1], op0=ALU.mult, op1=ALU.add))
        if left is not None:
            la, Bl = left
            C.dve([Bl, Bcw, Byc], [Byc], lambda: nc.vector.scalar_tensor_tensor(out=yc[:, 0:1], in0=la, scalar=cw[:, 0:1], in1=yc[:, 0:1], op0=ALU.mult, op1=ALU.add))
        if right is not None:
            ra, Br = right
            C.dve([Br, Bcw, Byc], [Byc], lambda: nc.vector.scalar_tensor_tensor(out=yc[:, TL - 1:TL], in0=ra, scalar=cw[:, 2:3], in1=yc[:, TL - 1:TL], op0=ALU.mult, op1=ALU.add))
        C.dve([Byc, Bgt], [Bycv], lambda: nc.vector.tensor_tensor(out=ycv[:], in0=yc[:], in1=gt[:], op=ALU.mult))
        C.store(SP, mixT.ap()[128:256, ti * TL:(ti + 1) * TL], ycv[:], Bycv, Bx_out[NTL + ti])

    def load_h(ti):
        hT, BhT = hTs[ti % 2]
        C.load(SP, hT[:], hTd.ap()[ti], BhT, [BhTd[ti]])

    def s1_fwd(ti):
        par = ti % 2
        ob, Bob = obl[ti % 3]; sg, Bsg = sgs[ti % 3]
        C.load(SP, ob[:], obwd.ap()[:, ti * TL:(ti + 1) * TL], Bob, [Bobwd[ti]])
        if ti + 1 < NTL:
            load_h(ti + 1)
        qgates(ti, 1, 0)
        C.load(SP, vts[ti % 3][0][:], vtd.ap()[ti], vts[ti % 3][1], [Bvtd[ti]])
        pg, Bpg = proj(4, par)
        C.act([Bpg], [Bsg], sg[:], pg[:], AF.Silu)
        u, Bu = ub[ti % 3]; gt, Bgt = gb[ti % 3]
        p5, Bp5 = proj(5, par)
        C.act([Bp5], [Bgt], gt[:], p5[:], AF.Copy)
        p7, Bp7 = proj(7, par)
        C.act([Bp7], [Bu], u[:], p7[:], AF.Copy)
        p6, Bp6 = proj(6, par)
        C.dve([Bp6, Bu], [Bu], lambda u=u, p6=p6: nc.vector.tensor_tensor(out=u[:], in0=p6[:], in1=u[:], op=ALU.mult))
        if ti > 0:
            left = None
            if ti > 1:
                upp, Bupp = ub[(ti - 2) % 3]
                left = (upp[:, TL - 1:TL], Bupp)
            conv_final(ti - 1, left, (u[:, 0:1], Bu))

    def s3_fwd(ti):
        chunks(ti, True)

    def s3_fin(ti):
        ob, Bob = obl[ti % 3]; sg, Bsg = sgs[ti % 3]
        C.dve([Bpo, Bob], [Bosum], lambda: nc.vector.tensor_tensor(out=osum[:], in0=ps_o[:], in1=ob[:], op=ALU.add))
        C.act([Bosum], [Bosq], osq[:], osum[:], AF.Square)
        C.pe([Bosq, Bones], [Bpm], lambda: nc.tensor.matmul(ps_mf[:], lhsT=ones[:], rhs=osq[:], start=True, stop=True))
        rstd_from_psum(C, ps_mf[:], Bpm, ro[:], Bro, 128, epsc[:], Beps)
        C.dve([Bosum, Bhw, Bro], [Byr], lambda: nc.vector.scalar_tensor_tensor(out=yr[:], in0=osum[:], scalar=hw[:, 0:1], in1=ro[:], op0=ALU.mult, op1=ALU.mult))
        C.dve([Byr, Bsg], [Byr], lambda: nc.vector.tensor_tensor(out=yr[:], in0=yr[:], in1=sg[:], op=ALU.mult))
        C.store(SP, mixT.ap()[0:128, ti * TL:(ti + 1) * TL], yr[:], Byr, Bx_out[ti])

    S.barrier()
    pass2[0] = True
    C.dve([], [BU], lambda: nc.vector.memset(U[:], 0.0))
    load_h(0)
    for k in range(-2, NTL):
        A = C.captured(s3_fwd, k) if 0 <= k < NTL else []
        Bl = C.captured(stage2, k + 1, True) if 0 <= k + 1 < NTL else []
        Cl = C.captured(s1_fwd, k + 2) if 0 <= k + 2 < NTL else []
        merge3(A, Bl, Cl)
        if 0 <= k < NTL:
            s3_fin(k)
    left = None
    if NTL > 1:
        upp, Bupp = ub[(NTL - 2) % 3]
        left = (upp[:, TL - 1:TL], Bupp)
    conv_final(NTL - 1, left, None)
    return C.finish(Bx_out)


_CACHE = {}


def _consts():
    m = np.zeros((64, 128), np.float32)
    m[:, 0:64] = np.triu(np.ones((64, 64), np.float32))
    m[:, 64:128] = np.tril(np.ones((64, 64), np.float32))
    rm = np.ones((128, 512), np.float32)
    rm[:, ::64] = 0.0
    return {"ident": np.eye(128, dtype=np.float32), "masks": m, "rmask": rm}


def _pc(v):
    return np.ascontiguousarray(v.reshape(-1, 128).T)


def run_mixer(xT, p, layer):
    ntok = xT.shape[1]
    key = ("M", ntok, layer)
    if key not in _CACHE:
        _CACHE[key] = build_mixer(ntok, layer)
    nc = _CACHE[key]
    w_in = p["w_in"][layer]
    cst = _consts()
    ins = []
    for h in range(NCORES):
        cols = np.concatenate([np.arange(g * 1024 + h * 128, g * 1024 + (h + 1) * 128) for g in range(8)])
        wa = w_in[:, cols]
        wa = np.ascontiguousarray(wa.reshape(DC, 128, 1024).transpose(1, 0, 2))
        hs = slice(h * 128, (h + 1) * 128)
        lb = np.stack([p["lb_fwd"][0, hs], p["lb_fwd"][1, hs], p["lb_bwd"][0, hs], p["lb_bwd"][1, hs]], axis=1)
        d = {"xT": xT, "wA": wa, "nw": _pc(p["attn_norm_w"][layer]), "lb": np.ascontiguousarray(lb, dtype=np.float32),
             "hw": np.ascontiguousarray(p["hgrn_norm_w"][layer, hs].reshape(128, 1)),
             "cw": np.ascontiguousarray(p["conv_w"][layer][:, hs].T)}
        d.update(cst)
        ins.append(d)
    res = run_bass_kernel_spmd(nc, ins, core_ids=list(range(NCORES)))
    mixT = np.empty((D, ntok), np.float32)
    for h in range(NCORES):
        o = res.results[h]["mixT"]
        mixT[h * 128:(h + 1) * 128] = o[0:128]
        mixT[1024 + h * 128:1024 + (h + 1) * 128] = o[128:256]
    return mixT


def build_ffn(nt):
    C = Ctx()
    nc, S = C.nc, C.S
    PE, ACT, DVE, POOL, SP = C.PE, C.ACT, C.DVE, C.POOL, C.SP
    TL = 512
    HT = 1024
    NH = nt // HT
    NU = FC // 2
    xT = C.din("xT", [D, nt])
    mT = C.din("mT", [D, nt])
    wo_d = C.din("wo", [4, 128, DC, 512])
    wgu_d = C.din("wgu", [NU, 128, DC, 512])
    wd_d = C.din("wd", [NU, 128, 2, D])
    nw_d = C.din("nw", [128, DC])
    fw_d = C.din("fw", [128, DC])
    oX = C.dout("oX", [D, nt])
    oN = C.dout("oN", [D, nt])
    xv = xT.ap().rearrange("(c p) t -> p c t", p=128)
    mv = mT.ap().rearrange("(c p) t -> p c t", p=128)
    oXv = oX.ap().rearrange("(c p) t -> p c t", p=128)
    oNv = oN.ap().rearrange("(c p) t -> p c t", p=128)

    nw, Bnw = C.sb([128, DC], F32, "nw"); C.load(SP, nw[:], nw_d.ap(), Bnw)
    fw, Bfw = C.sb([128, DC], F32, "fw"); C.load(SP, fw[:], fw_d.ap(), Bfw)
    ones, Bones = C.sb([128, 128], BF16, "ones")
    C.dve([], [Bones], lambda: nc.vector.memset(ones[:], 1.0))
    epsc, Beps = C.sb([128, 1], F32, "eps")
    C.dve([], [Beps], lambda: nc.vector.memset(epsc[:], EPS))

    acc, _ = C.sb([128, DC, HT], F32, "acc")
    Bacc = [[Buf("acc%d_%d" % (c, t)) for t in range(2)] for c in range(DC)]
    Baccl = [b for r in Bacc for b in r]
    mb, _ = C.sb([128, DC, HT], BF16, "mb")
    Bmb = [Buf("mb%d" % t) for t in range(2)]
    h2, Bh2 = C.sb([128, DC, HT], BF16, "h2")
    Bh2t = [Buf("h2_%d" % t) for t in range(2)]
    rstd, _ = C.sb([128, HT], F32, "rstd")
    Brs = [Buf("rs%d" % t) for t in range(2)]
    wg = [C.sb([128, DC, 512], BF16, "wg") for _ in range(2)]
    wd = [C.sb([128, 2, D], BF16, "wd") for _ in range(2)]
    aT = [C.sb([128, 2, HT], BF16, "aT") for _ in range(2)]
    sgt = [C.sb([128, TL], F32, "sg") for _ in range(2)]
    outn, Boutn = C.sb([128, 4, TL], F32, "outn")
    pss, Bpss = C.ps([128, TL], F32, "pss")
    pg = [C.ps([128, TL], F32, "pg") for _ in range(2)]
    pu = [C.ps([128, TL], F32, "pu") for _ in range(2)]
    pd = [C.ps([128, TL], F32, "pd") for _ in range(3)]
    Bouts = []
    cnt = {"g": 0, "d": 0, "s": 0}

    def norm_half(wvec, Bwv, dst, Bdst_t):
        for tt in range(2):
            ts = slice(tt * TL, (tt + 1) * TL)
            C.act([Bacc[c][tt] for c in range(DC)], [Bmb[tt]], mb[:, :, ts], acc[:, :, ts], AF.Square)
            C.pe([Bmb[tt], Bones], [Bpss], lambda ts=ts: [nc.tensor.matmul(pss[:], lhsT=ones[:], rhs=mb[:, c, ts], start=(c == 0), stop=(c == DC - 1)) for c in range(DC)])
            rstd_from_psum(C, pss[:], Bpss, rstd[:, ts], Brs[tt], D, epsc[:], Beps)

    for hf in range(NH):
        t0 = hf * HT
        for tt in range(2):
            ts = slice(tt * TL, (tt + 1) * TL)
            for c4 in range(4):
                cs = slice(c4 * 4, c4 * 4 + 4)
                bl = [Bacc[c][tt] for c in range(c4 * 4, c4 * 4 + 4)]
                owner = bl[0]
                S.dma(SP, [], bl, lambda cs=cs, ts=ts, tt=tt, t0=t0: nc.sync.dma_start(out=acc[:, cs, ts], in_=xv[:, cs, t0 + tt * TL:t0 + (tt + 1) * TL]), owner)
            S.dma(POOL, [], [Bmb[tt]], lambda ts=ts, tt=tt, t0=t0: nc.gpsimd.dma_start(out=mb[:, :, ts], in_=mv[:, :, t0 + tt * TL:t0 + (tt + 1) * TL]), Bmb[tt])
        for uo in range(4):
            w, Bw = wg[cnt["g"] % 2]; cnt["g"] += 1
            C.load(POOL, w[:], wo_d.ap()[uo], Bw)
            for dl in range(4):
                dc = uo * 4 + dl
                for tt in range(2):
                    ts = slice(tt * TL, (tt + 1) * TL)
                    p, Bp = pd[cnt["d"] % 3]; cnt["d"] += 1
                    C.pe([Bw, Bmb[tt]], [Bp], lambda w=w, p=p, dl=dl, ts=ts: [nc.tensor.matmul(p[:], lhsT=w[:, e, dl * 128:(dl + 1) * 128], rhs=mb[:, e, ts], start=(e == 0), stop=(e == DC - 1)) for e in range(DC)])
                    C.dve([Bp, Bacc[dc][tt]], [Bacc[dc][tt]], lambda p=p, dc=dc, ts=ts: nc.vector.tensor_tensor(out=acc[:, dc, ts], in0=p[:], in1=acc[:, dc, ts], op=ALU.add))
        norm_half(nw, Bnw, h2, Bh2t)
        for tt in range(2):
            ts = slice(tt * TL, (tt + 1) * TL)
            for c in range(DC):
                C.dve([Bacc[c][tt], Bnw, Brs[tt]], [Bh2t[tt]], lambda c=c, ts=ts: nc.vector.scalar_tensor_tensor(
                    out=h2[:, c, ts], in0=acc[:, c, ts], scalar=nw[:, c:c + 1], in1=rstd[:, ts], op0=ALU.mult, op1=ALU.mult))
        for u in range(NU):
            w, Bw = wg[cnt["g"] % 2]; cnt["g"] += 1
            w2, Bw2 = wd[u % 2]
            a, Ba = aT[u % 2]
            C.load(POOL, w[:], wgu_d.ap()[u], Bw)
            C.load(POOL, w2[:], wd_d.ap()[u], Bw2)
            for fl in range(2):
                for tt in range(2):
                    ts = slice(tt * TL, (tt + 1) * TL)
                    k = cnt["s"] % 2; cnt["s"] += 1
                    pgk, Bpg = pg[k]; puk, Bpu = pu[k]; sgk, Bsg = sgt[k]
                    C.pe([Bw, Bh2t[tt]], [Bpg], lambda w=w, pgk=pgk, fl=fl, ts=ts: [nc.tensor.matmul(pgk[:], lhsT=w[:, c, fl * 128:(fl + 1) * 128], rhs=h2[:, c, ts], start=(c == 0), stop=(c == DC - 1)) for c in range(DC)])
                    C.pe([Bw, Bh2t[tt]], [Bpu], lambda w=w, puk=puk, fl=fl, ts=ts: [nc.tensor.matmul(puk[:], lhsT=w[:, c, 256 + fl * 128:256 + (fl + 1) * 128], rhs=h2[:, c, ts], start=(c == 0), stop=(c == DC - 1)) for c in range(DC)])
                    C.act([Bpg], [Bsg], sgk[:], pgk[:], AF.Silu)
                    C.dve([Bsg, Bpu], [Ba], lambda a=a, sgk=sgk, puk=puk, fl=fl, ts=ts: nc.vector.tensor_tensor(out=a[:, fl, ts], in0=sgk[:], in1=puk[:], op=ALU.mult))
            for dc in range(DC):
                for tt in range(2):
                    ts = slice(tt * TL, (tt + 1) * TL)
                    p, Bp = pd[cnt["d"] % 3]; cnt["d"] += 1
                    C.pe([Bw2, Ba], [Bp], lambda w2=w2, a=a, p=p, dc=dc, ts=ts: [nc.tensor.matmul(p[:], lhsT=w2[:, fl, dc * 128:(dc + 1) * 128], rhs=a[:, fl, ts], start=(fl == 0), stop=(fl == 1)) for fl in range(2)])
                    C.dve([Bp, Bacc[dc][tt]], [Bacc[dc][tt]], lambda p=p, dc=dc, ts=ts: nc.vector.tensor_tensor(out=acc[:, dc, ts], in0=p[:], in1=acc[:, dc, ts], op=ALU.add))
        for tt in range(2):
            ts = slice(tt * TL, (tt + 1) * TL)
            for c4 in range(4):
                cs = slice(c4 * 4, c4 * 4 + 4)
                bl = [Bacc[c][tt] for c in range(c4 * 4, c4 * 4 + 4)]
                bo = Buf("ox"); Bouts.append(bo)
                S.dma(SP, bl, [bo], lambda cs=cs, ts=ts, tt=tt, t0=t0: nc.sync.dma_start(out=oXv[:, cs, t0 + tt * TL:t0 + (tt + 1) * TL], in_=acc[:, cs, ts]), bl[0])
        norm_half(fw, Bfw, None, None)
        for tt in range(2):
            ts = slice(tt * TL, (tt + 1) * TL)
            for c4 in range(4):
                for cl in range(4):
                    c = c4 * 4 + cl
                    C.dve([Bacc[c][tt], Bfw, Brs[tt]], [Boutn], lambda c=c, cl=cl, ts=ts: nc.vector.scalar_tensor_tensor(
                        out=outn[:, cl, :], in0=acc[:, c, ts], scalar=fw[:, c:c + 1], in1=rstd[:, ts], op0=ALU.mult, op1=ALU.mult))
                bo = Buf("on"); Bouts.append(bo)
                S.dma(SP, [Boutn], [bo], lambda tt=tt, c4=c4, t0=t0: nc.sync.dma_start(out=oNv[:, c4 * 4:c4 * 4 + 4, t0 + tt * TL:t0 + (tt + 1) * TL], in_=outn[:]), Boutn)
    return C.finish(Bouts)


def run_ffn(xT, mixT, p, layer):
    ntok = xT.shape[1]
    nt = ntok // NCORES
    key = ("F", nt)
    if key not in _CACHE:
        _CACHE[key] = build_ffn(nt)
    nc = _CACHE[key]
    w_out = p["w_out"][layer]; w_gu = p["w_gate_up"][layer]; w_dn = p["w_down"][layer]
    wo = np.ascontiguousarray(w_out.reshape(DC, 128, 4, 512).transpose(2, 1, 0, 3))
    NU = FC // 2
    gcols = w_gu[:, :DFF].reshape(D, NU, 256); ucols = w_gu[:, DFF:].reshape(D, NU, 256)
    wgu = np.concatenate([gcols, ucols], axis=2)
    wgu = np.ascontiguousarray(wgu.reshape(DC, 128, NU, 512).transpose(2, 1, 0, 3))
    wd = np.ascontiguousarray(w_dn.reshape(NU, 2, 128, D).transpose(0, 2, 1, 3))
    common = {"wo": wo, "wgu": wgu, "wd": wd, "nw": _pc(p["ffn_norm_w"][layer]), "fw": _pc(p["final_norm_w"])}
    ins = []
    for c in range(NCORES):
        d = {"xT": np.ascontiguousarray(xT[:, c * nt:(c + 1) * nt]), "mT": np.ascontiguousarray(mixT[:, c * nt:(c + 1) * nt])}
        d.update(common)
        ins.append(d)
    res = run_bass_kernel_spmd(nc, ins, core_ids=list(range(NCORES)))
    x2 = np.concatenate([res.results[c]["oX"] for c in range(NCORES)], axis=1)
    xn = np.concatenate([res.results[c]["oN"] for c in range(NCORES)], axis=1)
    return x2, xn


def kernel(x, attn_norm_w, w_in, lb_fwd, lb_bwd, hgrn_norm_w, conv_w, w_out, ffn_norm_w, w_gate_up, w_down, final_norm_w):
    p = {"attn_norm_w": np.asarray(attn_norm_w, np.float32), "w_in": np.asarray(w_in, np.float32),
         "lb_fwd": np.asarray(lb_fwd, np.float32), "lb_bwd": np.asarray(lb_bwd, np.float32),
         "hgrn_norm_w": np.asarray(hgrn_norm_w, np.float32), "conv_w": np.asarray(conv_w, np.float32),
         "w_out": np.asarray(w_out, np.float32), "ffn_norm_w": np.asarray(ffn_norm_w, np.float32),
         "w_gate_up": np.asarray(w_gate_up, np.float32), "w_down": np.asarray(w_down, np.float32),
         "final_norm_w": np.asarray(final_norm_w, np.float32)}
    x = np.asarray(x, np.float32)
    xT = np.ascontiguousarray(x[0].T)
    xn = None
    for layer in range(2):
        mixT = run_mixer(xT, p, layer)
        xT, xn = run_ffn(xT, mixT, p, layer)
    return np.ascontiguousarray(xn.T)[None].astype(np.float32)
```
